# Optimizing a Trainium2 kernel written in Bass

```python
import jax, jax.numpy as jnp
from jax import lax
import numpy as np

D_MODEL = 2048
BATCH = 4
SEQ = 8192
DEPTH = 1

GRID_W = 64
CTX_LEN = 256
RMS_EPS = 1e-6
N_MOD = 6
MLA_HEADS = 8
Q_LORA = 512
KV_LORA = 256
QK_NOPE = 128
QK_ROPE = 64
V_HEAD = 128
MLA_WIDTH = MLA_HEADS * V_HEAD
ROPE_THETA = 10000.0
Q_BLOCK = 128
HG_HEADS = 8
HG_FDIM = 128
HG_IDIM = 128
HG_FW = HG_HEADS * HG_FDIM
HG_IW = HG_HEADS * HG_IDIM
HG_CHUNK = 64
D_FF = -(-(8 * D_MODEL) // (3 * 256)) * 256
IN_WIDTHS = (Q_LORA, KV_LORA, QK_ROPE, HG_FW, HG_FW, HG_FW, HG_IW, HG_IW, 2 * D_MODEL)
N_IN = sum(IN_WIDTHS)

kernel_name = "hybrid_mla_hgrn2_dit_block"


def rmsnorm(x, g):
    xf = x.astype(jnp.float32)
    y = xf * lax.rsqrt(jnp.mean(xf * xf, axis=-1, keepdims=True) + RMS_EPS)
    return (y * g.astype(jnp.float32)).astype(x.dtype)


def modulate(h, shift, scale):
    return h * (1 + scale) + shift


def axial_rope_tables(rows):
    pairs = QK_ROPE // 4
    row = jnp.repeat(jnp.arange(rows, dtype=jnp.float32), GRID_W)
    col = jnp.tile(jnp.arange(GRID_W, dtype=jnp.float32), rows)
    inv = ROPE_THETA ** (-jnp.arange(pairs, dtype=jnp.float32) / pairs)
    ang = jnp.concatenate([row[:, None] * inv, col[:, None] * inv], axis=-1)
    return jnp.cos(ang)[:, None, :], jnp.sin(ang)[:, None, :]


def apply_rope(x, cos, sin):
    half = QK_ROPE // 2
    xf = x.astype(jnp.float32)
    x1, x2 = xf[..., :half], xf[..., half:]
    out = jnp.concatenate([x1 * cos - x2 * sin, x2 * cos + x1 * sin], axis=-1)
    return out.astype(x.dtype)


def forget_gate(f_logit, lb):
    f = lb + (1.0 - lb) * jax.nn.sigmoid(f_logit.astype(jnp.float32))
    return 1.0 - f, jnp.log(f)


def mixer_projections(h, w_in_l, q_norm_g, kv_norm_g, w_uq_l, w_ukv_l, lb_l, rope):
    bsz, length, _ = h.shape
    p = h @ w_in_l
    c_q, c_kv, k_r, hg_q, hg_ff, hg_fb, hg_i, hg_g, gate_logits = jnp.split(
        p, np.cumsum(IN_WIDTHS)[:-1].tolist(), axis=-1)
    q = (rmsnorm(c_q, q_norm_g) @ w_uq_l).reshape(bsz, length, MLA_HEADS, QK_NOPE + QK_ROPE)
    kv = (rmsnorm(c_kv, kv_norm_g) @ w_ukv_l).reshape(bsz, length, MLA_HEADS, QK_NOPE + V_HEAD)
    q_nope, q_rope = q[..., :QK_NOPE], q[..., QK_NOPE:]
    k_nope, v = kv[..., :QK_NOPE], kv[..., QK_NOPE:]
    k_rope = k_r[:, :, None, :]
    if rope is not None:
        cos, sin = rope
        q_rope = apply_rope(q_rope, cos, sin)
        k_rope = apply_rope(k_rope, cos, sin)
    q = jnp.concatenate([q_nope, q_rope], axis=-1)
    k = jnp.concatenate([k_nope, jnp.broadcast_to(k_rope, (bsz, length, MLA_HEADS, QK_ROPE))], axis=-1)
    heads = lambda t: t.reshape(bsz, length, HG_HEADS, -1)
    hq = heads(jax.nn.silu(hg_q)) * (HG_FDIM ** -0.5)
    k_f, lf_f = forget_gate(heads(hg_ff), lb_l[0])
    k_b, lf_b = forget_gate(heads(hg_fb), lb_l[1])
    hv = heads(hg_i)
    hgate = heads(hg_g)
    return (q, k, v), (hq, k_f, lf_f, k_b, lf_b, hv, hgate), gate_logits.reshape(bsz, length, 2, D_MODEL)


def block_attention(q, k, v):
    bsz, length, nh, dqk = q.shape
    nb = length // Q_BLOCK
    scale = dqk ** -0.5
    qb = q.reshape(bsz, nb, Q_BLOCK, nh, dqk).swapaxes(0, 1)

    def one_block(qblk):
        s = jnp.einsum('bqhd,bkhd->bhqk', qblk, k).astype(jnp.float32) * scale
        p = jax.nn.softmax(s, axis=-1)
        return jnp.einsum('bhqk,bkhv->bqhv', p.astype(v.dtype), v)

    out = lax.map(one_block, qb)
    return out.swapaxes(0, 1).reshape(bsz, length, nh, v.shape[-1])


def gla_chunkwise(q, k, v, log_f, s0):
    out_dtype = v.dtype
    bsz, length, nh, dk = q.shape
    dv = v.shape[-1]
    n = length // HG_CHUNK
    q, k, v, log_f = [t.astype(jnp.float32).reshape(bsz, n, HG_CHUNK, nh, -1) for t in (q, k, v, log_f)]
    b = jnp.cumsum(log_f, axis=2)
    b_last = b[:, :, -1]
    q_d = q * jnp.exp(b)
    k_d = k * jnp.exp(-b)
    k_tail = k * jnp.exp(b_last[:, :, None] - b)
    mask = jnp.tril(jnp.ones((HG_CHUNK, HG_CHUNK), dtype=bool))
    scores = jnp.where(mask, jnp.einsum('bnthd,bnshd->bnhts', q_d, k_d), 0.0)
    o_intra = jnp.einsum('bnhts,bnshv->bnthv', scores, v)
    u = jnp.einsum('bnshd,bnshv->bnhdv', k_tail, v)
    decay = jnp.exp(b_last)

    def step(s, inp):
        d, u_j = inp
        return d[..., None] * s + u_j, s

    s_final, s_starts = lax.scan(step, s0.astype(jnp.float32),
                                 (jnp.moveaxis(decay, 1, 0), jnp.moveaxis(u, 1, 0)))
    s_starts = jnp.moveaxis(s_starts, 0, 1)
    o_inter = jnp.einsum('bnthd,bnhdv->bnthv', q_d, s_starts)
    o = (o_intra + o_inter).reshape(bsz, length, nh, dv)
    return o.astype(out_dtype), s_final


def hgrn2_bidirectional(hq, k_f, lf_f, k_b, lf_b, hv, s_f0, s_b0):
    o_f, s_f = gla_chunkwise(hq, k_f, hv, lf_f, s_f0)
    rev = lambda t: jnp.flip(t, axis=1)
    o_b, s_b = gla_chunkwise(rev(hq), rev(k_b), rev(hv), rev(lf_b), s_b0)
    return o_f + rev(o_b), s_f, s_b


def merge_branches(attn_o, hg_o, hgate, gate_logits, o_norm_g, w_br_mla_l, w_br_hgrn_l, w_out_l):
    bsz, length = attn_o.shape[:2]
    hg_o = rmsnorm(hg_o, o_norm_g) * jax.nn.silu(hgate)
    y_a = attn_o.reshape(bsz, length, MLA_WIDTH) @ w_br_mla_l
    y_h = hg_o.reshape(bsz, length, HG_IW) @ w_br_hgrn_l
    g = jax.nn.sigmoid(gate_logits)
    return (g[..., 0, :] * y_a + g[..., 1, :] * y_h) @ w_out_l


def swiglu(h, w_in_l, w_out_l):
    a, b = jnp.split(h @ w_in_l, 2, axis=-1)
    return (jax.nn.silu(a) * b) @ w_out_l


def setup_inputs(seed: int = 0) -> dict:
    key = jax.random.key(seed)
    ks = jax.random.split(key, 19)
    f32 = jnp.float32
    nrm = lambda k, shape, fan_in: jax.random.normal(k, shape, f32) * (fan_in ** -0.5)
    gain = lambda k, shape: 1.0 + 0.05 * jax.random.normal(k, shape, f32)
    return {
        "x": jax.random.normal(ks[0], (BATCH, SEQ, D_MODEL), f32),
        "c": jax.random.normal(ks[1], (BATCH, D_MODEL), f32),
        "ctx": jax.random.normal(ks[2], (BATCH, CTX_LEN, D_MODEL), f32),
        "c_ctx": jax.random.normal(ks[3], (D_MODEL,), f32),
        "w_mod": nrm(ks[4], (DEPTH, D_MODEL, N_MOD * D_MODEL), D_MODEL),
        "b_mod": 0.02 * jax.random.normal(ks[5], (DEPTH, N_MOD * D_MODEL), f32),
        "norm_g": gain(ks[6], (DEPTH, 4, D_MODEL)),
        "w_in": nrm(ks[7], (DEPTH, D_MODEL, N_IN), D_MODEL),
        "mla_q_norm": gain(ks[8], (DEPTH, Q_LORA)),
        "mla_kv_norm": gain(ks[9], (DEPTH, KV_LORA)),
        "w_uq": nrm(ks[10], (DEPTH, Q_LORA, MLA_HEADS * (QK_NOPE + QK_ROPE)), Q_LORA),
        "w_ukv": nrm(ks[11], (DEPTH, KV_LORA, MLA_HEADS * (QK_NOPE + V_HEAD)), KV_LORA),
        "hgrn_lb": 0.1 * jax.random.normal(ks[12], (DEPTH + 1, 2, HG_HEADS, HG_FDIM), f32),
        "hgrn_o_norm": gain(ks[13], (DEPTH, HG_IDIM)),
        "w_br_mla": nrm(ks[14], (DEPTH, MLA_WIDTH, D_MODEL), MLA_WIDTH),
        "w_br_hgrn": nrm(ks[15], (DEPTH, HG_IW, D_MODEL), HG_IW),
        "w_out": nrm(ks[16], (DEPTH, D_MODEL, D_MODEL), D_MODEL),
        "w_ffn_in": nrm(ks[17], (DEPTH, D_MODEL, 2 * D_FF), D_MODEL),
        "w_ffn_out": nrm(ks[18], (DEPTH, D_FF, D_MODEL), D_FF),
    }


def reference(x, c, ctx, c_ctx, w_mod, b_mod, norm_g, w_in, mla_q_norm, mla_kv_norm, w_uq, w_ukv,
              hgrn_lb, hgrn_o_norm, w_br_mla, w_br_hgrn, w_out, w_ffn_in, w_ffn_out):
    bsz, length, _ = x.shape
    rows = length // GRID_W
    rope = axial_rope_tables(rows)
    lb_all = jnp.cumsum(jax.nn.softmax(hgrn_lb.astype(jnp.float32), axis=0), axis=0)
    zero_state = jnp.zeros((bsz, HG_HEADS, HG_FDIM, HG_IDIM), jnp.float32)
    for layer in range(DEPTH):
        last = layer == DEPTH - 1
        mod = jax.nn.silu(c) @ w_mod[layer] + b_mod[layer]
        mod_c = jax.nn.silu(c_ctx) @ w_mod[layer] + b_mod[layer]
        sh_a, sc_a, gt_a, sh_f, sc_f, gt_f = [t[:, None, :] for t in jnp.split(mod, N_MOD, axis=-1)]
        csh_a, csc_a, cgt_a, csh_f, csc_f, cgt_f = jnp.split(mod_c, N_MOD, axis=-1)

        h = modulate(rmsnorm(x, norm_g[layer, 0]), sh_a, sc_a)
        hc = modulate(rmsnorm(ctx, norm_g[layer, 0]), csh_a, csc_a)
        (q_l, k_l, v_l), hg_l, gl_l = mixer_projections(
            h, w_in[layer], mla_q_norm[layer], mla_kv_norm[layer], w_uq[layer], w_ukv[layer], lb_all[layer], rope)
        (q_c, k_c, v_c), hg_c, gl_c = mixer_projections(
            hc, w_in[layer], mla_q_norm[layer], mla_kv_norm[layer], w_uq[layer], w_ukv[layer], lb_all[layer], None)
        attn_l = block_attention(q_l, jnp.concatenate([k_c, k_l], axis=1), jnp.concatenate([v_c, v_l], axis=1))
        hgo_c, s_f, s_b = hgrn2_bidirectional(*hg_c[:6], zero_state, zero_state)
        hgo_l, _, _ = hgrn2_bidirectional(*hg_l[:6], s_f, s_b)
        out_l = merge_branches(attn_l, hgo_l, hg_l[6], gl_l, hgrn_o_norm[layer],
                               w_br_mla[layer], w_br_hgrn[layer], w_out[layer])
        x = x + gt_a * rmsnorm(out_l, norm_g[layer, 1])
        h = modulate(rmsnorm(x, norm_g[layer, 2]), sh_f, sc_f)
        x = x + gt_f * rmsnorm(swiglu(h, w_ffn_in[layer], w_ffn_out[layer]), norm_g[layer, 3])

        if not last:
            attn_c = block_attention(q_c, k_c, v_c)
            out_c = merge_branches(attn_c, hgo_c, hg_c[6], gl_c, hgrn_o_norm[layer],
                                   w_br_mla[layer], w_br_hgrn[layer], w_out[layer])
            ctx = ctx + cgt_a * rmsnorm(out_c, norm_g[layer, 1])
            hc = modulate(rmsnorm(ctx, norm_g[layer, 2]), csh_f, csc_f)
            ctx = ctx + cgt_f * rmsnorm(swiglu(hc, w_ffn_in[layer], w_ffn_out[layer]), norm_g[layer, 3])
    return x
```

```python
import os
import numpy as np
from contextlib import ExitStack
import concourse.bass as bass
import concourse.mybir as mybir
from concourse.bass_utils import run_bass_kernel_spmd

F32 = mybir.dt.float32
BF16 = mybir.dt.bfloat16
AF = mybir.ActivationFunctionType
ALU = mybir.AluOpType

D = 2048
KC = 16
T = 512
L = 8192
LH = 4096
CTX = 256
NKEY = L + CTX
NKT = NKEY // 128
DFF = 5632
EPS = 1e-6
NCH = NKEY // 64
PHASES = os.environ.get("MK_PHASES", "WABHC")


class Buf:
    __slots__ = ('name', 'w', 'r', 'dsem', 'dcnt', 'psum')

    def __init__(self, name):
        self.name = name
        self.w = None
        self.r = {}
        self.dsem = None
        self.dcnt = 0
        self.psum = False


class TB:
    def __init__(self, t, name):
        self.t = t
        self.b = Buf(name)


def _b(x):
    return x.b if isinstance(x, TB) else x


class Sched:
    def __init__(self, nc, stack):
        self.nc = nc
        self.stack = stack
        self.engs = ['pe', 'act', 'dve', 'pool', 'sp']
        self.sem = {e: stack.enter_context(nc.semaphore('s_' + e)) for e in self.engs}
        self.cnt = {e: 0 for e in self.engs}
        self.prog = {e: [] for e in self.engs}
        self.waited = {}
        self.anchors = []
        self.nsem = 0

    def _deps(self, eng, reads, writes, skip=None):
        deps = []
        for b in reads:
            b = _b(b)
            if b.w is not None:
                deps.append(b.w)
            if b.psum:
                for t in b.r.values():
                    if t[2] != eng:
                        deps.append(t)
        for b in writes:
            b = _b(b)
            if b.w is not None and b.w[2] != eng:
                deps.append(b.w)
            for t in b.r.values():
                if t[2] != eng:
                    deps.append(t)
        for (s, v, src) in deps:
            if src == eng and eng == 'pe':
                continue
            if skip is not None and s is skip:
                continue
            key = (eng, id(s))
            if self.waited.get(key, 0) >= v:
                continue
            self.waited[key] = v
            self.prog[eng].append(lambda e, s=s, v=v: e.wait_ge(s, v))

    def _mark(self, tok, reads, writes):
        for b in reads:
            _b(b).r[id(tok[0])] = tok
        for b in writes:
            b = _b(b)
            b.w = tok
            b.r = {}

    def op(self, eng, fn, reads=(), writes=()):
        self._deps(eng, reads, writes)
        if self.cnt[eng] >= 30000:
            self.nsem += 1
            self.sem[eng] = self.stack.enter_context(self.nc.semaphore('s_%s_%d' % (eng, self.nsem)))
            self.cnt[eng] = 0
        self.cnt[eng] += 1
        s = self.sem[eng]
        tok = (s, self.cnt[eng], eng)
        self.prog[eng].append(lambda e, fn=fn, s=s: fn(e).then_inc(s, 1))
        self._mark(tok, reads, writes)

    def dma(self, eng, out, in_, anchor, reads=(), writes=()):
        a = _b(anchor)
        if a.dsem is None:
            a.dsem = self.stack.enter_context(self.nc.semaphore('d_' + a.name))
            self.anchors.append(a)
        self._deps(eng, reads, writes, skip=a.dsem)
        a.dcnt += 16
        tok = (a.dsem, a.dcnt, 'dma')
        self.prog[eng].append(lambda e, s=a.dsem, o=out, i=in_: e.dma_start(out=o, in_=i).then_inc(s, 16))
        self._mark(tok, reads, writes)

    def drain(self, eng):
        for a in self.anchors:
            if a.dcnt > 0:
                key = (eng, id(a.dsem))
                if self.waited.get(key, 0) >= a.dcnt:
                    continue
                self.waited[key] = a.dcnt
                self.prog[eng].append(lambda e, s=a.dsem, v=a.dcnt: e.wait_ge(s, v))

    def emit(self):
        nc = self.nc
        prog = self.prog
        with nc.Block() as block:
            @block.tensor
            def _(e):
                for t in prog['pe']:
                    t(e)

            @block.scalar
            def _(e):
                for t in prog['act']:
                    t(e)

            @block.vector
            def _(e):
                for t in prog['dve']:
                    t(e)

            @block.gpsimd
            def _(e):
                for t in prog['pool']:
                    t(e)

            @block.sync
            def _(e):
                for t in prog['sp']:
                    t(e)
        self.prog = {e: [] for e in self.engs}


class Rot:
    def __init__(self, items):
        self.items = items
        self.i = 0

    def next(self):
        x = self.items[self.i % len(self.items)]
        self.i += 1
        return x


def build_nc(debug=False):
    nc = bass.Bass("TRN2", target_bir_lowering=False)

    def din(name, shape, dt=F32):
        return nc.dram_tensor(name, shape, dt, kind="ExternalInput").ap()

    dbg_out = set(os.environ.get("MK_DBG", "").split(",")) if debug else set()

    def dscr(name, shape, dt=BF16):
        ext = debug and (name in dbg_out or "ALL" in dbg_out)
        return nc.dram_tensor(name, shape, dt, kind=("ExternalOutput" if ext else "Internal")).ap()

    xo = din("xo", [LH, D])
    xt = din("xt", [LH, D])
    cx = din("cx", [CTX, D])
    cc = din("cc", [128, KC, 2])
    w_mod = din("w_mod", [D, 6 * D])
    b_mod = din("b_mod", [1, 6 * D])
    ng_fm = din("ng_fm", [128, 4, KC])
    ng_row = din("ng_row", [1, 4 * D])
    w_in = din("w_in", [D, 10048])
    qng = din("qng", [128, 4])
    kvng = din("kvng", [128, 2])
    w_uq = din("w_uq", [512, 1536])
    w_ukv = din("w_ukv", [256, 2048])
    lbT = din("lbT", [128, 2, 16])
    ong = din("ong", [128, 1])
    w_br_mla = din("w_br_mla", [1024, D])
    w_br_hgrn = din("w_br_hgrn", [1024, D])
    w_out = din("w_out", [D, D])
    w_ffn_in = din("w_ffn_in", [D, 2 * DFF])
    w_ffn_out = din("w_ffn_out", [DFF, D])
    cos2 = din("cos2", [128, L])
    sin2 = din("sin2", [128, L])
    y = nc.dram_tensor("y", [LH, D], F32, kind="ExternalOutput").ap()

    W1 = dscr("W1", [20, 128, KC * 512])
    WBA = dscr("WBA", [4, 128, 8 * 512])
    WBH = dscr("WBH", [4, 128, 8 * 512])
    WO = dscr("WO", [4, 128, KC * 512])
    WF1 = dscr("WF1", [22, 128, KC * 512])
    WF2 = dscr("WF2", [16, 128, 11 * 512])
    KTn = dscr("KTn", [8, 128, NKEY])
    KTr = dscr("KTr", [128, NKEY])
    VH = dscr("VH", [8, 128, NKT, 128])
    QTn = dscr("QTn", [8, 128, LH])
    QTr = dscr("QTr", [4, 128, LH])
    KD = dscr("KD", [2, 8, 128, NKEY])
    QD = dscr("QD", [2, 8, 128, LH])
    HV = dscr("HV", [NKEY, 1024])
    GH = dscr("GH", [8, 128, LH])
    GG = dscr("GG", [32, 128, LH])
    AT = dscr("AT", [8, 128, LH])
    O1 = dscr("O1", [8, 128, LH], F32)
    HO = dscr("HO", [8, 128, LH])
    GAB = dscr("GAB", [2, 128, D], F32)

    with ExitStack() as top:
        S = Sched(nc, top)

        def sbt(st, name, shape, dt):
            return TB(st.enter_context(nc.sbuf_tensor(name, shape, dt)), name)

        def pst(st, name, shape, dt):
            tb = TB(st.enter_context(nc.psum_tensor(name, shape, dt)), name)
            tb.b.psum = True
            return tb

        def rsqrt(out_ap, in_ap, const, reads, outb):
            S.op('act', lambda e: e.activation(out=out_ap, in_=in_ap, func=AF.Sqrt, bias=float(const), scale=1.0),
                 reads=reads, writes=(outb,))
            S.op('dve', lambda e: e.reciprocal(out=out_ap, in_=out_ap), reads=(outb,), writes=(outb,))

        ident = sbt(top, "ident", [128, 128], BF16)
        ones_bf = sbt(top, "ones_bf", [128, 128], BF16)
        ones_f = sbt(top, "ones_f", [1, 128], F32)
        mskF = sbt(top, "mskF", [128, 512], F32)
        mask1 = sbt(top, "mask1", [128, 128], F32)
        mask2 = sbt(top, "mask2", [128, 128], F32)
        modT = sbt(top, "modT", [128, 256], F32)
        gsA = sbt(top, "gsA", [128, KC], F32)
        shA = sbt(top, "shA", [128, KC], F32)
        gsC = sbt(top, "gsC", [128, KC], F32)
        shC = sbt(top, "shC", [128, KC], F32)
        gsF = sbt(top, "gsF", [128, KC], F32)
        shF = sbt(top, "shF", [128, KC], F32)
        ngf = sbt(top, "ngf", [128, 4, KC], F32)
        oml = sbt(top, "oml", [128, 16], F32)
        lbt = sbt(top, "lbt", [128, 2, 16], F32)
        ongs = sbt(top, "ongs", [128, 1], F32)
        stAH = ExitStack()
        dec = sbt(stAH, "dec", [128, 2, 8, NCH], F32)
        PB = [pst(top, "pb%d" % i, [128, 512], F32) for i in range(8)]

        S.op('pool', lambda e: e.memset(ident.t[:, :], 1.0), writes=(ident,))
        S.op('pool', lambda e: e.affine_select(out=ident.t[:, :], in_=ident.t[:, :], pattern=[[-1, 128]],
                                               compare_op=ALU.is_equal, fill=0.0, base=0, channel_multiplier=1),
             reads=(ident,), writes=(ident,))
        S.op('pool', lambda e: e.memset(ones_bf.t[:, :], 1.0), writes=(ones_bf,))
        S.op('pool', lambda e: e.memset(ones_f.t[:, :], 1.0), writes=(ones_f,))
        S.op('pool', lambda e: e.memset(mskF.t[:, :], 1.0), writes=(mskF,))
        mv = mskF.t[:, :].rearrange("p (c t) -> p c t", t=64)
        S.op('pool', lambda e: e.memset(mv[:, :, 0:1], 0.0), reads=(mskF,), writes=(mskF,))
        for (mk, sign) in ((mask1, 1), (mask2, -1)):
            S.op('pool', lambda e, mk=mk: e.memset(mk.t[:, :], 1.0), writes=(mk,))
            S.op('pool', lambda e, mk=mk, sign=sign: e.affine_select(
                out=mk.t[:, :], in_=mk.t[:, :], pattern=[[sign, 128]], compare_op=ALU.is_ge, fill=0.0,
                base=0, channel_multiplier=-sign), reads=(mk,), writes=(mk,))
            S.op('pool', lambda e, mk=mk: e.memset(mk.t[0:64, 64:128], 0.0), reads=(mk,), writes=(mk,))
            S.op('pool', lambda e, mk=mk: e.memset(mk.t[64:128, 0:64], 0.0), reads=(mk,), writes=(mk,))

        if 'W' in PHASES:
            with ExitStack() as st:
                stf = [sbt(st, "stf%d" % i, [128, KC, 512], F32) for i in range(2)]
                stb = [sbt(st, "stb%d" % i, [128, KC, 512], BF16) for i in range(2)]
                rowp = sbt(st, "rowp", [1, 512], F32)
                rowg = sbt(st, "rowg", [1, 512], F32)
                bmp = [sbt(st, "bmp%d" % i, [1, 512], F32) for i in range(2)]
                gpc = [sbt(st, "gpc%d" % i, [1, 512], F32) for i in range(2)]
                cct = sbt(st, "cct", [128, KC, 2], F32)
                scT = sbt(st, "scT", [128, KC, 2], BF16)
                one2 = sbt(st, "one2", [1, 2], F32)
                tmpm = sbt(st, "tmpm", [128, KC], F32)
                gtm = [sbt(st, "gtm%d" % i, [128, 512], F32) for i in range(2)]
                uidx = [0]
                cast_engs = ['act', 'dve']

                def cast(eng, o, i, rd, wr, scale=None):
                    if eng == 'act':
                        if scale is None:
                            S.op('act', lambda e: e.activation(out=o, in_=i, func=AF.Copy), reads=rd, writes=wr)
                        else:
                            S.op('act', lambda e: e.activation(out=o, in_=i, func=AF.Copy, scale=scale), reads=rd,
                                 writes=wr)
                    else:
                        if scale is None:
                            S.op(eng, lambda e: e.tensor_copy(out=o, in_=i), reads=rd, writes=wr)
                        else:
                            S.op(eng, lambda e: e.tensor_scalar(out=o, in0=i, scalar1=scale, scalar2=None,
                                                                 op0=ALU.mult), reads=rd, writes=wr)

                def precast(src, r0, nk, c0, ncol, dst):
                    u = uidx[0]
                    uidx[0] += 1
                    f = stf[u % 2]
                    b = stb[u % 2]
                    srcv = src[r0:r0 + nk * 128, c0:c0 + ncol].rearrange("(k p) c -> p k c", p=128)
                    S.dma('sp', f.t[:, 0:nk, 0:ncol], srcv, anchor=f, writes=(f,))
                    return u, f, b

                def simple_unit(src, r0, nk, c0, dst):
                    u, f, b = precast(src, r0, nk, c0, 512, dst)
                    h = nk // 2
                    cast('act', b.t[:, 0:h, :], f.t[:, 0:h, :], (f,), (b,))
                    cast('dve', b.t[:, h:nk, :], f.t[:, h:nk, :], (f,), (b,))
                    S.dma('pool', dst.rearrange("p (k c) -> p k c", c=512), b.t[:, 0:nk, :], anchor=b, reads=(b,))

                S.dma('sp', cct.t[:, :, :], cc[:, :, :], anchor=cct, writes=(cct,))
                S.op('act', lambda e: e.activation(out=scT.t[:, :, :], in_=cct.t[:, :, :], func=AF.Silu),
                     reads=(cct,), writes=(scT,))
                S.op('pool', lambda e: e.memset(one2.t[:, :], 1.0), writes=(one2,))
                S.dma('sp', ngf.t[:, :, :], ng_fm[:, :, :], anchor=ngf, writes=(ngf,))
                S.dma('sp', lbt.t[:, :, :], lbT[:, :, :], anchor=lbt, writes=(lbt,))
                S.dma('sp', ongs.t[:, :], ong[:, :], anchor=ongs, writes=(ongs,))
                modps = PB[7]
                mrot = Rot([PB[0], PB[1], PB[2], PB[3]])
                brot = Rot([PB[4], PB[5]])
                for blk in range(24):
                    which = blk // 4
                    u, f, b = precast(w_mod, 0, KC, blk * 512, 512, None)
                    cast('act', b.t[:, 0:8, :], f.t[:, 0:8, :], (f,), (b,))
                    cast('dve', b.t[:, 8:16, :], f.t[:, 8:16, :], (f,), (b,))
                    bp = bmp[blk % 2]
                    S.dma('sp', bp.t[:, :], b_mod[0:1, blk * 512:(blk + 1) * 512], anchor=bp, writes=(bp,))
                    for m in range(2):
                        if m == 1 and which >= 2:
                            continue
                        pb = mrot.next()
                        for k in range(KC):
                            S.op('pe', lambda e, pb=pb, k=k, m=m, b=b: e.matmul(
                                pb.t[0:1, :], lhsT=scT.t[:, k, m:m + 1], rhs=b.t[:, k, :], start=(k == 0),
                                stop=(k == KC - 1)), reads=(scT, b), writes=(pb,))
                        S.op('dve', lambda e, pb=pb, bp=bp: e.tensor_tensor(out=rowp.t[:, :], in0=pb.t[0:1, :],
                                                                            in1=bp.t[:, :], op=ALU.add),
                             reads=(pb, bp), writes=(rowp,))
                        if which in (0, 1, 3, 4):
                            for j in range(4):
                                k = (blk % 4) * 4 + j
                                col = ((m * 6 + which) * KC + k)
                                S.op('pe', lambda e, j=j, col=col: e.matmul(
                                    modps.t[:, 2 * col:2 * col + 2], lhsT=rowp.t[0:1, j * 128:(j + 1) * 128],
                                    rhs=one2.t[0:1, 0:2], start=True, stop=True), reads=(rowp, one2),
                                     writes=(modps,))
                        elif m == 0:
                            gi = 1 if which == 2 else 3
                            gp = gpc[blk % 2]
                            cb = (blk % 4) * 512
                            S.dma('sp', gp.t[:, :], ng_row[0:1, gi * D + cb: gi * D + cb + 512], anchor=gp,
                                  writes=(gp,))
                            S.op('dve', lambda e, gp=gp: e.scalar_tensor_tensor(
                                out=rowg.t[:, :], in0=rowp.t[:, :], scalar=float(np.sqrt(D)), in1=gp.t[:, :],
                                op0=ALU.mult, op1=ALU.mult), reads=(rowp, gp), writes=(rowg,))
                            pbb = brot.next()
                            S.op('pe', lambda e, pbb=pbb: e.matmul(pbb.t[:, :], lhsT=ones_f.t[0:1, :],
                                                                   rhs=rowg.t[0:1, :], start=True, stop=True),
                                 reads=(ones_f, rowg), writes=(pbb,))
                            gt_ = gtm[blk % 2]
                            S.op('act', lambda e, pbb=pbb, gt_=gt_: e.activation(
                                out=gt_.t[:, :], in_=pbb.t[:, :], func=AF.Copy), reads=(pbb,), writes=(gt_,))
                            S.dma('pool', GAB[0 if which == 2 else 1, :, cb:cb + 512], gt_.t[:, :], anchor=gt_,
                                  reads=(gt_,))
                S.op('dve', lambda e: e.tensor_copy(out=modT.t[:, 0:192], in_=modps.t[:, 0:384:2]), reads=(modps,),
                     writes=(modT,))

                def mcol(m, which):
                    c0 = (m * 6 + which) * KC
                    return modT.t[:, c0:c0 + KC]

                sq = float(np.sqrt(D))
                for (gs_, sh_, m, wsc, wsh, gi) in ((gsA, shA, 0, 1, 0, 0), (gsC, shC, 1, 1, 0, 0),
                                                    (gsF, shF, 0, 4, 3, 2)):
                    S.op('dve', lambda e, m=m, wsc=wsc: e.tensor_scalar(
                        out=tmpm.t[:, :], in0=mcol(m, wsc), scalar1=1.0, scalar2=sq, op0=ALU.add, op1=ALU.mult),
                         reads=(modT,), writes=(tmpm,))
                    S.op('dve', lambda e, gs_=gs_, gi=gi: e.tensor_tensor(out=gs_.t[:, :], in0=tmpm.t[:, :],
                                                                          in1=ngf.t[:, gi, :], op=ALU.mult),
                         reads=(tmpm, ngf), writes=(gs_,))
                    S.op('dve', lambda e, sh_=sh_, m=m, wsh=wsh: e.tensor_copy(out=sh_.t[:, :], in_=mcol(m, wsh)),
                         reads=(modT,), writes=(sh_,))
                S.op('dve', lambda e: e.tensor_tensor(out=oml.t[:, :], in0=lbt.t[:, 0, :], in1=lbt.t[:, 1, :],
                                                      op=ALU.subtract), reads=(lbt,), writes=(oml,))
                S.op('act', lambda e: e.activation(out=oml.t[:, :], in_=oml.t[:, :], func=AF.Sigmoid, scale=-1.0),
                     reads=(oml,), writes=(oml,))
                S.op('dve', lambda e: e.tensor_scalar(out=ongs.t[:, :], in0=ongs.t[:, :],
                                                      scalar1=float(np.sqrt(128.0)), scalar2=None, op0=ALU.mult),
                     reads=(ongs,), writes=(ongs,))

                simple_unit(w_in, 0, KC, 0, W1[0])
                u, f, b = precast(w_in, 0, KC, 512, 320, None)
                cast('act', b.t[:, :, 0:256], f.t[:, :, 0:256], (f,), (b,))
                cast('dve', b.t[:, :, 256:320], f.t[:, :, 256:320], (f,), (b,))
                cast('dve', b.t[:, :, 320:384], f.t[:, :, 256:320], (f,), (b,))
                for o in (384, 448):
                    cast('dve', b.t[:, :, o:o + 32], f.t[:, :, 288:320], (f,), (b,), scale=-1.0)
                    cast('dve', b.t[:, :, o + 32:o + 64], f.t[:, :, 256:288], (f,), (b,))
                S.dma('pool', W1[1].rearrange("p (k c) -> p k c", c=512), b.t[:, :, :], anchor=b, reads=(b,))
                for blk in range(2, 20):
                    simple_unit(w_in, 0, KC, 832 + (blk - 2) * 512, W1[blk])
                for blk in range(4):
                    simple_unit(w_br_mla, 0, 8, blk * 512, WBA[blk])
                    simple_unit(w_br_hgrn, 0, 8, blk * 512, WBH[blk])
                    simple_unit(w_out, 0, KC, blk * 512, WO[blk])
                for blk in range(22):
                    simple_unit(w_ffn_in, 0, KC, blk * 512, WF1[blk])
                for blk in range(4):
                    for kp in range(4):
                        simple_unit(w_ffn_out, kp * 11 * 128, 11, blk * 512, WF2[blk * 4 + kp])
                S.emit()

        if 'A' in PHASES:
            with ExitStack() as st:
                wq = sbt(st, "wq", [128, 4, 2048], BF16)
                wkv = sbt(st, "wkv", [128, 2, 8, 256], BF16)
                with ExitStack() as st2:
                    wqf = sbt(st2, "wqf", [128, 4, 8, 192], F32)
                    wkvf = sbt(st2, "wkvf", [128, 2, 2048], F32)
                    qg = sbt(st2, "qg", [128, 4], F32)
                    kvg = sbt(st2, "kvg", [128, 2], F32)
                    S.dma('sp', wqf.t[:, :, :, :], w_uq.rearrange("(k p) (h c) -> p k h c", p=128, c=192),
                          anchor=wqf, writes=(wqf,))
                    S.dma('sp', wkvf.t[:, :, :], w_ukv.rearrange("(k p) c -> p k c", p=128), anchor=wkvf,
                          writes=(wkvf,))
                    S.dma('sp', qg.t[:, :], qng[:, :], anchor=qg, writes=(qg,))
                    S.dma('sp', kvg.t[:, :], kvng[:, :], anchor=kvg, writes=(kvg,))
                    S.op('dve', lambda e: e.tensor_scalar(out=qg.t[:, :], in0=qg.t[:, :],
                                                          scalar1=float(np.sqrt(512.0)), scalar2=None,
                                                          op0=ALU.mult), reads=(qg,), writes=(qg,))
                    S.op('dve', lambda e: e.tensor_scalar(out=kvg.t[:, :], in0=kvg.t[:, :], scalar1=16.0,
                                                          scalar2=None, op0=ALU.mult), reads=(kvg,), writes=(kvg,))
                    wq4 = wq.t[:, :, :].rearrange("p k (j c) -> p k j c", c=128)
                    for k in range(4):
                        g = qg.t[:, k:k + 1]
                        S.op('dve', lambda e, k=k, g=g: e.tensor_scalar(
                            out=wq4[:, k, 0:8, :], in0=wqf.t[:, k, :, 0:128], scalar1=g, scalar2=None,
                            op0=ALU.mult), reads=(wqf, qg), writes=(wq,))
                        ropev = wq.t[:, k, 1024:1536].rearrange("p (h c) -> p h c", c=64)
                        S.op('dve', lambda e, k=k, g=g, ropev=ropev: e.tensor_scalar(
                            out=ropev, in0=wqf.t[:, k, :, 128:192], scalar1=g, scalar2=None, op0=ALU.mult),
                             reads=(wqf, qg), writes=(wq,))
                        rotv = wq.t[:, k, 1536:2048].rearrange("p (h c) -> p h c", c=64)
                        S.op('dve', lambda e, k=k, g=g, rotv=rotv: e.tensor_scalar(
                            out=rotv[:, :, 0:32], in0=wqf.t[:, k, :, 160:192], scalar1=g, scalar2=-1.0,
                            op0=ALU.mult, op1=ALU.mult), reads=(wqf, qg), writes=(wq,))
                        S.op('dve', lambda e, k=k, g=g, rotv=rotv: e.tensor_scalar(
                            out=rotv[:, :, 32:64], in0=wqf.t[:, k, :, 128:160], scalar1=g, scalar2=None,
                            op0=ALU.mult), reads=(wqf, qg), writes=(wq,))
                    for k in range(2):
                        g = kvg.t[:, k:k + 1]
                        S.op('dve', lambda e, k=k, g=g: e.tensor_scalar(
                            out=wkv.t[:, k, :, :], in0=wkvf.t[:, k, :].rearrange("p (h c) -> p h c", c=256),
                            scalar1=g, scalar2=None, op0=ALU.mult), reads=(wkvf, kvg), writes=(wkv,))
                    S.emit()
                wsl = [sbt(st, "wsl%d" % i, [128, KC, 512], BF16) for i in range(3)]
                wrot = Rot(wsl)
                hTs = [sbt(st, "hT%d" % i, [128, KC, T], BF16) for i in range(2)]
                xts = [sbt(st, "xt%d" % i, [128, D], F32) for i in range(2)]
                xnb = [sbt(st, "xn%d" % i, [128, D], BF16) for i in range(2)]
                ssq = sbt(st, "ssq", [128, 4], F32)
                rstd = sbt(st, "rstd", [128, 4], F32)
                cq = sbt(st, "cq", [128, 4, T], BF16)
                sqq = sbt(st, "sqq", [128, 4, T], BF16)
                ckv = sbt(st, "ckv", [128, 2, T], BF16)
                sqk = sbt(st, "sqk", [128, 2, T], BF16)
                rq = sbt(st, "rq", [128, T], F32)
                rkv = sbt(st, "rkv", [128, T], F32)
                rkvt = sbt(st, "rkvt", [128, 4], F32)
                crq = sbt(st, "crq", [128, T], F32)
                srq = sbt(st, "srq", [128, T], F32)
                cts = [sbt(st, "ct%d" % i, [128, T], F32) for i in range(1)]
                sts = [sbt(st, "st%d" % i, [128, T], F32) for i in range(1)]
                qs = sbt(st, "qs", [128, 8, T], BF16)
                stg = Rot([sbt(st, "stg%d" % i, [128, T], BF16) for i in range(10)])
                tmp = Rot([sbt(st, "tmp%d" % i, [128, T], F32) for i in range(8)])
                trb = [PB[6], PB[7]]
                trrot = Rot(trb)
                arot = Rot([PB[0], PB[1], PB[2], PB[3]])
                srot = Rot([PB[4], PB[5]])

                tiles = [('ctx', cx, 0, CTX, 0, None)]
                for i in range(8):
                    tiles.append(('oth', xt, i * T, T, CTX + LH + i * T, LH + i * T))
                for i in range(8):
                    tiles.append(('own', xo, i * T, T, CTX + i * T, i * T))
                if debug and os.environ.get("MK_NT"):
                    nt_ = int(os.environ["MK_NT"])
                    no_ = int(os.environ.get("MK_NO", "1"))
                    tiles = tiles[:1] + tiles[1:1 + no_] + tiles[9:9 + nt_]

                def load_sub(ti, s):
                    kind, src, r0, n, kr0, tp = tiles[ti]
                    xtile = xts[s % 2]
                    S.dma('sp', xtile.t[:, :], src[r0 + s * 128:r0 + (s + 1) * 128, :], anchor=xtile,
                          writes=(xtile,))

                def prep(ti, hT):
                    kind, src, r0, n, kr0, tp = tiles[ti]
                    gs_ = gsC if kind == 'ctx' else gsA
                    sh_ = shC if kind == 'ctx' else shA
                    nsub = n // 128
                    for s in range(nsub):
                        xtile = xts[s % 2]
                        xn = xnb[s % 2]
                        S.op('act', lambda e, xtile=xtile, xn=xn, s=s: e.activation(
                            out=xn.t[:, :], in_=xtile.t[:, :], func=AF.Square, accum_out=ssq.t[:, s:s + 1]),
                             reads=(xtile,), writes=(xn, ssq))
                        rsqrt(rstd.t[:, s:s + 1], ssq.t[:, s:s + 1], D * EPS, (ssq,), rstd)
                        S.op('dve', lambda e, xtile=xtile, xn=xn, s=s: e.tensor_scalar(
                            out=xn.t[:, :], in0=xtile.t[:, :], scalar1=rstd.t[:, s:s + 1], scalar2=None,
                            op0=ALU.mult), reads=(xtile, rstd, xn), writes=(xn,))
                        if s + 2 < nsub:
                            load_sub(ti, s + 2)
                        elif ti + 1 < len(tiles):
                            s2 = s + 2 - nsub
                            if s2 < tiles[ti + 1][3] // 128:
                                load_sub(ti + 1, s2)
                        for g in range(2):
                            pb = trrot.next()
                            pbv = pb.t[:, :].bitcast(BF16)
                            for kk in range(8):
                                k = g * 8 + kk
                                S.op('pe', lambda e, pbv=pbv, kk=kk, k=k, xn=xn: e.transpose(
                                    out=pbv[:, kk * 128:(kk + 1) * 128], in_=xn.t[:, k * 128:(k + 1) * 128],
                                    identity=ident.t[:, :]), reads=(xn, ident), writes=(pb,))
                            for kk in range(8):
                                k = g * 8 + kk
                                S.op('act', lambda e, pbv=pbv, kk=kk, k=k, s=s, hT=hT, gs_=gs_, sh_=sh_: e.activation(
                                    out=hT.t[:, k, s * 128:(s + 1) * 128], in_=pbv[:, kk * 128:(kk + 1) * 128],
                                    func=AF.Identity, scale=gs_.t[:, k:k + 1], bias=sh_.t[:, k:k + 1]),
                                     reads=(pb, gs_, sh_), writes=(hT,))

                def load_w(blk):
                    w = wrot.next()
                    S.dma('sp', w.t[:, :, :], W1[blk].rearrange("p (k c) -> p k c", c=512), anchor=w, writes=(w,))
                    return w

                def mm_fm(w, c, hT, n, pb):
                    for k in range(KC):
                        S.op('pe', lambda e, k=k: e.matmul(pb.t[:, 0:n], lhsT=w.t[:, k, c * 128:(c + 1) * 128],
                                                           rhs=hT.t[:, k, 0:n], start=(k == 0), stop=(k == KC - 1)),
                             reads=(w, hT), writes=(pb,))

                def store(sg, dst, n):
                    S.dma('pool', dst, sg.t[:, 0:n], anchor=sg, reads=(sg,))

                def act_evac(func, pb, n, out, outb, scale=None):
                    if scale is None:
                        S.op('act', lambda e: e.activation(out=out, in_=pb.t[:, 0:n], func=func), reads=(pb,),
                             writes=(outb,))
                    else:
                        S.op('act', lambda e: e.activation(out=out, in_=pb.t[:, 0:n], func=func, scale=scale),
                             reads=(pb,), writes=(outb,))

                def do_tile(ti, hT, mid):
                    kind, src, r0, n, kr0, tp = tiles[ti]
                    nsub = n // 128
                    own = kind == 'own'
                    ct = cts[0]
                    stt = sts[0]
                    if kind != 'ctx':
                        S.dma('sp', ct.t[:, 0:n], cos2[:, tp:tp + n], anchor=ct, writes=(ct,))
                        S.dma('sp', stt.t[:, 0:n], sin2[:, tp:tp + n], anchor=stt, writes=(stt,))
                    w = load_w(1)
                    for i in range(2):
                        pb = arot.next()
                        mm_fm(w, i, hT, n, pb)
                        act_evac(AF.Copy, pb, n, ckv.t[:, i, 0:n], ckv)
                        act_evac(AF.Square, pb, n, sqk.t[:, i, 0:n], sqk)
                    pbA = arot.next()
                    mm_fm(w, 2, hT, n, pbA)
                    if kind == 'ctx':
                        sg = stg.next()
                        act_evac(AF.Copy, pbA, n, sg.t[:, 0:n], sg)
                        store(sg, KTr[:, kr0:kr0 + n], n)
                    else:
                        pbB = arot.next()
                        mm_fm(w, 3, hT, n, pbB)
                        t1 = tmp.next()
                        t2 = tmp.next()
                        S.op('dve', lambda e, t1=t1, pbA=pbA: e.tensor_tensor(out=t1.t[:, 0:n], in0=pbA.t[:, 0:n], in1=ct.t[:, 0:n],
                                                              op=ALU.mult), reads=(pbA, ct), writes=(t1,))
                        S.op('dve', lambda e, t2=t2, pbB=pbB: e.tensor_tensor(out=t2.t[:, 0:n], in0=pbB.t[:, 0:n], in1=stt.t[:, 0:n],
                                                              op=ALU.mult), reads=(pbB, stt), writes=(t2,))
                        sg = stg.next()
                        S.op('pool', lambda e, sg=sg, t1=t1, t2=t2: e.tensor_tensor(out=sg.t[:, 0:n], in0=t1.t[:, 0:n], in1=t2.t[:, 0:n],
                                                               op=ALU.add), reads=(t1, t2), writes=(sg,))
                        store(sg, KTr[:, kr0:kr0 + n], n)
                    pbs = srot.next()
                    for i in range(2):
                        S.op('pe', lambda e, i=i, pbs=pbs: e.matmul(pbs.t[:, 0:n], lhsT=ones_bf.t[:, :], rhs=sqk.t[:, i, 0:n],
                                                           start=(i == 0), stop=(i == 1)), reads=(ones_bf, sqk),
                             writes=(pbs,))
                    rsqrt(rkv.t[:, 0:n], pbs.t[:, 0:n], 256 * EPS, (pbs,), rkv)
                    pbt = srot.next()
                    for s in range(nsub):
                        for i in range(2):
                            S.op('pe', lambda e, i=i, s=s, pbt=pbt: e.matmul(
                                pbt.t[:, 2 * s:2 * s + 2], lhsT=sqk.t[:, i, s * 128:(s + 1) * 128],
                                rhs=ones_bf.t[:, 0:2], start=(i == 0), stop=(i == 1)), reads=(ones_bf, sqk),
                                 writes=(pbt,))
                    rsqrt(rkvt.t[:, 0:nsub], pbt.t[:, 0:2 * nsub:2], 256 * EPS, (pbt,), rkvt)
                    for h in range(8):
                        pb = arot.next()
                        for k in range(2):
                            S.op('pe', lambda e, k=k, h=h, pb=pb: e.matmul(
                                pb.t[:, 0:n], lhsT=wkv.t[:, k, h, 0:128], rhs=ckv.t[:, k, 0:n], start=(k == 0),
                                stop=(k == 1)), reads=(wkv, ckv), writes=(pb,))
                        sg = stg.next()
                        S.op('dve', lambda e, pb=pb, sg=sg: e.tensor_tensor(out=sg.t[:, 0:n], in0=pb.t[:, 0:n],
                                                                            in1=rkv.t[:, 0:n], op=ALU.mult),
                             reads=(pb, rkv), writes=(sg,))
                        store(sg, KTn[h, :, kr0:kr0 + n], n)
                    for s in range(nsub):
                        kt = (kr0 + s * 128) // 128
                        for j in range(2):
                            pb = arot.next()
                            for k in range(2):
                                S.op('pe', lambda e, k=k, j=j, s=s, pb=pb: e.matmul(
                                    pb.t[:, :].rearrange("p (h c) -> p h c", c=128),
                                    lhsT=ckv.t[:, k, s * 128:(s + 1) * 128], rhs=wkv.t[:, k, 4 * j:4 * j + 4, 128:256],
                                    start=(k == 0), stop=(k == 1)), reads=(wkv, ckv), writes=(pb,))
                            sg = stg.next()
                            S.op('act', lambda e, pb=pb, sg=sg, s=s: e.activation(
                                out=sg.t[:, :], in_=pb.t[:, :], func=AF.Copy, scale=rkvt.t[:, s:s + 1]),
                                 reads=(pb, rkvt), writes=(sg,))
                            S.dma('pool', VH[4 * j:4 * j + 4, :, kt, :].rearrange("h p c -> p h c"),
                                  sg.t[:, :].rearrange("p (h c) -> p h c", c=128), anchor=sg, reads=(sg,))
                    if own:
                        w = load_w(0)
                        for i in range(4):
                            pb = arot.next()
                            mm_fm(w, i, hT, n, pb)
                            act_evac(AF.Copy, pb, n, cq.t[:, i, 0:n], cq)
                            act_evac(AF.Square, pb, n, sqq.t[:, i, 0:n], sqq)
                        pbs = srot.next()
                        for i in range(4):
                            S.op('pe', lambda e, i=i, pbs=pbs: e.matmul(pbs.t[:, 0:n], lhsT=ones_bf.t[:, :],
                                                               rhs=sqq.t[:, i, 0:n], start=(i == 0), stop=(i == 3)),
                                 reads=(ones_bf, sqq), writes=(pbs,))
                        rsqrt(rq.t[:, 0:n], pbs.t[:, 0:n], 512 * EPS, (pbs,), rq)
                        S.op('dve', lambda e: e.tensor_tensor(out=crq.t[:, 0:n], in0=ct.t[:, 0:n], in1=rq.t[:, 0:n],
                                                              op=ALU.mult), reads=(ct, rq), writes=(crq,))
                        S.op('dve', lambda e: e.tensor_tensor(out=srq.t[:, 0:n], in0=stt.t[:, 0:n], in1=rq.t[:, 0:n],
                                                              op=ALU.mult), reads=(stt, rq), writes=(srq,))

                        def qmm(j, pb):
                            for k in range(4):
                                S.op('pe', lambda e, k=k: e.matmul(pb.t[:, 0:n], lhsT=wq.t[:, k, j * 128:(j + 1) * 128],
                                                                   rhs=cq.t[:, k, 0:n], start=(k == 0), stop=(k == 3)),
                                     reads=(wq, cq), writes=(pb,))

                        for j in range(8):
                            pb = arot.next()
                            qmm(j, pb)
                            sg = stg.next()
                            S.op('dve', lambda e, pb=pb, sg=sg: e.tensor_tensor(out=sg.t[:, 0:n], in0=pb.t[:, 0:n],
                                                                                in1=rq.t[:, 0:n], op=ALU.mult),
                                 reads=(pb, rq), writes=(sg,))
                            store(sg, QTn[j, :, tp:tp + n], n)
                        for j in range(4):
                            pbA = arot.next()
                            qmm(8 + j, pbA)
                            pbB = arot.next()
                            qmm(12 + j, pbB)
                            t1 = tmp.next()
                            t2 = tmp.next()
                            S.op('dve', lambda e, pbA=pbA, t1=t1: e.tensor_tensor(
                                out=t1.t[:, 0:n], in0=pbA.t[:, 0:n], in1=crq.t[:, 0:n], op=ALU.mult),
                                 reads=(pbA, crq), writes=(t1,))
                            S.op('dve', lambda e, pbB=pbB, t2=t2: e.tensor_tensor(
                                out=t2.t[:, 0:n], in0=pbB.t[:, 0:n], in1=srq.t[:, 0:n], op=ALU.mult),
                                 reads=(pbB, srq), writes=(t2,))
                            sg = stg.next()
                            S.op('pool', lambda e, sg=sg, t1=t1, t2=t2: e.tensor_tensor(
                                out=sg.t[:, 0:n], in0=t1.t[:, 0:n], in1=t2.t[:, 0:n], op=ALU.add), reads=(t1, t2),
                                 writes=(sg,))
                            store(sg, QTr[j, :, tp:tp + n], n)
                        for blk in (2, 3):
                            w = load_w(blk)
                            for c in range(4):
                                h = (blk - 2) * 4 + c
                                pb = arot.next()
                                mm_fm(w, c, hT, n, pb)
                                act_evac(AF.Silu, pb, n, qs.t[:, h, 0:n], qs)
                    fblks = (4, 5, 6, 7) if kind != 'oth' else (6, 7)
                    for blk in fblks:
                        w = load_w(blk)
                        r = (blk - 4) // 2
                        for c in range(4):
                            h = (blk % 2) * 4 + c
                            pb = arot.next()
                            mm_fm(w, c, hT, n, pb)
                            sgm = tmp.next()
                            act_evac(AF.Sigmoid, pb, n, sgm.t[:, 0:n], sgm, scale=-1.0)
                            kk_ = tmp.next()
                            S.op('dve', lambda e, sgm=sgm, kk_=kk_, r=r, h=h: e.tensor_scalar(
                                out=kk_.t[:, 0:n], in0=sgm.t[:, 0:n], scalar1=oml.t[:, r * 8 + h:r * 8 + h + 1],
                                scalar2=None, op0=ALU.mult), reads=(sgm, oml), writes=(kk_,))
                            lf = tmp.next()
                            S.op('act', lambda e, kk_=kk_, lf=lf: e.activation(
                                out=lf.t[:, 0:n], in_=kk_.t[:, 0:n], func=AF.Ln, scale=-1.0, bias=1.0),
                                 reads=(kk_,), writes=(lf,))
                            bb = tmp.next()
                            S.op('dve', lambda e, lf=lf, bb=bb: e.tensor_tensor_scan(
                                out=bb.t[:, 0:n], data0=mskF.t[:, 0:n], data1=lf.t[:, 0:n], initial=0.0,
                                op0=ALU.mult, op1=ALU.add), reads=(lf, mskF), writes=(bb,))
                            if r == 1:
                                d1 = tmp.next()
                                S.op('pool', lambda e, lf=lf, bb=bb, d1=d1: e.tensor_tensor(
                                    out=d1.t[:, 0:n], in0=lf.t[:, 0:n], in1=bb.t[:, 0:n], op=ALU.subtract),
                                     reads=(lf, bb), writes=(d1,))
                                b2 = tmp.next()
                                bb3 = bb.t[:, 0:n].rearrange("p (c t) -> p c t", t=64)
                                S.op('dve', lambda e, d1=d1, b2=b2, bb3=bb3: e.tensor_tensor(
                                    out=b2.t[:, 0:n].rearrange("p (c t) -> p c t", t=64),
                                    in0=d1.t[:, 0:n].rearrange("p (c t) -> p c t", t=64),
                                    in1=bb3[:, :, 63:64].to_broadcast([128, n // 64, 64]), op=ALU.add),
                                     reads=(d1, bb), writes=(b2,))
                                bb = b2
                            E = tmp.next()
                            S.op('act', lambda e, bb=bb, E=E: e.activation(out=E.t[:, 0:n], in_=bb.t[:, 0:n],
                                                                           func=AF.Exp), reads=(bb,), writes=(E,))
                            Ei = tmp.next()
                            S.op('act', lambda e, bb=bb, Ei=Ei: e.activation(out=Ei.t[:, 0:n], in_=bb.t[:, 0:n],
                                                                             func=AF.Exp, scale=-1.0), reads=(bb,),
                                 writes=(Ei,))
                            sg = stg.next()
                            S.op('dve', lambda e, kk_=kk_, Ei=Ei, sg=sg: e.tensor_tensor(
                                out=sg.t[:, 0:n], in0=kk_.t[:, 0:n], in1=Ei.t[:, 0:n], op=ALU.mult),
                                 reads=(kk_, Ei), writes=(sg,))
                            store(sg, KD[r, h, :, kr0:kr0 + n], n)
                            if own:
                                sg2 = stg.next()
                                S.op('dve', lambda e, E=E, sg2=sg2, h=h: e.scalar_tensor_tensor(
                                    out=sg2.t[:, 0:n], in0=qs.t[:, h, 0:n], scalar=float(128.0 ** -0.5),
                                    in1=E.t[:, 0:n], op0=ALU.mult, op1=ALU.mult), reads=(qs, E), writes=(sg2,))
                                store(sg2, QD[r, h, :, tp:tp + n], n)
                            E3 = E.t[:, 0:n].rearrange("p (c t) -> p c t", t=64)
                            ci0 = kr0 // 64
                            col = 63 if r == 0 else 0
                            S.op('pool', lambda e, E3=E3, r=r, h=h, ci0=ci0, col=col: e.tensor_copy(
                                out=dec.t[:, r, h, ci0:ci0 + n // 64], in_=E3[:, :, col]), reads=(E,), writes=(dec,))
                    mid()
                    for blk in (8, 9):
                        w = load_w(blk)
                        j = blk - 8
                        for s in range(nsub):
                            pb = arot.next()
                            for k in range(KC):
                                S.op('pe', lambda e, k=k, s=s, pb=pb, w=w: e.matmul(
                                    pb.t[:, :], lhsT=hT.t[:, k, s * 128:(s + 1) * 128], rhs=w.t[:, k, :],
                                    start=(k == 0), stop=(k == KC - 1)), reads=(w, hT), writes=(pb,))
                            sg = stg.next()
                            S.op('act', lambda e, pb=pb, sg=sg: e.activation(out=sg.t[:, :], in_=pb.t[:, :],
                                                                             func=AF.Copy), reads=(pb,), writes=(sg,))
                            S.dma('pool', HV[kr0 + s * 128:kr0 + (s + 1) * 128, j * 512:(j + 1) * 512], sg.t[:, :],
                                  anchor=sg, reads=(sg,))
                    if own:
                        for blk in (10, 11):
                            w = load_w(blk)
                            for c in range(4):
                                h = (blk - 10) * 4 + c
                                pb = arot.next()
                                mm_fm(w, c, hT, n, pb)
                                sg = stg.next()
                                act_evac(AF.Silu, pb, n, sg.t[:, 0:n], sg)
                                store(sg, GH[h, :, tp:tp + n], n)
                        for blk in range(12, 20):
                            w = load_w(blk)
                            for c in range(4):
                                gi = (blk - 12) * 4 + c
                                pb = arot.next()
                                mm_fm(w, c, hT, n, pb)
                                sg = stg.next()
                                act_evac(AF.Sigmoid, pb, n, sg.t[:, 0:n], sg)
                                store(sg, GG[gi, :, tp:tp + n], n)

                load_sub(0, 0)
                load_sub(0, 1)
                prep(0, hTs[0])
                for ti in range(len(tiles)):
                    def mid(ti=ti):
                        if ti + 1 < len(tiles):
                            prep(ti + 1, hTs[(ti + 1) % 2])
                    do_tile(ti, hTs[ti % 2], mid)
                S.emit()
        if 'B' in PHASES:
            with ExitStack() as st:
                kn = [sbt(st, "kn%d" % i, [128, NKEY], BF16) for i in range(2)]
                vh = [sbt(st, "vh%d" % i, [128, NKT, 128], BF16) for i in range(2)]
                qn = [sbt(st, "qn%d" % i, [128, LH], BF16) for i in range(2)]
                qr = [sbt(st, "qr%d" % i, [128, LH], BF16) for i in range(2)]
                kre = sbt(st, "kre", [128, NKEY], BF16)
                kro = sbt(st, "kro", [128, NKEY], BF16)
                pT = [sbt(st, "pT%d" % i, [128, T], BF16) for i in range(4)]
                osb = Rot([sbt(st, "osb%d" % i, [128, T], BF16) for i in range(2)])
                rinv = Rot([sbt(st, "rinv%d" % i, [128, T], F32) for i in range(2)])
                sbank = [PB[0], PB[1], PB[2]]
                obank = [PB[3], PB[4]]
                rbank = [PB[5], PB[6]]
                nqt = LH // T
                if debug and os.environ.get("MK_NT"):
                    nqt = int(os.environ["MK_NT"])
                heads = list(range(8))
                if debug and os.environ.get("MK_NH"):
                    heads = list(range(int(os.environ["MK_NH"])))
                S.drain('sp')
                S.op('pool', lambda e: e.memset(kre.t[64:128, :], 0.0), writes=(kre,))
                S.op('pool', lambda e: e.memset(kro.t[0:64, :], 0.0), writes=(kro,))
                S.dma('sp', kre.t[0:64, :], KTr[0:64, :], anchor=kre, reads=(), writes=(kre,))
                S.dma('sp', kro.t[64:128, :], KTr[64:128, :], anchor=kro, reads=(), writes=(kro,))
                steps = [(h, qt, kt) for h in heads for qt in range(nqt) for kt in range(NKT)]
                per_head = nqt * NKT
                sc_att = float(192.0 ** -0.5)

                def load_head(h):
                    S.dma('sp', kn[h % 2].t[:, :], KTn[h, :, :], anchor=kn[h % 2], writes=(kn[h % 2],))
                    S.dma('sp', vh[h % 2].t[:, :, :], VH[h, :, :, :], anchor=vh[h % 2], writes=(vh[h % 2],))
                    S.dma('sp', qn[h % 2].t[:, 0:nqt * T], QTn[h, :, 0:nqt * T], anchor=qn[h % 2],
                          writes=(qn[h % 2],))
                    if h % 2 == 0:
                        p = h // 2
                        S.dma('sp', qr[p % 2].t[:, 0:nqt * T], QTr[p, :, 0:nqt * T], anchor=qr[p % 2],
                              writes=(qr[p % 2],))

                def emit_qk(g):
                    h, qt, kt = steps[g]
                    sb = sbank[g % 3]
                    krx = kre if h % 2 == 0 else kro
                    qrx = qr[(h // 2) % 2]
                    knx = kn[h % 2]
                    qnx = qn[h % 2]
                    S.op('pe', lambda e: e.matmul(sb.t[:, :], lhsT=knx.t[:, kt * 128:(kt + 1) * 128],
                                                  rhs=qnx.t[:, qt * T:(qt + 1) * T], start=True, stop=False),
                         reads=(knx, qnx), writes=(sb,))
                    S.op('pe', lambda e: e.matmul(sb.t[:, :], lhsT=krx.t[:, kt * 128:(kt + 1) * 128],
                                                  rhs=qrx.t[:, qt * T:(qt + 1) * T], start=False, stop=True),
                         reads=(krx, qrx), writes=(sb,))
                    p = pT[g % 4]
                    S.op('act', lambda e: e.activation(out=p.t[:, :], in_=sb.t[:, :], func=AF.Exp, scale=sc_att),
                         reads=(sb,), writes=(p,))

                def emit_pv(g):
                    h, qt, kt = steps[g]
                    idx = (g // NKT) % 2
                    ob = obank[idx]
                    rb = rbank[idx]
                    p = pT[g % 4]
                    vx = vh[h % 2]
                    S.op('pe', lambda e: e.matmul(ob.t[:, :], lhsT=vx.t[:, kt, :], rhs=p.t[:, :], start=(kt == 0),
                                                  stop=(kt == NKT - 1)), reads=(vx, p), writes=(ob,))
                    S.op('pe', lambda e: e.matmul(rb.t[:, :], lhsT=ones_bf.t[:, :], rhs=p.t[:, :], start=(kt == 0),
                                                  stop=(kt == NKT - 1)), reads=(ones_bf, p), writes=(rb,))
                    if kt == NKT - 1:
                        ri = rinv.next()
                        os_ = osb.next()
                        S.op('dve', lambda e: e.reciprocal(out=ri.t[:, :], in_=rb.t[:, :]), reads=(rb,), writes=(ri,))
                        S.op('dve', lambda e: e.tensor_tensor(out=os_.t[:, :], in0=ob.t[:, :], in1=ri.t[:, :],
                                                              op=ALU.mult), reads=(ob, ri), writes=(os_,))
                        S.dma('pool', AT[h, :, qt * T:(qt + 1) * T], os_.t[:, :], anchor=os_, reads=(os_,))

                load_head(heads[0])
                G = len(steps)
                for g in range(G + 2):
                    if g >= 2 and (g - 2) % per_head == 0:
                        hn = (g - 2) // per_head + 1
                        if hn < len(heads):
                            load_head(heads[hn])
                    if g < G:
                        emit_qk(g)
                    if g >= 2:
                        emit_pv(g - 2)
                S.emit()
        if 'H' in PHASES:
            with ExitStack() as st:
                kdb = [sbt(st, "kdb%d" % i, [128, 8, T], BF16) for i in range(2)]
                qdb = [sbt(st, "qdb%d" % i, [128, 8, T], BF16) for i in range(2)]
                hvb = [sbt(st, "hvb%d" % i, [128, 4, 1024], BF16) for i in range(2)]
                o1b = [sbt(st, "o1b%d" % i, [128, 8, T], F32) for i in range(2)]
                ost = [sbt(st, "ost%d" % i, [128, 8, T], F32) for i in range(2)]
                hst = [sbt(st, "hst%d" % i, [128, 8, T], BF16) for i in range(2)]
                Tst = [sbt(st, "Tst%d" % h, [128, 128], F32) for h in range(8)]
                Sbf = [[sbt(st, "Sbf%d_%d" % (h, i), [128, 128], BF16) for i in range(2)] for h in range(8)]
                kdA = [sbt(st, "kdA%d" % i, [128, 128], BF16) for i in range(8)]
                kdB = [sbt(st, "kdB%d" % i, [128, 128], BF16) for i in range(8)]
                smk = [sbt(st, "smk%d" % i, [128, 128], BF16) for i in range(8)]
                ptrb = PB[7]
                ptrv = ptrb.t[:, :].bitcast(BF16)
                ptr_slot = [ptrb] * 8
                psS_slot = [PB[4]] * 4
                psUa_slot = [PB[5]] * 4
                psUb_slot = [PB[6]] * 4
                psO = [PB[0], PB[1], PB[2], PB[3]]
                psS = PB[4]
                psU = [PB[5], PB[6]]
                for i in range(8):
                    S.op('pool', lambda e, i=i: e.memset(kdA[i].t[:, :], 0.0), writes=(kdA[i],))
                    S.op('pool', lambda e, i=i: e.memset(kdB[i].t[:, :], 0.0), writes=(kdB[i],))
                masks = (mask1, mask2)
                n_own = 8
                n_oth = 8
                if debug and os.environ.get("MK_NT"):
                    n_own = int(os.environ["MK_NT"])
                    n_oth = 1
                for r in range(int(os.environ.get("MK_HR", "2")) if debug else 2):
                    S.drain('sp')
                    for h in range(8):
                        S.op('pool', lambda e, h=h: e.memset(Tst[h].t[:, :], 0.0), writes=(Tst[h],))
                        S.op('pool', lambda e, h=h: e.memset(Sbf[h][0].t[:, :], 0.0), writes=(Sbf[h][0],))
                    scur = [0] * 8
                    if r == 0:
                        htiles = [('ctx', 0, CTX, None)] + [('own', CTX + i * T, T, i * T) for i in range(n_own)]
                    else:
                        htiles = [('ctx', 0, CTX, None)]
                        htiles += [('oth', CTX + LH + i * T, T, None) for i in reversed(range(n_oth))]
                        htiles += [('own', CTX + i * T, T, i * T) for i in reversed(range(n_own))]
                    first = [True]

                    def hload(i):
                        kind, row0, n, tok0 = htiles[i]
                        bi = i % 2
                        S.dma('sp', kdb[bi].t[:, :, 0:n], KD[r, :, :, row0:row0 + n].rearrange("h p t -> p h t"),
                              anchor=kdb[bi], writes=(kdb[bi],))
                        S.dma('sp', hvb[bi].t[:, 0:n // 128, :],
                              HV[row0:row0 + n, :].rearrange("(s p) c -> p s c", p=128), anchor=hvb[bi],
                              writes=(hvb[bi],))
                        if kind == 'own':
                            S.dma('sp', qdb[bi].t[:, :, 0:n],
                                  QD[r, :, :, tok0:tok0 + n].rearrange("h p t -> p h t"), anchor=qdb[bi],
                                  writes=(qdb[bi],))
                            if r == 1:
                                S.dma('sp', o1b[bi].t[:, :, 0:n],
                                      O1[:, :, tok0:tok0 + n].rearrange("h p t -> p h t"), anchor=o1b[bi],
                                      writes=(o1b[bi],))

                    def hcompute(i):
                        kind, row0, n, tok0 = htiles[i]
                        bi = i % 2
                        own = kind == 'own'
                        nsub = n // 128
                        kd_, qd_, hv_, o1_ = kdb[bi], qdb[bi], hvb[bi], o1b[bi]
                        os_ = ost[i % 2]
                        hs_ = hst[i % 2]
                        subs = list(range(nsub)) if r == 0 else list(reversed(range(nsub)))
                        corder = (0, 1) if r == 0 else (1, 0)
                        def do_chunk(s_, hs, j, c):
                            ci = (row0 + s_ * 128 + c * 64) // 64
                            kdx = kdA if c == 0 else kdB
                            psUx = psU[j]
                            slots = psUa_slot if j == 0 else psUb_slot
                            csl = slice(s_ * 128 + c * 64, s_ * 128 + c * 64 + 64)
                            for h in hs:
                                q = h % 4
                                if own and HL >= 4:
                                    sb_ = Sbf[h][scur[h] % 2]
                                    S.op('pe', lambda e, h=h, q=q, sb_=sb_: e.matmul(
                                        psO[q].t[:, c * 64:c * 64 + 64], lhsT=sb_.t[:, :],
                                        rhs=qd_.t[:, h, csl], start=False, stop=(j == 1)),
                                         reads=(sb_, qd_), writes=(psO[q],))
                                S.op('pe', lambda e, h=h, q=q: e.matmul(
                                    psUx.t[:, q * 128:(q + 1) * 128], lhsT=kdx[h].t[:, :],
                                    rhs=hv_.t[:, s_, h * 128:(h + 1) * 128], start=True, stop=True),
                                     reads=(kdx[h], hv_), writes=(slots[q],))
                            if HL < 3:
                                return
                            for h in hs:
                                q = h % 4
                                dcur = dec.t[:, r, h, ci:ci + 1]
                                dp = dprev[h] if dprev[h] is not None else dcur
                                S.op('dve', lambda e, h=h, q=q, dp=dp: e.scalar_tensor_tensor(
                                    out=Tst[h].t[:, :], in0=Tst[h].t[:, :], scalar=dp,
                                    in1=psUx.t[:, q * 128:(q + 1) * 128], op0=ALU.mult, op1=ALU.add),
                                     reads=(Tst[h], slots[q], dec), writes=(Tst[h],))
                                scur[h] += 1
                                sb2 = Sbf[h][scur[h] % 2]
                                S.op('act', lambda e, h=h, sb2=sb2, dcur=dcur: e.activation(
                                    out=sb2.t[:, :], in_=Tst[h].t[:, :], func=AF.Copy, scale=dcur),
                                     reads=(Tst[h], dec), writes=(sb2,))
                                dprev[h] = dcur

                        HL = int(os.environ.get("MK_HLVL", "9"))

                        def do_group(s_, hs):
                            ssl = slice(s_ * 128, (s_ + 1) * 128)
                            if HL < 1:
                                return
                            for h in hs:
                                S.op('pe', lambda e, h=h: e.transpose(
                                    out=ptrv[:, h * 128:(h + 1) * 128], in_=kd_.t[:, h, ssl],
                                    identity=ident.t[:, :]), reads=(kd_, ident), writes=(ptr_slot[h],))
                            for h in hs:
                                S.op('act', lambda e, h=h: e.activation(
                                    out=kdA[h].t[0:64, :], in_=ptrv[0:64, h * 128:(h + 1) * 128], func=AF.Copy),
                                     reads=(ptr_slot[h],), writes=(kdA[h],))
                                S.op('act', lambda e, h=h: e.activation(
                                    out=kdB[h].t[64:128, :], in_=ptrv[64:128, h * 128:(h + 1) * 128],
                                    func=AF.Copy), reads=(ptr_slot[h],), writes=(kdB[h],))
                            if HL < 2:
                                return
                            if own and HL >= 4:
                                for h in hs:
                                    q = h % 4
                                    S.op('pe', lambda e, h=h, q=q: e.matmul(
                                        psS.t[:, q * 128:(q + 1) * 128], lhsT=kd_.t[:, h, ssl],
                                        rhs=qd_.t[:, h, ssl], start=True, stop=True), reads=(kd_, qd_),
                                         writes=(psS_slot[q],))
                                for h in hs:
                                    q = h % 4
                                    S.op('dve', lambda e, h=h, q=q: e.tensor_tensor(
                                        out=smk[h].t[:, :], in0=psS.t[:, q * 128:(q + 1) * 128],
                                        in1=masks[r].t[:, :], op=ALU.mult), reads=(psS_slot[q], masks[r]),
                                         writes=(smk[h],))
                                for h in hs:
                                    q = h % 4
                                    S.op('pe', lambda e, h=h, q=q: e.matmul(
                                        psO[q].t[:, 0:128], lhsT=hv_.t[:, s_, h * 128:(h + 1) * 128],
                                        rhs=smk[h].t[:, :], start=True, stop=False), reads=(hv_, smk[h]),
                                         writes=(psO[q],))
                            for j, c in enumerate(corder):
                                do_chunk(s_, hs, j, c)
                            if own and HL >= 4:
                                for h in hs:
                                    q = h % 4
                                    if r == 0:
                                        S.op('act', lambda e, h=h, q=q: e.activation(
                                            out=os_.t[:, h, ssl], in_=psO[q].t[:, 0:128], func=AF.Copy),
                                             reads=(psO[q],), writes=(os_,))
                                    else:
                                        S.op('dve', lambda e, h=h, q=q: e.tensor_tensor(
                                            out=hs_.t[:, h, ssl], in0=psO[q].t[:, 0:128], in1=o1_.t[:, h, ssl],
                                            op=ALU.add), reads=(psO[q], o1_), writes=(hs_,))

                        for s_ in subs:
                            for grp in range(2):
                                do_group(s_, list(range(grp * 4, grp * 4 + 4)))
                        if own and HL >= 4:
                            if r == 0:
                                S.dma('pool', O1[:, :, tok0:tok0 + n].rearrange("h p t -> p h t"), os_.t[:, :, 0:n],
                                      anchor=os_, reads=(os_,))
                            else:
                                S.dma('pool', HO[:, :, tok0:tok0 + n].rearrange("h p t -> p h t"), hs_.t[:, :, 0:n],
                                      anchor=hs_, reads=(hs_,))

                    dprev = [None] * 8

                    if debug and os.environ.get("MK_HT"):
                        htiles = htiles[:int(os.environ["MK_HT"])]
                    hload(0)
                    for i in range(len(htiles)):
                        if i + 1 < len(htiles):
                            hload(i + 1)
                        hcompute(i)
                    S.emit()
        stAH.close()
        if 'C' in PHASES:
            with ExitStack() as st:
                x4 = sbt(st, "x4", [128, 4, D], F32)
                raw = sbt(st, "raw", [128, 4, D], F32)
                act = sbt(st, "act", [128, 44, T], BF16)
                fT = sbt(st, "fT", [128, KC, T], BF16)
                xnc = [sbt(st, "xnc%d" % i, [128, D], BF16) for i in range(2)]
                wslc = Rot([sbt(st, "wslc%d" % i, [128, KC, 512], BF16) for i in range(2)])
                hob = Rot([sbt(st, "hob%d" % i, [128, T], BF16) for i in range(2)])
                ghb = Rot([sbt(st, "ghb%d" % i, [128, T], BF16) for i in range(2)])
                gab = Rot([sbt(st, "gab%d" % i, [128, T], BF16) for i in range(2)])
                gbb = Rot([sbt(st, "gbb%d" % i, [128, T], BF16) for i in range(2)])
                sqb = Rot([sbt(st, "sqb%d" % i, [128, T], BF16) for i in range(2)])
                Gb = sbt(st, "Gb", [128, D], F32)
                sa = sbt(st, "sa", [128, 4, T], BF16)
                tmpc = Rot([sbt(st, "tmpc%d" % i, [128, T], F32) for i in range(4)])
                ssq4 = sbt(st, "ssq4", [128, 16], F32)
                ssq1 = sbt(st, "ssq1", [128, 4], F32)
                rstdc = sbt(st, "rstdc", [128, 4], F32)
                crot = Rot([PB[0], PB[1], PB[2], PB[3]])
                c2rot = Rot([PB[4], PB[5]])
                ctr = Rot([PB[6], PB[7]])
                n_own = 8
                if debug and os.environ.get("MK_NT"):
                    n_own = int(os.environ["MK_NT"])
                S.drain('sp')

                def loadw(src, nk):
                    w = wslc.next()
                    S.dma('sp', w.t[:, 0:nk, :], src.rearrange("p (k c) -> p k c", c=512), anchor=w, writes=(w,))
                    return w

                def finalize(gidx, addx, tok0, store):
                    S.dma('sp', Gb.t[:, :], GAB[gidx, :, :], anchor=Gb, writes=(Gb,))
                    for s_ in range(4):
                        S.op('dve', lambda e, s_=s_: e.tensor_reduce(
                            out=ssq1.t[:, s_:s_ + 1], in_=ssq4.t[:, s_ * 4:(s_ + 1) * 4], axis=mybir.AxisListType.X,
                            op=ALU.add), reads=(ssq4,), writes=(ssq1,))
                        rsqrt(rstdc.t[:, s_:s_ + 1], ssq1.t[:, s_:s_ + 1], D * EPS, (ssq1,), rstdc)
                        S.op('dve', lambda e, s_=s_: e.scalar_tensor_tensor(
                            out=raw.t[:, s_, :], in0=raw.t[:, s_, :], scalar=rstdc.t[:, s_:s_ + 1], in1=Gb.t[:, :],
                            op0=ALU.mult, op1=ALU.mult), reads=(raw, rstdc, Gb), writes=(raw,))
                        if store:
                            S.op('dve', lambda e, s_=s_: e.tensor_tensor(out=raw.t[:, s_, :], in0=raw.t[:, s_, :],
                                                                         in1=x4.t[:, s_, :], op=ALU.add),
                                 reads=(raw, x4), writes=(raw,))
                            S.dma('pool', y[tok0 + s_ * 128:tok0 + (s_ + 1) * 128, :], raw.t[:, s_, :], anchor=raw,
                                  reads=(raw,))
                        else:
                            S.op('dve', lambda e, s_=s_: e.tensor_tensor(out=x4.t[:, s_, :], in0=x4.t[:, s_, :],
                                                                         in1=raw.t[:, s_, :], op=ALU.add),
                                 reads=(raw, x4), writes=(x4,))

                def evac_raw(pb, s_, blk):
                    S.op('act', lambda e: e.activation(out=raw.t[:, s_, blk * 512:(blk + 1) * 512], in_=pb.t[:, :],
                                                       func=AF.Copy), reads=(pb,), writes=(raw,))
                    jb = sqb.next()
                    S.op('act', lambda e: e.activation(out=jb.t[:, :], in_=pb.t[:, :], func=AF.Square,
                                                       accum_out=ssq4.t[:, s_ * 4 + blk:s_ * 4 + blk + 1]),
                         reads=(pb,), writes=(jb, ssq4))

                def ctile(i):
                    tok0 = i * T
                    S.dma('sp', act.t[:, 0:8, :], AT[:, :, tok0:tok0 + T].rearrange("h p t -> p h t"), anchor=act,
                          writes=(act,))
                    S.dma('sp', x4.t[:, :, :], xo[tok0:tok0 + T, :].rearrange("(s p) c -> p s c", p=128), anchor=x4,
                          writes=(x4,))

                    def hnorm(h):
                        ho = hob.next()
                        gh = ghb.next()
                        S.dma('sp', ho.t[:, :], HO[h, :, tok0:tok0 + T], anchor=ho, writes=(ho,))
                        S.dma('sp', gh.t[:, :], GH[h, :, tok0:tok0 + T], anchor=gh, writes=(gh,))
                        sq_ = sqb.next()
                        S.op('act', lambda e: e.activation(out=sq_.t[:, :], in_=ho.t[:, :], func=AF.Square),
                             reads=(ho,), writes=(sq_,))
                        pb = c2rot.next()
                        S.op('pe', lambda e: e.matmul(pb.t[:, :], lhsT=ones_bf.t[:, :], rhs=sq_.t[:, :], start=True,
                                                      stop=True), reads=(ones_bf, sq_), writes=(pb,))
                        rt = tmpc.next()
                        rsqrt(rt.t[:, :], pb.t[:, :], 128 * EPS, (pb,), rt)
                        t_ = tmpc.next()
                        S.op('dve', lambda e: e.tensor_tensor(out=t_.t[:, :], in0=ho.t[:, :], in1=rt.t[:, :],
                                                              op=ALU.mult), reads=(ho, rt), writes=(t_,))
                        S.op('dve', lambda e: e.scalar_tensor_tensor(
                            out=act.t[:, 8 + h, :], in0=t_.t[:, :], scalar=ongs.t[:, 0:1], in1=gh.t[:, :],
                            op0=ALU.mult, op1=ALU.mult), reads=(t_, ongs, gh), writes=(act,))

                    CL = int(os.environ.get("MK_CLVL", "9")) if debug else 9
                    for h in range(8):
                        hnorm(h)
                    if CL < 2:
                        return

                    def merge_chunk(wa, wh, cc, c):
                        pbA = crot.next()
                        pbH = crot.next()
                        for k in range(8):
                            S.op('pe', lambda e, k=k: e.matmul(pbA.t[:, :], lhsT=wa.t[:, k, cc * 128:(cc + 1) * 128],
                                                               rhs=act.t[:, k, :], start=(k == 0), stop=(k == 7)),
                                 reads=(wa, act), writes=(pbA,))
                        for k in range(8):
                            S.op('pe', lambda e, k=k: e.matmul(pbH.t[:, :], lhsT=wh.t[:, k, cc * 128:(cc + 1) * 128],
                                                               rhs=act.t[:, 8 + k, :], start=(k == 0), stop=(k == 7)),
                                 reads=(wh, act), writes=(pbH,))
                        ga = gab.next()
                        gb = gbb.next()
                        S.dma('sp', ga.t[:, :], GG[c, :, tok0:tok0 + T], anchor=ga, writes=(ga,))
                        S.dma('sp', gb.t[:, :], GG[16 + c, :, tok0:tok0 + T], anchor=gb, writes=(gb,))
                        t1 = tmpc.next()
                        t2 = tmpc.next()
                        S.op('dve', lambda e: e.tensor_tensor(out=t1.t[:, :], in0=pbA.t[:, :], in1=ga.t[:, :],
                                                              op=ALU.mult), reads=(pbA, ga), writes=(t1,))
                        S.op('dve', lambda e: e.tensor_tensor(out=t2.t[:, :], in0=pbH.t[:, :], in1=gb.t[:, :],
                                                              op=ALU.mult), reads=(pbH, gb), writes=(t2,))
                        S.op('pool', lambda e: e.tensor_tensor(out=fT.t[:, c, :], in0=t1.t[:, :], in1=t2.t[:, :],
                                                               op=ALU.add), reads=(t1, t2), writes=(fT,))

                    for blk in range(4):
                        wa = loadw(WBA[blk], 8)
                        wh = loadw(WBH[blk], 8)
                        for cc in range(4):
                            merge_chunk(wa, wh, cc, blk * 4 + cc)

                    if CL < 3:
                        return

                    def tm_block(w, blk, nk, src, kofs, banks, first, last):
                        for s_ in range(4):
                            pb = banks[s_]
                            for k in range(nk):
                                S.op('pe', lambda e, k=k, pb=pb, s_=s_: e.matmul(
                                    pb.t[:, :], lhsT=src.t[:, kofs + k, s_ * 128:(s_ + 1) * 128], rhs=w.t[:, k, :],
                                    start=(first and k == 0), stop=(last and k == nk - 1)), reads=(src, w),
                                     writes=(pb,))

                    for blk in range(4):
                        w = loadw(WO[blk], KC)
                        banks = [PB[(blk % 2) * 4 + q] for q in range(4)]
                        tm_block(w, blk, KC, fT, 0, banks, True, True)
                        for s_ in range(4):
                            evac_raw(banks[s_], s_, blk)
                    finalize(0, True, tok0, False)
                    if CL < 4:
                        return

                    def prep_sub(s_):
                        xn = xnc[s_ % 2]
                        S.op('act', lambda e: e.activation(out=xn.t[:, :], in_=x4.t[:, s_, :], func=AF.Square,
                                                           accum_out=ssq1.t[:, s_:s_ + 1]), reads=(x4,),
                             writes=(xn, ssq1))
                        rsqrt(rstdc.t[:, s_:s_ + 1], ssq1.t[:, s_:s_ + 1], D * EPS, (ssq1,), rstdc)
                        S.op('dve', lambda e: e.tensor_scalar(out=xn.t[:, :], in0=x4.t[:, s_, :],
                                                              scalar1=rstdc.t[:, s_:s_ + 1], scalar2=None,
                                                              op0=ALU.mult), reads=(x4, rstdc, xn), writes=(xn,))
                        PSL = int(os.environ.get("MK_PS", "9")) if debug else 9
                        if PSL < 1:
                            return
                        for g in range(2):
                            pb = ctr.next()
                            pbv = pb.t[:, :].bitcast(BF16)
                            for kk in range(8):
                                k = g * 8 + kk
                                S.op('pe', lambda e, kk=kk, k=k, pbv=pbv: e.transpose(
                                    out=pbv[:, kk * 128:(kk + 1) * 128], in_=xn.t[:, k * 128:(k + 1) * 128],
                                    identity=ident.t[:, :]), reads=(xn, ident), writes=(pb,))
                            if PSL < 2:
                                continue
                            for kk in range(8):
                                k = g * 8 + kk
                                S.op('act', lambda e, kk=kk, k=k, pbv=pbv: e.activation(
                                    out=fT.t[:, k, s_ * 128:(s_ + 1) * 128], in_=pbv[:, kk * 128:(kk + 1) * 128],
                                    func=AF.Identity, scale=gsF.t[:, k:k + 1], bias=shF.t[:, k:k + 1]),
                                     reads=(pb, gsF, shF), writes=(fT,))

                    for s_ in range(4):
                        prep_sub(s_)
                    if CL < 5:
                        return

                    def ffn_chunk(w, cc, c, is_a):
                        pb = crot.next()
                        for k in range(KC):
                            S.op('pe', lambda e, k=k: e.matmul(pb.t[:, :], lhsT=w.t[:, k, cc * 128:(cc + 1) * 128],
                                                               rhs=fT.t[:, k, :], start=(k == 0), stop=(k == KC - 1)),
                                 reads=(w, fT), writes=(pb,))
                        if is_a:
                            S.op('act', lambda e: e.activation(out=sa.t[:, cc, :], in_=pb.t[:, :], func=AF.Silu),
                                 reads=(pb,), writes=(sa,))
                        else:
                            S.op('dve', lambda e: e.tensor_tensor(out=act.t[:, c, :], in0=pb.t[:, :],
                                                                  in1=sa.t[:, cc, :], op=ALU.mult), reads=(pb, sa),
                                 writes=(act,))

                    for blk in range(11):
                        w = loadw(WF1[blk], KC)
                        for cc in range(4):
                            ffn_chunk(w, cc, blk * 4 + cc, True)
                        w = loadw(WF1[11 + blk], KC)
                        for cc in range(4):
                            ffn_chunk(w, cc, blk * 4 + cc, False)

                    if CL < 6:
                        return
                    for blk in range(4):
                        banks = [PB[(blk % 2) * 4 + q] for q in range(4)]
                        for kp in range(4):
                            w = loadw(WF2[blk * 4 + kp], 11)
                            tm_block(w, blk, 11, act, kp * 11, banks, kp == 0, kp == 3)
                        for s_ in range(4):
                            evac_raw(banks[s_], s_, blk)
                    finalize(1, True, tok0, True)

                for i in range(n_own):
                    ctile(i)
                S.drain('sp')
                S.emit()
        return nc, S, locals()


def _rope_tables():
    pairs = 16
    pos = np.arange(L)
    row = (pos // 64).astype(np.float32)
    col = (pos % 64).astype(np.float32)
    inv = (np.float32(10000.0) ** (-np.arange(pairs, dtype=np.float32) / np.float32(pairs))).astype(np.float32)
    ang = np.concatenate([row[:, None] * inv, col[:, None] * inv], axis=-1).astype(np.float32)
    cos = np.cos(ang).astype(np.float32)
    sin = np.sin(ang).astype(np.float32)
    return np.ascontiguousarray(np.tile(cos.T, (4, 1))), np.ascontiguousarray(np.tile(sin.T, (4, 1)))


def core_inputs(inp, core, shared):
    b = core // 2
    rev = core % 2
    f = lambda a: np.ascontiguousarray(a, dtype=np.float32)
    x = inp['x'][b]
    ctx = inp['ctx'][b]
    if rev:
        x = x[::-1]
        ctx = ctx[::-1]
    key = ('w', rev)
    if key not in shared:
        w_in = inp['w_in'][0]
        lb = inp['hgrn_lb']
        cos2, sin2 = shared['rope']
        if rev:
            w_in = np.concatenate([w_in[:, :1856], w_in[:, 2880:3904], w_in[:, 1856:2880], w_in[:, 3904:]], axis=1)
            lb = lb[:, ::-1]
            cos2 = cos2[:, ::-1]
            sin2 = sin2[:, ::-1]
        shared[key] = dict(w_in=f(w_in), lbT=f(lb.transpose(3, 0, 1, 2).reshape(128, 2, 16)), cos2=f(cos2),
                           sin2=f(sin2))
    if 'common' not in shared:
        ng = inp['norm_g'][0]
        shared['common'] = dict(
            w_mod=f(inp['w_mod'][0]), b_mod=f(inp['b_mod'][0].reshape(1, -1)),
            ng_fm=f(ng.reshape(4, KC, 128).transpose(2, 0, 1)), ng_row=f(ng.reshape(1, -1)),
            qng=f(inp['mla_q_norm'][0].reshape(4, 128).T), kvng=f(inp['mla_kv_norm'][0].reshape(2, 128).T),
            w_uq=f(inp['w_uq'][0]), w_ukv=f(inp['w_ukv'][0]), ong=f(inp['hgrn_o_norm'][0].reshape(128, 1)),
            w_br_mla=f(inp['w_br_mla'][0]), w_br_hgrn=f(inp['w_br_hgrn'][0]), w_out=f(inp['w_out'][0]),
            w_ffn_in=f(inp['w_ffn_in'][0]), w_ffn_out=f(inp['w_ffn_out'][0]))
    m = dict(shared['common'])
    m.update(shared[key])
    cc = np.stack([inp['c'][b], inp['c_ctx']], axis=-1)
    m['cc'] = f(cc.reshape(KC, 128, 2).transpose(1, 0, 2))
    m['xo'] = f(x[:LH])
    m['xt'] = f(x[LH:])
    m['cx'] = f(ctx)
    return m


_NC_CACHE = {}


def kernel(**inputs):
    inp = {k: np.asarray(v) for k, v in inputs.items()}
    if 'nc' not in _NC_CACHE:
        _NC_CACHE['nc'] = build_nc(False)[0]
    nc = _NC_CACHE['nc']
    shared = {'rope': _rope_tables()}
    in_maps = [core_inputs(inp, c, shared) for c in range(8)]
    res = run_bass_kernel_spmd(nc, in_maps, core_ids=list(range(8)))
    out = np.empty((4, L, D), np.float32)
    for c in range(8):
        yv = np.asarray(res.results[c]["y"], dtype=np.float32)
        b = c // 2
        if c % 2 == 0:
            out[b, :LH] = yv
        else:
            out[b, LH:] = yv[::-1]
    return out
```

```python
import os
import numpy as np
from contextlib import ExitStack
import concourse.bass as bass
import concourse.mybir as mybir
from concourse.bass_utils import run_bass_kernel_spmd

F32 = mybir.dt.float32
BF16 = mybir.dt.bfloat16
AF = mybir.ActivationFunctionType
ALU = mybir.AluOpType

D = 2048
KC = 16
T = 512
L = 8192
LH = 4096
CTX = 256
NKEY = L + CTX
NKT = NKEY // 128
DFF = 5632
EPS = 1e-6
NCH = NKEY // 64
PHASES = os.environ.get("MK_PHASES", "WABHC")


class Buf:
    __slots__ = ('name', 'w', 'r', 'dsem', 'dcnt', 'psum')

    def __init__(self, name):
        self.name = name
        self.w = None
        self.r = {}
        self.dsem = None
        self.dcnt = 0
        self.psum = False


class TB:
    def __init__(self, t, name):
        self.t = t
        self.b = Buf(name)


def _b(x):
    return x.b if isinstance(x, TB) else x


class Sched:
    def __init__(self, nc, stack):
        self.nc = nc
        self.stack = stack
        self.engs = ['pe', 'act', 'dve', 'pool', 'sp']
        self.sem = {e: stack.enter_context(nc.semaphore('s_' + e)) for e in self.engs}
        self.cnt = {e: 0 for e in self.engs}
        self.prog = {e: [] for e in self.engs}
        self.waited = {}
        self.anchors = []
        self.nsem = 0

    def _deps(self, eng, reads, writes, skip=None):
        deps = []
        for b in reads:
            b = _b(b)
            if b.w is not None:
                deps.append(b.w)
            if b.psum:
                for t in b.r.values():
                    if t[2] != eng:
                        deps.append(t)
        for b in writes:
            b = _b(b)
            if b.w is not None and b.w[2] != eng:
                deps.append(b.w)
            for t in b.r.values():
                if t[2] != eng:
                    deps.append(t)
        for (s, v, src) in deps:
            if src == eng and eng == 'pe':
                continue
            if skip is not None and s is skip:
                continue
            key = (eng, id(s))
            if self.waited.get(key, 0) >= v:
                continue
            self.waited[key] = v
            self.prog[eng].append(lambda e, s=s, v=v: e.wait_ge(s, v))

    def _mark(self, tok, reads, writes):
        for b in reads:
            _b(b).r[id(tok[0])] = tok
        for b in writes:
            b = _b(b)
            b.w = tok
            b.r = {}

    def op(self, eng, fn, reads=(), writes=()):
        self._deps(eng, reads, writes)
        if self.cnt[eng] >= 30000:
            self.nsem += 1
            self.sem[eng] = self.stack.enter_context(self.nc.semaphore('s_%s_%d' % (eng, self.nsem)))
            self.cnt[eng] = 0
        self.cnt[eng] += 1
        s = self.sem[eng]
        tok = (s, self.cnt[eng], eng)
        self.prog[eng].append(lambda e, fn=fn, s=s: fn(e).then_inc(s, 1))
        self._mark(tok, reads, writes)

    def dma(self, eng, out, in_, anchor, reads=(), writes=()):
        a = _b(anchor)
        if a.dsem is None:
            a.dsem = self.stack.enter_context(self.nc.semaphore('d_' + a.name))
            self.anchors.append(a)
        self._deps(eng, reads, writes, skip=a.dsem)
        a.dcnt += 16
        tok = (a.dsem, a.dcnt, 'dma')
        self.prog[eng].append(lambda e, s=a.dsem, o=out, i=in_: e.dma_start(out=o, in_=i).then_inc(s, 16))
        self._mark(tok, reads, writes)

    def drain(self, eng):
        for a in self.anchors:
            if a.dcnt > 0:
                key = (eng, id(a.dsem))
                if self.waited.get(key, 0) >= a.dcnt:
                    continue
                self.waited[key] = a.dcnt
                self.prog[eng].append(lambda e, s=a.dsem, v=a.dcnt: e.wait_ge(s, v))

    def emit(self):
        nc = self.nc
        prog = self.prog
        with nc.Block() as block:
            @block.tensor
            def _(e):
                for t in prog['pe']:
                    t(e)

            @block.scalar
            def _(e):
                for t in prog['act']:
                    t(e)

            @block.vector
            def _(e):
                for t in prog['dve']:
                    t(e)

            @block.gpsimd
            def _(e):
                for t in prog['pool']:
                    t(e)

            @block.sync
            def _(e):
                for t in prog['sp']:
                    t(e)
        self.prog = {e: [] for e in self.engs}


class _View:
    def __init__(self, t, idx):
        self.tt = t
        self.idx = idx

    def __getitem__(self, key):
        return self.tt[key[0], self.idx, key[1]]


class SV(TB):
    def __init__(self, tile, idx):
        self.b = tile.b
        self.t = _View(tile.t, idx)


class Grouper:
    def __init__(self, S, rot):
        self.S = S
        self.rot = rot
        self.cur = None
        self.cnt = 0

    def slot(self):
        if self.cur is None:
            self.cur = self.rot.next()
            self.cnt = 0
        sv = SV(self.cur, self.cnt)
        self.cnt += 1
        return sv

    def flush(self, dst, ncol, src=None):
        tile = self.cur
        if src is None:
            src = tile.t[:, 0:self.cnt, 0:ncol] if self.cnt > 1 else tile.t[:, 0, 0:ncol]
        self.S.dma('pool', dst, src, anchor=tile, reads=(tile,))
        self.cur = None


class Rot:
    def __init__(self, items):
        self.items = items
        self.i = 0

    def next(self):
        x = self.items[self.i % len(self.items)]
        self.i += 1
        return x


def build_nc(debug=False):
    nc = bass.Bass("TRN2", target_bir_lowering=False)

    def din(name, shape, dt=F32):
        return nc.dram_tensor(name, shape, dt, kind="ExternalInput").ap()

    dbg_out = set(os.environ.get("MK_DBG", "").split(",")) if debug else set()

    def dscr(name, shape, dt=BF16):
        ext = debug and (name in dbg_out or "ALL" in dbg_out)
        return nc.dram_tensor(name, shape, dt, kind=("ExternalOutput" if ext else "Internal")).ap()

    xo = din("xo", [LH, D])
    xt = din("xt", [LH, D])
    cx = din("cx", [CTX, D])
    cc = din("cc", [128, KC, 2])
    w_mod = din("w_mod", [D, 6 * D])
    b_mod = din("b_mod", [1, 6 * D])
    ng_fm = din("ng_fm", [128, 4, KC])
    ng_row = din("ng_row", [1, 4 * D])
    w_in = din("w_in", [D, 10048])
    qng = din("qng", [128, 4])
    kvng = din("kvng", [128, 2])
    w_uq = din("w_uq", [512, 1536])
    w_ukv = din("w_ukv", [256, 2048])
    lbT = din("lbT", [128, 2, 16])
    ong = din("ong", [128, 1])
    w_br_mla = din("w_br_mla", [1024, D])
    w_br_hgrn = din("w_br_hgrn", [1024, D])
    w_out = din("w_out", [D, D])
    w_ffn_in = din("w_ffn_in", [D, 2 * DFF])
    w_ffn_out = din("w_ffn_out", [DFF, D])
    cos2 = din("cos2", [128, L])
    sin2 = din("sin2", [128, L])
    y = nc.dram_tensor("y", [LH, D], F32, kind="ExternalOutput").ap()

    W1 = dscr("W1", [20, 128, KC * 512])
    WBA = dscr("WBA", [4, 128, 8 * 512])
    WBH = dscr("WBH", [4, 128, 8 * 512])
    WO = dscr("WO", [4, 128, KC * 512])
    WF1 = dscr("WF1", [22, 128, KC * 512])
    WF2 = dscr("WF2", [16, 128, 11 * 512])
    KTn = dscr("KTn", [8, 128, NKEY])
    KTr = dscr("KTr", [128, NKEY])
    VH = dscr("VH", [8, 128, NKT, 128])
    QTn = dscr("QTn", [8, 128, LH])
    QTr = dscr("QTr", [4, 128, LH])
    KD = dscr("KD", [2, 8, 128, NKEY])
    QD = dscr("QD", [2, 8, 128, LH])
    HV = dscr("HV", [NKEY, 1024])
    GH = dscr("GH", [8, 128, LH])
    GG = dscr("GG", [32, 128, LH])
    AT = dscr("AT", [8, 128, LH])
    O1 = dscr("O1", [8, 128, LH], F32)
    HO = dscr("HO", [8, 128, LH])
    GAB = dscr("GAB", [2, 128, D], F32)

    with ExitStack() as top:
        S = Sched(nc, top)

        def sbt(st, name, shape, dt):
            return TB(st.enter_context(nc.sbuf_tensor(name, shape, dt)), name)

        def pst(st, name, shape, dt):
            tb = TB(st.enter_context(nc.psum_tensor(name, shape, dt)), name)
            tb.b.psum = True
            return tb

        def rsqrt(out_ap, in_ap, const, reads, outb):
            S.op('act', lambda e: e.activation(out=out_ap, in_=in_ap, func=AF.Sqrt, bias=float(const), scale=1.0),
                 reads=reads, writes=(outb,))
            S.op('dve', lambda e: e.reciprocal(out=out_ap, in_=out_ap), reads=(outb,), writes=(outb,))

        ident = sbt(top, "ident", [128, 128], BF16)
        ones_bf = sbt(top, "ones_bf", [128, 128], BF16)
        ones_f = sbt(top, "ones_f", [1, 128], F32)
        mskF = sbt(top, "mskF", [128, 512], F32)
        mask1 = sbt(top, "mask1", [128, 128], F32)
        mask2 = sbt(top, "mask2", [128, 128], F32)
        modT = sbt(top, "modT", [128, 256], F32)
        gsA = sbt(top, "gsA", [128, KC], F32)
        shA = sbt(top, "shA", [128, KC], F32)
        gsC = sbt(top, "gsC", [128, KC], F32)
        shC = sbt(top, "shC", [128, KC], F32)
        gsF = sbt(top, "gsF", [128, KC], F32)
        shF = sbt(top, "shF", [128, KC], F32)
        ngf = sbt(top, "ngf", [128, 4, KC], F32)
        oml = sbt(top, "oml", [128, 16], F32)
        lbt = sbt(top, "lbt", [128, 2, 16], F32)
        ongs = sbt(top, "ongs", [128, 1], F32)
        stAH = ExitStack()
        dec = sbt(stAH, "dec", [128, 2, 8, NCH], F32)
        PB = [pst(top, "pb%d" % i, [128, 512], F32) for i in range(8)]

        S.op('pool', lambda e: e.memset(ident.t[:, :], 1.0), writes=(ident,))
        S.op('pool', lambda e: e.affine_select(out=ident.t[:, :], in_=ident.t[:, :], pattern=[[-1, 128]],
                                               compare_op=ALU.is_equal, fill=0.0, base=0, channel_multiplier=1),
             reads=(ident,), writes=(ident,))
        S.op('pool', lambda e: e.memset(ones_bf.t[:, :], 1.0), writes=(ones_bf,))
        S.op('pool', lambda e: e.memset(ones_f.t[:, :], 1.0), writes=(ones_f,))
        S.op('pool', lambda e: e.memset(mskF.t[:, :], 1.0), writes=(mskF,))
        mv = mskF.t[:, :].rearrange("p (c t) -> p c t", t=64)
        S.op('pool', lambda e: e.memset(mv[:, :, 0:1], 0.0), reads=(mskF,), writes=(mskF,))
        for (mk, sign) in ((mask1, 1), (mask2, -1)):
            S.op('pool', lambda e, mk=mk: e.memset(mk.t[:, :], 1.0), writes=(mk,))
            S.op('pool', lambda e, mk=mk, sign=sign: e.affine_select(
                out=mk.t[:, :], in_=mk.t[:, :], pattern=[[sign, 128]], compare_op=ALU.is_ge, fill=0.0,
                base=0, channel_multiplier=-sign), reads=(mk,), writes=(mk,))
            S.op('pool', lambda e, mk=mk: e.memset(mk.t[0:64, 64:128], 0.0), reads=(mk,), writes=(mk,))
            S.op('pool', lambda e, mk=mk: e.memset(mk.t[64:128, 0:64], 0.0), reads=(mk,), writes=(mk,))

        if 'W' in PHASES:
            with ExitStack() as st:
                stf = [sbt(st, "stf%d" % i, [128, KC, 512], F32) for i in range(2)]
                stb = [sbt(st, "stb%d" % i, [128, KC, 512], BF16) for i in range(2)]
                rowp = sbt(st, "rowp", [1, 512], F32)
                rowg = sbt(st, "rowg", [1, 512], F32)
                bmp = [sbt(st, "bmp%d" % i, [1, 512], F32) for i in range(2)]
                gpc = [sbt(st, "gpc%d" % i, [1, 512], F32) for i in range(2)]
                cct = sbt(st, "cct", [128, KC, 2], F32)
                scT = sbt(st, "scT", [128, KC, 2], BF16)
                one2 = sbt(st, "one2", [1, 2], F32)
                tmpm = sbt(st, "tmpm", [128, KC], F32)
                gtm = [sbt(st, "gtm%d" % i, [128, 512], F32) for i in range(2)]
                uidx = [0]
                cast_engs = ['act', 'dve']

                def cast(eng, o, i, rd, wr, scale=None):
                    if eng == 'act':
                        if scale is None:
                            S.op('act', lambda e: e.activation(out=o, in_=i, func=AF.Copy), reads=rd, writes=wr)
                        else:
                            S.op('act', lambda e: e.activation(out=o, in_=i, func=AF.Copy, scale=scale), reads=rd,
                                 writes=wr)
                    else:
                        if scale is None:
                            S.op(eng, lambda e: e.tensor_copy(out=o, in_=i), reads=rd, writes=wr)
                        else:
                            S.op(eng, lambda e: e.tensor_scalar(out=o, in0=i, scalar1=scale, scalar2=None,
                                                                 op0=ALU.mult), reads=rd, writes=wr)

                def precast(src, r0, nk, c0, ncol, dst):
                    u = uidx[0]
                    uidx[0] += 1
                    f = stf[u % 2]
                    b = stb[u % 2]
                    srcv = src[r0:r0 + nk * 128, c0:c0 + ncol].rearrange("(k p) c -> p k c", p=128)
                    S.dma('sp', f.t[:, 0:nk, 0:ncol], srcv, anchor=f, writes=(f,))
                    return u, f, b

                def simple_unit(src, r0, nk, c0, dst):
                    u, f, b = precast(src, r0, nk, c0, 512, dst)
                    h = nk // 2
                    cast('act', b.t[:, 0:h, :], f.t[:, 0:h, :], (f,), (b,))
                    cast('dve', b.t[:, h:nk, :], f.t[:, h:nk, :], (f,), (b,))
                    S.dma('pool', dst.rearrange("p (k c) -> p k c", c=512), b.t[:, 0:nk, :], anchor=b, reads=(b,))

                S.dma('sp', cct.t[:, :, :], cc[:, :, :], anchor=cct, writes=(cct,))
                S.op('act', lambda e: e.activation(out=scT.t[:, :, :], in_=cct.t[:, :, :], func=AF.Silu),
                     reads=(cct,), writes=(scT,))
                S.op('pool', lambda e: e.memset(one2.t[:, :], 1.0), writes=(one2,))
                S.dma('sp', ngf.t[:, :, :], ng_fm[:, :, :], anchor=ngf, writes=(ngf,))
                S.dma('sp', lbt.t[:, :, :], lbT[:, :, :], anchor=lbt, writes=(lbt,))
                S.dma('sp', ongs.t[:, :], ong[:, :], anchor=ongs, writes=(ongs,))
                modps = PB[7]
                mrot = Rot([PB[0], PB[1], PB[2], PB[3]])
                brot = Rot([PB[4], PB[5]])
                for blk in range(24):
                    which = blk // 4
                    u, f, b = precast(w_mod, 0, KC, blk * 512, 512, None)
                    cast('act', b.t[:, 0:8, :], f.t[:, 0:8, :], (f,), (b,))
                    cast('dve', b.t[:, 8:16, :], f.t[:, 8:16, :], (f,), (b,))
                    bp = bmp[blk % 2]
                    S.dma('sp', bp.t[:, :], b_mod[0:1, blk * 512:(blk + 1) * 512], anchor=bp, writes=(bp,))
                    for m in range(2):
                        if m == 1 and which >= 2:
                            continue
                        pb = mrot.next()
                        for k in range(KC):
                            S.op('pe', lambda e, pb=pb, k=k, m=m, b=b: e.matmul(
                                pb.t[0:1, :], lhsT=scT.t[:, k, m:m + 1], rhs=b.t[:, k, :], start=(k == 0),
                                stop=(k == KC - 1)), reads=(scT, b), writes=(pb,))
                        S.op('dve', lambda e, pb=pb, bp=bp: e.tensor_tensor(out=rowp.t[:, :], in0=pb.t[0:1, :],
                                                                            in1=bp.t[:, :], op=ALU.add),
                             reads=(pb, bp), writes=(rowp,))
                        if which in (0, 1, 3, 4):
                            for j in range(4):
                                k = (blk % 4) * 4 + j
                                col = ((m * 6 + which) * KC + k)
                                S.op('pe', lambda e, j=j, col=col: e.matmul(
                                    modps.t[:, 2 * col:2 * col + 2], lhsT=rowp.t[0:1, j * 128:(j + 1) * 128],
                                    rhs=one2.t[0:1, 0:2], start=True, stop=True), reads=(rowp, one2),
                                     writes=(modps,))
                        elif m == 0:
                            gi = 1 if which == 2 else 3
                            gp = gpc[blk % 2]
                            cb = (blk % 4) * 512
                            S.dma('sp', gp.t[:, :], ng_row[0:1, gi * D + cb: gi * D + cb + 512], anchor=gp,
                                  writes=(gp,))
                            S.op('dve', lambda e, gp=gp: e.scalar_tensor_tensor(
                                out=rowg.t[:, :], in0=rowp.t[:, :], scalar=float(np.sqrt(D)), in1=gp.t[:, :],
                                op0=ALU.mult, op1=ALU.mult), reads=(rowp, gp), writes=(rowg,))
                            pbb = brot.next()
                            S.op('pe', lambda e, pbb=pbb: e.matmul(pbb.t[:, :], lhsT=ones_f.t[0:1, :],
                                                                   rhs=rowg.t[0:1, :], start=True, stop=True),
                                 reads=(ones_f, rowg), writes=(pbb,))
                            gt_ = gtm[blk % 2]
                            S.op('act', lambda e, pbb=pbb, gt_=gt_: e.activation(
                                out=gt_.t[:, :], in_=pbb.t[:, :], func=AF.Copy), reads=(pbb,), writes=(gt_,))
                            S.dma('pool', GAB[0 if which == 2 else 1, :, cb:cb + 512], gt_.t[:, :], anchor=gt_,
                                  reads=(gt_,))
                S.op('dve', lambda e: e.tensor_copy(out=modT.t[:, 0:192], in_=modps.t[:, 0:384:2]), reads=(modps,),
                     writes=(modT,))

                def mcol(m, which):
                    c0 = (m * 6 + which) * KC
                    return modT.t[:, c0:c0 + KC]

                sq = float(np.sqrt(D))
                for (gs_, sh_, m, wsc, wsh, gi) in ((gsA, shA, 0, 1, 0, 0), (gsC, shC, 1, 1, 0, 0),
                                                    (gsF, shF, 0, 4, 3, 2)):
                    S.op('dve', lambda e, m=m, wsc=wsc: e.tensor_scalar(
                        out=tmpm.t[:, :], in0=mcol(m, wsc), scalar1=1.0, scalar2=sq, op0=ALU.add, op1=ALU.mult),
                         reads=(modT,), writes=(tmpm,))
                    S.op('dve', lambda e, gs_=gs_, gi=gi: e.tensor_tensor(out=gs_.t[:, :], in0=tmpm.t[:, :],
                                                                          in1=ngf.t[:, gi, :], op=ALU.mult),
                         reads=(tmpm, ngf), writes=(gs_,))
                    S.op('dve', lambda e, sh_=sh_, m=m, wsh=wsh: e.tensor_copy(out=sh_.t[:, :], in_=mcol(m, wsh)),
                         reads=(modT,), writes=(sh_,))
                S.op('dve', lambda e: e.tensor_tensor(out=oml.t[:, :], in0=lbt.t[:, 0, :], in1=lbt.t[:, 1, :],
                                                      op=ALU.subtract), reads=(lbt,), writes=(oml,))
                S.op('act', lambda e: e.activation(out=oml.t[:, :], in_=oml.t[:, :], func=AF.Sigmoid, scale=-1.0),
                     reads=(oml,), writes=(oml,))
                S.op('dve', lambda e: e.tensor_scalar(out=ongs.t[:, :], in0=ongs.t[:, :],
                                                      scalar1=float(np.sqrt(128.0)), scalar2=None, op0=ALU.mult),
                     reads=(ongs,), writes=(ongs,))

                simple_unit(w_in, 0, KC, 0, W1[0])
                u, f, b = precast(w_in, 0, KC, 512, 320, None)
                cast('act', b.t[:, :, 0:256], f.t[:, :, 0:256], (f,), (b,))
                cast('dve', b.t[:, :, 256:320], f.t[:, :, 256:320], (f,), (b,))
                cast('dve', b.t[:, :, 320:384], f.t[:, :, 256:320], (f,), (b,))
                for o in (384, 448):
                    cast('dve', b.t[:, :, o:o + 32], f.t[:, :, 288:320], (f,), (b,), scale=-1.0)
                    cast('dve', b.t[:, :, o + 32:o + 64], f.t[:, :, 256:288], (f,), (b,))
                S.dma('pool', W1[1].rearrange("p (k c) -> p k c", c=512), b.t[:, :, :], anchor=b, reads=(b,))
                for blk in range(2, 20):
                    simple_unit(w_in, 0, KC, 832 + (blk - 2) * 512, W1[blk])
                for blk in range(4):
                    simple_unit(w_br_mla, 0, 8, blk * 512, WBA[blk])
                    simple_unit(w_br_hgrn, 0, 8, blk * 512, WBH[blk])
                    simple_unit(w_out, 0, KC, blk * 512, WO[blk])
                for blk in range(22):
                    simple_unit(w_ffn_in, 0, KC, blk * 512, WF1[blk])
                for blk in range(4):
                    for kp in range(4):
                        simple_unit(w_ffn_out, kp * 11 * 128, 11, blk * 512, WF2[blk * 4 + kp])
                S.emit()

        if 'A' in PHASES:
            with ExitStack() as st:
                wq = sbt(st, "wq", [128, 4, 2048], BF16)
                wkv = sbt(st, "wkv", [128, 2, 8, 256], BF16)
                with ExitStack() as st2:
                    wqf = sbt(st2, "wqf", [128, 4, 8, 192], F32)
                    wkvf = sbt(st2, "wkvf", [128, 2, 2048], F32)
                    qg = sbt(st2, "qg", [128, 4], F32)
                    kvg = sbt(st2, "kvg", [128, 2], F32)
                    S.dma('sp', wqf.t[:, :, :, :], w_uq.rearrange("(k p) (h c) -> p k h c", p=128, c=192),
                          anchor=wqf, writes=(wqf,))
                    S.dma('sp', wkvf.t[:, :, :], w_ukv.rearrange("(k p) c -> p k c", p=128), anchor=wkvf,
                          writes=(wkvf,))
                    S.dma('sp', qg.t[:, :], qng[:, :], anchor=qg, writes=(qg,))
                    S.dma('sp', kvg.t[:, :], kvng[:, :], anchor=kvg, writes=(kvg,))
                    S.op('dve', lambda e: e.tensor_scalar(out=qg.t[:, :], in0=qg.t[:, :],
                                                          scalar1=float(np.sqrt(512.0)), scalar2=None,
                                                          op0=ALU.mult), reads=(qg,), writes=(qg,))
                    S.op('dve', lambda e: e.tensor_scalar(out=kvg.t[:, :], in0=kvg.t[:, :], scalar1=16.0,
                                                          scalar2=None, op0=ALU.mult), reads=(kvg,), writes=(kvg,))
                    wq4 = wq.t[:, :, :].rearrange("p k (j c) -> p k j c", c=128)
                    for k in range(4):
                        g = qg.t[:, k:k + 1]
                        S.op('dve', lambda e, k=k, g=g: e.tensor_scalar(
                            out=wq4[:, k, 0:8, :], in0=wqf.t[:, k, :, 0:128], scalar1=g, scalar2=None,
                            op0=ALU.mult), reads=(wqf, qg), writes=(wq,))
                        ropev = wq.t[:, k, 1024:1536].rearrange("p (h c) -> p h c", c=64)
                        S.op('dve', lambda e, k=k, g=g, ropev=ropev: e.tensor_scalar(
                            out=ropev, in0=wqf.t[:, k, :, 128:192], scalar1=g, scalar2=None, op0=ALU.mult),
                             reads=(wqf, qg), writes=(wq,))
                        rotv = wq.t[:, k, 1536:2048].rearrange("p (h c) -> p h c", c=64)
                        S.op('dve', lambda e, k=k, g=g, rotv=rotv: e.tensor_scalar(
                            out=rotv[:, :, 0:32], in0=wqf.t[:, k, :, 160:192], scalar1=g, scalar2=-1.0,
                            op0=ALU.mult, op1=ALU.mult), reads=(wqf, qg), writes=(wq,))
                        S.op('dve', lambda e, k=k, g=g, rotv=rotv: e.tensor_scalar(
                            out=rotv[:, :, 32:64], in0=wqf.t[:, k, :, 128:160], scalar1=g, scalar2=None,
                            op0=ALU.mult), reads=(wqf, qg), writes=(wq,))
                    for k in range(2):
                        g = kvg.t[:, k:k + 1]
                        S.op('dve', lambda e, k=k, g=g: e.tensor_scalar(
                            out=wkv.t[:, k, :, :], in0=wkvf.t[:, k, :].rearrange("p (h c) -> p h c", c=256),
                            scalar1=g, scalar2=None, op0=ALU.mult), reads=(wkvf, kvg), writes=(wkv,))
                    S.emit()
                wsl = [sbt(st, "wsl%d" % i, [128, KC, 512], BF16) for i in range(3)]
                wrot = Rot(wsl)
                hTs = [sbt(st, "hT%d" % i, [128, KC, T], BF16) for i in range(2)]
                xts = [sbt(st, "xt%d" % i, [128, D], F32) for i in range(2)]
                xnb = [sbt(st, "xn%d" % i, [128, D], BF16) for i in range(2)]
                ssq = sbt(st, "ssq", [128, 4], F32)
                rstd = sbt(st, "rstd", [128, 4], F32)
                cq = sbt(st, "cq", [128, 4, T], BF16)
                sqq = sbt(st, "sqq", [128, 4, T], BF16)
                ckv = sbt(st, "ckv", [128, 2, T], BF16)
                sqk = sbt(st, "sqk", [128, 2, T], BF16)
                rq = sbt(st, "rq", [128, T], F32)
                rkv = sbt(st, "rkv", [128, T], F32)
                rkvt = sbt(st, "rkvt", [128, 4], F32)
                crq = sbt(st, "crq", [128, T], F32)
                srq = sbt(st, "srq", [128, T], F32)
                cts = [sbt(st, "ct%d" % i, [128, T], F32) for i in range(1)]
                sts = [sbt(st, "st%d" % i, [128, T], F32) for i in range(1)]
                qs = sbt(st, "qs", [128, 8, T], BF16)
                stg4 = Rot([sbt(st, "stg%d" % i, [128, 4, T], BF16) for i in range(4)])
                tmp = Rot([sbt(st, "tmp%d" % i, [128, T], F32) for i in range(8)])
                trb = [PB[6], PB[7]]
                trrot = Rot(trb)
                arot = Rot([PB[0], PB[1], PB[2], PB[3]])
                srot = Rot([PB[4], PB[5]])

                tiles = [('ctx', cx, 0, CTX, 0, None)]
                for i in range(8):
                    tiles.append(('oth', xt, i * T, T, CTX + LH + i * T, LH + i * T))
                for i in range(8):
                    tiles.append(('own', xo, i * T, T, CTX + i * T, i * T))
                if debug and os.environ.get("MK_NT"):
                    nt_ = int(os.environ["MK_NT"])
                    no_ = int(os.environ.get("MK_NO", "1"))
                    tiles = tiles[:1] + tiles[1:1 + no_] + tiles[9:9 + nt_]

                def load_sub(ti, s):
                    kind, src, r0, n, kr0, tp = tiles[ti]
                    xtile = xts[s % 2]
                    S.dma('sp', xtile.t[:, :], src[r0 + s * 128:r0 + (s + 1) * 128, :], anchor=xtile,
                          writes=(xtile,))

                def prep(ti, hT):
                    kind, src, r0, n, kr0, tp = tiles[ti]
                    gs_ = gsC if kind == 'ctx' else gsA
                    sh_ = shC if kind == 'ctx' else shA
                    nsub = n // 128
                    for s in range(nsub):
                        xtile = xts[s % 2]
                        xn = xnb[s % 2]
                        S.op('act', lambda e, xtile=xtile, xn=xn, s=s: e.activation(
                            out=xn.t[:, :], in_=xtile.t[:, :], func=AF.Square, accum_out=ssq.t[:, s:s + 1]),
                             reads=(xtile,), writes=(xn, ssq))
                        rsqrt(rstd.t[:, s:s + 1], ssq.t[:, s:s + 1], D * EPS, (ssq,), rstd)
                        S.op('dve', lambda e, xtile=xtile, xn=xn, s=s: e.tensor_scalar(
                            out=xn.t[:, :], in0=xtile.t[:, :], scalar1=rstd.t[:, s:s + 1], scalar2=None,
                            op0=ALU.mult), reads=(xtile, rstd, xn), writes=(xn,))
                        if s + 2 < nsub:
                            load_sub(ti, s + 2)
                        elif ti + 1 < len(tiles):
                            s2 = s + 2 - nsub
                            if s2 < tiles[ti + 1][3] // 128:
                                load_sub(ti + 1, s2)
                        for g in range(2):
                            pb = trrot.next()
                            pbv = pb.t[:, :].bitcast(BF16)
                            for kk in range(8):
                                k = g * 8 + kk
                                S.op('pe', lambda e, pbv=pbv, kk=kk, k=k, xn=xn: e.transpose(
                                    out=pbv[:, kk * 128:(kk + 1) * 128], in_=xn.t[:, k * 128:(k + 1) * 128],
                                    identity=ident.t[:, :]), reads=(xn, ident), writes=(pb,))
                            for kk in range(8):
                                k = g * 8 + kk
                                S.op('act', lambda e, pbv=pbv, kk=kk, k=k, s=s, hT=hT, gs_=gs_, sh_=sh_: e.activation(
                                    out=hT.t[:, k, s * 128:(s + 1) * 128], in_=pbv[:, kk * 128:(kk + 1) * 128],
                                    func=AF.Identity, scale=gs_.t[:, k:k + 1], bias=sh_.t[:, k:k + 1]),
                                     reads=(pb, gs_, sh_), writes=(hT,))

                def load_w(blk):
                    w = wrot.next()
                    S.dma('sp', w.t[:, :, :], W1[blk].rearrange("p (k c) -> p k c", c=512), anchor=w, writes=(w,))
                    return w

                def mm_fm(w, c, hT, n, pb):
                    for k in range(KC):
                        S.op('pe', lambda e, k=k: e.matmul(pb.t[:, 0:n], lhsT=w.t[:, k, c * 128:(c + 1) * 128],
                                                           rhs=hT.t[:, k, 0:n], start=(k == 0), stop=(k == KC - 1)),
                             reads=(w, hT), writes=(pb,))

                def store(sg, dst, n):
                    S.dma('pool', dst, sg.t[:, 0:n], anchor=sg, reads=(sg,))

                def act_evac(func, pb, n, out, outb, scale=None):
                    if scale is None:
                        S.op('act', lambda e: e.activation(out=out, in_=pb.t[:, 0:n], func=func), reads=(pb,),
                             writes=(outb,))
                    else:
                        S.op('act', lambda e: e.activation(out=out, in_=pb.t[:, 0:n], func=func, scale=scale),
                             reads=(pb,), writes=(outb,))

                def do_tile(ti, hT, mid):
                    kind, src, r0, n, kr0, tp = tiles[ti]
                    nsub = n // 128
                    own = kind == 'own'
                    ct = cts[0]
                    stt = sts[0]
                    if kind != 'ctx':
                        S.dma('sp', ct.t[:, 0:n], cos2[:, tp:tp + n], anchor=ct, writes=(ct,))
                        S.dma('sp', stt.t[:, 0:n], sin2[:, tp:tp + n], anchor=stt, writes=(stt,))
                    w = load_w(1)
                    for i in range(2):
                        pb = arot.next()
                        mm_fm(w, i, hT, n, pb)
                        act_evac(AF.Copy, pb, n, ckv.t[:, i, 0:n], ckv)
                        act_evac(AF.Square, pb, n, sqk.t[:, i, 0:n], sqk)
                    pbA = arot.next()
                    mm_fm(w, 2, hT, n, pbA)
                    if kind == 'ctx':
                        gk = Grouper(S, stg4)
                        sg = gk.slot()
                        act_evac(AF.Copy, pbA, n, sg.t[:, 0:n], sg)
                        gk.flush(KTr[:, kr0:kr0 + n], n)
                    else:
                        pbB = arot.next()
                        mm_fm(w, 3, hT, n, pbB)
                        t1 = tmp.next()
                        t2 = tmp.next()
                        S.op('dve', lambda e, t1=t1, pbA=pbA: e.tensor_tensor(out=t1.t[:, 0:n], in0=pbA.t[:, 0:n], in1=ct.t[:, 0:n],
                                                              op=ALU.mult), reads=(pbA, ct), writes=(t1,))
                        S.op('dve', lambda e, t2=t2, pbB=pbB: e.tensor_tensor(out=t2.t[:, 0:n], in0=pbB.t[:, 0:n], in1=stt.t[:, 0:n],
                                                              op=ALU.mult), reads=(pbB, stt), writes=(t2,))
                        gk = Grouper(S, stg4)
                        sg = gk.slot()
                        S.op('pool', lambda e, sg=sg, t1=t1, t2=t2: e.tensor_tensor(out=sg.t[:, 0:n], in0=t1.t[:, 0:n], in1=t2.t[:, 0:n],
                                                               op=ALU.add), reads=(t1, t2), writes=(sg,))
                        gk.flush(KTr[:, kr0:kr0 + n], n)
                    pbs = srot.next()
                    for i in range(2):
                        S.op('pe', lambda e, i=i, pbs=pbs: e.matmul(pbs.t[:, 0:n], lhsT=ones_bf.t[:, :], rhs=sqk.t[:, i, 0:n],
                                                           start=(i == 0), stop=(i == 1)), reads=(ones_bf, sqk),
                             writes=(pbs,))
                    rsqrt(rkv.t[:, 0:n], pbs.t[:, 0:n], 256 * EPS, (pbs,), rkv)
                    pbt = srot.next()
                    for s in range(nsub):
                        for i in range(2):
                            S.op('pe', lambda e, i=i, s=s, pbt=pbt: e.matmul(
                                pbt.t[:, 2 * s:2 * s + 2], lhsT=sqk.t[:, i, s * 128:(s + 1) * 128],
                                rhs=ones_bf.t[:, 0:2], start=(i == 0), stop=(i == 1)), reads=(ones_bf, sqk),
                                 writes=(pbt,))
                    rsqrt(rkvt.t[:, 0:nsub], pbt.t[:, 0:2 * nsub:2], 256 * EPS, (pbt,), rkvt)
                    gk = Grouper(S, stg4)
                    for h in range(8):
                        pb = arot.next()
                        for k in range(2):
                            S.op('pe', lambda e, k=k, h=h, pb=pb: e.matmul(
                                pb.t[:, 0:n], lhsT=wkv.t[:, k, h, 0:128], rhs=ckv.t[:, k, 0:n], start=(k == 0),
                                stop=(k == 1)), reads=(wkv, ckv), writes=(pb,))
                        sg = gk.slot()
                        S.op('dve', lambda e, pb=pb, sg=sg: e.tensor_tensor(out=sg.t[:, 0:n], in0=pb.t[:, 0:n],
                                                                            in1=rkv.t[:, 0:n], op=ALU.mult),
                             reads=(pb, rkv), writes=(sg,))
                        if h % 4 == 3:
                            gk.flush(KTn[h - 3:h + 1, :, kr0:kr0 + n].rearrange("h p t -> p h t"), n)
                    for s in range(nsub):
                        kt = (kr0 + s * 128) // 128
                        gv = Grouper(S, stg4)
                        for j in range(2):
                            pb = arot.next()
                            for k in range(2):
                                S.op('pe', lambda e, k=k, j=j, s=s, pb=pb: e.matmul(
                                    pb.t[:, :].rearrange("p (h c) -> p h c", c=128),
                                    lhsT=ckv.t[:, k, s * 128:(s + 1) * 128], rhs=wkv.t[:, k, 4 * j:4 * j + 4, 128:256],
                                    start=(k == 0), stop=(k == 1)), reads=(wkv, ckv), writes=(pb,))
                            sg = gv.slot()
                            S.op('act', lambda e, pb=pb, sg=sg, s=s: e.activation(
                                out=sg.t[:, 0:T], in_=pb.t[:, :], func=AF.Copy, scale=rkvt.t[:, s:s + 1]),
                                 reads=(pb, rkvt), writes=(sg,))
                        gv.flush(VH[:, :, kt, :].rearrange("h p c -> p h c"), T,
                                 src=gv.cur.t[:, 0:2, :].rearrange("p j (h c) -> p (j h) c", c=128))
                    if own:
                        w = load_w(0)
                        for i in range(4):
                            pb = arot.next()
                            mm_fm(w, i, hT, n, pb)
                            act_evac(AF.Copy, pb, n, cq.t[:, i, 0:n], cq)
                            act_evac(AF.Square, pb, n, sqq.t[:, i, 0:n], sqq)
                        pbs = srot.next()
                        for i in range(4):
                            S.op('pe', lambda e, i=i, pbs=pbs: e.matmul(pbs.t[:, 0:n], lhsT=ones_bf.t[:, :],
                                                               rhs=sqq.t[:, i, 0:n], start=(i == 0), stop=(i == 3)),
                                 reads=(ones_bf, sqq), writes=(pbs,))
                        rsqrt(rq.t[:, 0:n], pbs.t[:, 0:n], 512 * EPS, (pbs,), rq)
                        S.op('dve', lambda e: e.tensor_tensor(out=crq.t[:, 0:n], in0=ct.t[:, 0:n], in1=rq.t[:, 0:n],
                                                              op=ALU.mult), reads=(ct, rq), writes=(crq,))
                        S.op('dve', lambda e: e.tensor_tensor(out=srq.t[:, 0:n], in0=stt.t[:, 0:n], in1=rq.t[:, 0:n],
                                                              op=ALU.mult), reads=(stt, rq), writes=(srq,))

                        def qmm(j, pb):
                            for k in range(4):
                                S.op('pe', lambda e, k=k: e.matmul(pb.t[:, 0:n], lhsT=wq.t[:, k, j * 128:(j + 1) * 128],
                                                                   rhs=cq.t[:, k, 0:n], start=(k == 0), stop=(k == 3)),
                                     reads=(wq, cq), writes=(pb,))

                        gq = Grouper(S, stg4)
                        for j in range(8):
                            pb = arot.next()
                            qmm(j, pb)
                            sg = gq.slot()
                            S.op('dve', lambda e, pb=pb, sg=sg: e.tensor_tensor(out=sg.t[:, 0:n], in0=pb.t[:, 0:n],
                                                                                in1=rq.t[:, 0:n], op=ALU.mult),
                                 reads=(pb, rq), writes=(sg,))
                            if j % 4 == 3:
                                gq.flush(QTn[j - 3:j + 1, :, tp:tp + n].rearrange("h p t -> p h t"), n)
                        gq = Grouper(S, stg4)
                        for j in range(4):
                            pbA = arot.next()
                            qmm(8 + j, pbA)
                            pbB = arot.next()
                            qmm(12 + j, pbB)
                            t1 = tmp.next()
                            t2 = tmp.next()
                            S.op('dve', lambda e, pbA=pbA, t1=t1: e.tensor_tensor(
                                out=t1.t[:, 0:n], in0=pbA.t[:, 0:n], in1=crq.t[:, 0:n], op=ALU.mult),
                                 reads=(pbA, crq), writes=(t1,))
                            S.op('dve', lambda e, pbB=pbB, t2=t2: e.tensor_tensor(
                                out=t2.t[:, 0:n], in0=pbB.t[:, 0:n], in1=srq.t[:, 0:n], op=ALU.mult),
                                 reads=(pbB, srq), writes=(t2,))
                            sg = gq.slot()
                            S.op('pool', lambda e, sg=sg, t1=t1, t2=t2: e.tensor_tensor(
                                out=sg.t[:, 0:n], in0=t1.t[:, 0:n], in1=t2.t[:, 0:n], op=ALU.add), reads=(t1, t2),
                                 writes=(sg,))
                        gq.flush(QTr[0:4, :, tp:tp + n].rearrange("h p t -> p h t"), n)
                        for blk in (2, 3):
                            w = load_w(blk)
                            for c in range(4):
                                h = (blk - 2) * 4 + c
                                pb = arot.next()
                                mm_fm(w, c, hT, n, pb)
                                act_evac(AF.Silu, pb, n, qs.t[:, h, 0:n], qs)
                    fblks = (4, 5, 6, 7) if kind != 'oth' else (6, 7)
                    for blk in fblks:
                        w = load_w(blk)
                        r = (blk - 4) // 2
                        gkd = Grouper(S, stg4)
                        gqd = Grouper(S, stg4)
                        for c in range(4):
                            h = (blk % 2) * 4 + c
                            pb = arot.next()
                            mm_fm(w, c, hT, n, pb)
                            sgm = tmp.next()
                            act_evac(AF.Sigmoid, pb, n, sgm.t[:, 0:n], sgm, scale=-1.0)
                            kk_ = tmp.next()
                            S.op('dve', lambda e, sgm=sgm, kk_=kk_, r=r, h=h: e.tensor_scalar(
                                out=kk_.t[:, 0:n], in0=sgm.t[:, 0:n], scalar1=oml.t[:, r * 8 + h:r * 8 + h + 1],
                                scalar2=None, op0=ALU.mult), reads=(sgm, oml), writes=(kk_,))
                            lf = tmp.next()
                            S.op('act', lambda e, kk_=kk_, lf=lf: e.activation(
                                out=lf.t[:, 0:n], in_=kk_.t[:, 0:n], func=AF.Ln, scale=-1.0, bias=1.0),
                                 reads=(kk_,), writes=(lf,))
                            bb = tmp.next()
                            S.op('dve', lambda e, lf=lf, bb=bb: e.tensor_tensor_scan(
                                out=bb.t[:, 0:n], data0=mskF.t[:, 0:n], data1=lf.t[:, 0:n], initial=0.0,
                                op0=ALU.mult, op1=ALU.add), reads=(lf, mskF), writes=(bb,))
                            if r == 1:
                                d1 = tmp.next()
                                S.op('pool', lambda e, lf=lf, bb=bb, d1=d1: e.tensor_tensor(
                                    out=d1.t[:, 0:n], in0=lf.t[:, 0:n], in1=bb.t[:, 0:n], op=ALU.subtract),
                                     reads=(lf, bb), writes=(d1,))
                                b2 = tmp.next()
                                bb3 = bb.t[:, 0:n].rearrange("p (c t) -> p c t", t=64)
                                S.op('dve', lambda e, d1=d1, b2=b2, bb3=bb3: e.tensor_tensor(
                                    out=b2.t[:, 0:n].rearrange("p (c t) -> p c t", t=64),
                                    in0=d1.t[:, 0:n].rearrange("p (c t) -> p c t", t=64),
                                    in1=bb3[:, :, 63:64].to_broadcast([128, n // 64, 64]), op=ALU.add),
                                     reads=(d1, bb), writes=(b2,))
                                bb = b2
                            E = tmp.next()
                            S.op('act', lambda e, bb=bb, E=E: e.activation(out=E.t[:, 0:n], in_=bb.t[:, 0:n],
                                                                           func=AF.Exp), reads=(bb,), writes=(E,))
                            Ei = tmp.next()
                            S.op('act', lambda e, bb=bb, Ei=Ei: e.activation(out=Ei.t[:, 0:n], in_=bb.t[:, 0:n],
                                                                             func=AF.Exp, scale=-1.0), reads=(bb,),
                                 writes=(Ei,))
                            sg = gkd.slot()
                            S.op('dve', lambda e, kk_=kk_, Ei=Ei, sg=sg: e.tensor_tensor(
                                out=sg.t[:, 0:n], in0=kk_.t[:, 0:n], in1=Ei.t[:, 0:n], op=ALU.mult),
                                 reads=(kk_, Ei), writes=(sg,))
                            if c == 3:
                                gkd.flush(KD[r, h - 3:h + 1, :, kr0:kr0 + n].rearrange("h p t -> p h t"), n)
                            if own:
                                sg2 = gqd.slot()
                                S.op('dve', lambda e, E=E, sg2=sg2, h=h: e.scalar_tensor_tensor(
                                    out=sg2.t[:, 0:n], in0=qs.t[:, h, 0:n], scalar=float(128.0 ** -0.5),
                                    in1=E.t[:, 0:n], op0=ALU.mult, op1=ALU.mult), reads=(qs, E), writes=(sg2,))
                                if c == 3:
                                    gqd.flush(QD[r, h - 3:h + 1, :, tp:tp + n].rearrange("h p t -> p h t"), n)
                            E3 = E.t[:, 0:n].rearrange("p (c t) -> p c t", t=64)
                            ci0 = kr0 // 64
                            col = 63 if r == 0 else 0
                            S.op('pool', lambda e, E3=E3, r=r, h=h, ci0=ci0, col=col: e.tensor_copy(
                                out=dec.t[:, r, h, ci0:ci0 + n // 64], in_=E3[:, :, col]), reads=(E,), writes=(dec,))
                    mid()
                    for blk in (8, 9):
                        w = load_w(blk)
                        j = blk - 8
                        ghv = Grouper(S, stg4)
                        for s in range(nsub):
                            pb = arot.next()
                            for k in range(KC):
                                S.op('pe', lambda e, k=k, s=s, pb=pb, w=w: e.matmul(
                                    pb.t[:, :], lhsT=hT.t[:, k, s * 128:(s + 1) * 128], rhs=w.t[:, k, :],
                                    start=(k == 0), stop=(k == KC - 1)), reads=(w, hT), writes=(pb,))
                            sg = ghv.slot()
                            S.op('act', lambda e, pb=pb, sg=sg: e.activation(out=sg.t[:, 0:T], in_=pb.t[:, :],
                                                                             func=AF.Copy), reads=(pb,), writes=(sg,))
                        ghv.flush(HV[kr0:kr0 + n, j * 512:(j + 1) * 512].rearrange("(s p) c -> p s c", p=128), T,
                                  src=ghv.cur.t[:, 0:nsub, :])
                    if own:
                        for blk in (10, 11):
                            w = load_w(blk)
                            gg_ = Grouper(S, stg4)
                            for c in range(4):
                                h = (blk - 10) * 4 + c
                                pb = arot.next()
                                mm_fm(w, c, hT, n, pb)
                                sg = gg_.slot()
                                act_evac(AF.Silu, pb, n, sg.t[:, 0:n], sg)
                            gg_.flush(GH[h - 3:h + 1, :, tp:tp + n].rearrange("h p t -> p h t"), n)
                        for blk in range(12, 20):
                            w = load_w(blk)
                            gg_ = Grouper(S, stg4)
                            for c in range(4):
                                gi = (blk - 12) * 4 + c
                                pb = arot.next()
                                mm_fm(w, c, hT, n, pb)
                                sg = gg_.slot()
                                act_evac(AF.Sigmoid, pb, n, sg.t[:, 0:n], sg)
                            gg_.flush(GG[gi - 3:gi + 1, :, tp:tp + n].rearrange("h p t -> p h t"), n)

                load_sub(0, 0)
                load_sub(0, 1)
                prep(0, hTs[0])
                for ti in range(len(tiles)):
                    def mid(ti=ti):
                        if ti + 1 < len(tiles):
                            prep(ti + 1, hTs[(ti + 1) % 2])
                    do_tile(ti, hTs[ti % 2], mid)
                S.emit()
        if 'B' in PHASES:
            with ExitStack() as st:
                kn = [sbt(st, "kn%d" % i, [128, NKEY], BF16) for i in range(2)]
                vh = [sbt(st, "vh%d" % i, [128, NKT, 128], BF16) for i in range(2)]
                qn = [sbt(st, "qn%d" % i, [128, LH], BF16) for i in range(2)]
                qr = [sbt(st, "qr%d" % i, [128, LH], BF16) for i in range(2)]
                kre = sbt(st, "kre", [128, NKEY], BF16)
                kro = sbt(st, "kro", [128, NKEY], BF16)
                NPT = 6
                pT = [sbt(st, "pT%d" % i, [128, T], BF16) for i in range(NPT)]
                osb = Rot([sbt(st, "osb%d" % i, [128, T], BF16) for i in range(2)])
                rinv = Rot([sbt(st, "rinv%d" % i, [128, T], F32) for i in range(2)])
                accD = [sbt(st, "accD%d" % i, [128, T], F32) for i in range(2)]
                accP = [sbt(st, "accP%d" % i, [128, T], F32) for i in range(2)]
                ones_ff = sbt(st, "ones_ff", [128, 128], F32)
                S.op('pool', lambda e: e.memset(ones_ff.t[:, :], 1.0), writes=(ones_ff,))
                sbank = [PB[0], PB[1], PB[2]]
                obank = [PB[3], PB[4]]
                rbank = [PB[5], PB[6]]
                nqt = LH // T
                if debug and os.environ.get("MK_NT"):
                    nqt = int(os.environ["MK_NT"])
                heads = list(range(8))
                if debug and os.environ.get("MK_NH"):
                    heads = list(range(int(os.environ["MK_NH"])))
                S.drain('sp')
                S.op('pool', lambda e: e.memset(kre.t[64:128, :], 0.0), writes=(kre,))
                S.op('pool', lambda e: e.memset(kro.t[0:64, :], 0.0), writes=(kro,))
                S.dma('sp', kre.t[0:64, :], KTr[0:64, :], anchor=kre, reads=(), writes=(kre,))
                S.dma('sp', kro.t[64:128, :], KTr[64:128, :], anchor=kro, reads=(), writes=(kro,))
                steps = [(h, qt, kt) for h in heads for qt in range(nqt) for kt in range(NKT)]
                per_head = nqt * NKT
                sc_att = float(192.0 ** -0.5)

                def load_head(h):
                    S.dma('sp', kn[h % 2].t[:, :], KTn[h, :, :], anchor=kn[h % 2], writes=(kn[h % 2],))
                    S.dma('sp', vh[h % 2].t[:, :, :], VH[h, :, :, :], anchor=vh[h % 2], writes=(vh[h % 2],))
                    S.dma('sp', qn[h % 2].t[:, 0:nqt * T], QTn[h, :, 0:nqt * T], anchor=qn[h % 2],
                          writes=(qn[h % 2],))
                    if h % 2 == 0:
                        p = h // 2
                        S.dma('sp', qr[p % 2].t[:, 0:nqt * T], QTr[p, :, 0:nqt * T], anchor=qr[p % 2],
                              writes=(qr[p % 2],))

                def rs_mode(kt):
                    if kt % 11 == 10:
                        return 'pe'
                    i = kt - kt // 11
                    return 'pool' if (i % 2) == 1 else 'dve'

                def emit_qk(g):
                    h, qt, kt = steps[g]
                    sb = sbank[g % 3]
                    krx = kre if h % 2 == 0 else kro
                    qrx = qr[(h // 2) % 2]
                    knx = kn[h % 2]
                    qnx = qn[h % 2]
                    S.op('pe', lambda e: e.matmul(sb.t[:, :], lhsT=knx.t[:, kt * 128:(kt + 1) * 128],
                                                  rhs=qnx.t[:, qt * T:(qt + 1) * T], start=True, stop=False),
                         reads=(knx, qnx), writes=(sb,))
                    S.op('pe', lambda e: e.matmul(sb.t[:, :], lhsT=krx.t[:, kt * 128:(kt + 1) * 128],
                                                  rhs=qrx.t[:, qt * T:(qt + 1) * T], start=False, stop=True),
                         reads=(krx, qrx), writes=(sb,))
                    p = pT[g % NPT]
                    S.op('act', lambda e: e.activation(out=p.t[:, :], in_=sb.t[:, :], func=AF.Exp, scale=sc_att),
                         reads=(sb,), writes=(p,))
                    idx = (g // NKT) % 2
                    mode = rs_mode(kt)
                    if mode != 'pe':
                        acc = accP[idx] if mode == 'pool' else accD[idx]
                        firstk = 1 if mode == 'pool' else 0
                        if kt == firstk:
                            S.op(mode, lambda e: e.tensor_copy(out=acc.t[:, :], in_=p.t[:, :]), reads=(p,),
                                 writes=(acc,))
                        else:
                            S.op(mode, lambda e: e.tensor_tensor(out=acc.t[:, :], in0=acc.t[:, :], in1=p.t[:, :],
                                                                 op=ALU.add), reads=(acc, p), writes=(acc,))

                def emit_pv(g):
                    h, qt, kt = steps[g]
                    idx = (g // NKT) % 2
                    ob = obank[idx]
                    rb = rbank[idx]
                    p = pT[g % NPT]
                    vx = vh[h % 2]
                    S.op('pe', lambda e: e.matmul(ob.t[:, :], lhsT=vx.t[:, kt, :], rhs=p.t[:, :], start=(kt == 0),
                                                  stop=(kt == NKT - 1)), reads=(vx, p), writes=(ob,))
                    if rs_mode(kt) == 'pe':
                        S.op('pe', lambda e: e.matmul(rb.t[:, :], lhsT=ones_bf.t[:, :], rhs=p.t[:, :], start=(kt == 10),
                                                      stop=False), reads=(ones_bf, p), writes=(rb,))
                    if kt == NKT - 1:
                        aD, aP = accD[idx], accP[idx]
                        S.op('pe', lambda e: e.matmul(rb.t[:, :], lhsT=ones_ff.t[:, :], rhs=aD.t[:, :], start=False,
                                                      stop=False), reads=(ones_ff, aD), writes=(rb,))
                        S.op('pe', lambda e: e.matmul(rb.t[:, :], lhsT=ones_ff.t[:, :], rhs=aP.t[:, :], start=False,
                                                      stop=True), reads=(ones_ff, aP), writes=(rb,))
                        ri = rinv.next()
                        os_ = osb.next()
                        S.op('dve', lambda e: e.reciprocal(out=ri.t[:, :], in_=rb.t[:, :]), reads=(rb,), writes=(ri,))
                        S.op('dve', lambda e: e.tensor_tensor(out=os_.t[:, :], in0=ob.t[:, :], in1=ri.t[:, :],
                                                              op=ALU.mult), reads=(ob, ri), writes=(os_,))
                        S.dma('pool', AT[h, :, qt * T:(qt + 1) * T], os_.t[:, :], anchor=os_, reads=(os_,))

                load_head(heads[0])
                G = len(steps)
                for g in range(G + 2):
                    if g >= 2 and (g - 2) % per_head == 0:
                        hn = (g - 2) // per_head + 1
                        if hn < len(heads):
                            load_head(heads[hn])
                    if g < G:
                        emit_qk(g)
                    if g >= 2:
                        emit_pv(g - 2)
                S.emit()
        if 'H' in PHASES:
            with ExitStack() as st:
                kdb = [sbt(st, "kdb%d" % i, [128, 8, T], BF16) for i in range(2)]
                qdb = [sbt(st, "qdb%d" % i, [128, 8, T], BF16) for i in range(2)]
                hvb = [sbt(st, "hvb%d" % i, [128, 4, 1024], BF16) for i in range(2)]
                o1b = [sbt(st, "o1b%d" % i, [128, 8, T], F32) for i in range(2)]
                ost = [sbt(st, "ost%d" % i, [128, 8, T], F32) for i in range(2)]
                hst = [sbt(st, "hst%d" % i, [128, 8, T], BF16) for i in range(2)]
                Tst = [sbt(st, "Tst%d" % h, [128, 128], F32) for h in range(8)]
                Sbf = [[sbt(st, "Sbf%d_%d" % (h, i), [128, 128], BF16) for i in range(2)] for h in range(8)]
                kdA = [sbt(st, "kdA%d" % i, [128, 128], BF16) for i in range(8)]
                kdB = [sbt(st, "kdB%d" % i, [128, 128], BF16) for i in range(8)]
                smk = [sbt(st, "smk%d" % i, [128, 128], BF16) for i in range(8)]
                ptrb = PB[7]
                ptrv = ptrb.t[:, :].bitcast(BF16)
                ptr_slot = [ptrb] * 8
                psS_slot = [PB[4]] * 4
                psUa_slot = [PB[5]] * 4
                psUb_slot = [PB[6]] * 4
                psO = [PB[0], PB[1], PB[2], PB[3]]
                psS = PB[4]
                psU = [PB[5], PB[6]]
                for i in range(8):
                    S.op('pool', lambda e, i=i: e.memset(kdA[i].t[:, :], 0.0), writes=(kdA[i],))
                    S.op('pool', lambda e, i=i: e.memset(kdB[i].t[:, :], 0.0), writes=(kdB[i],))
                masks = (mask1, mask2)
                n_own = 8
                n_oth = 8
                if debug and os.environ.get("MK_NT"):
                    n_own = int(os.environ["MK_NT"])
                    n_oth = 1
                for r in range(int(os.environ.get("MK_HR", "2")) if debug else 2):
                    S.drain('sp')
                    for h in range(8):
                        S.op('pool', lambda e, h=h: e.memset(Tst[h].t[:, :], 0.0), writes=(Tst[h],))
                        S.op('pool', lambda e, h=h: e.memset(Sbf[h][0].t[:, :], 0.0), writes=(Sbf[h][0],))
                    scur = [0] * 8
                    if r == 0:
                        htiles = [('ctx', 0, CTX, None)] + [('own', CTX + i * T, T, i * T) for i in range(n_own)]
                    else:
                        htiles = [('ctx', 0, CTX, None)]
                        htiles += [('oth', CTX + LH + i * T, T, None) for i in reversed(range(n_oth))]
                        htiles += [('own', CTX + i * T, T, i * T) for i in reversed(range(n_own))]
                    first = [True]

                    def hload(i):
                        kind, row0, n, tok0 = htiles[i]
                        bi = i % 2
                        S.dma('sp', kdb[bi].t[:, :, 0:n], KD[r, :, :, row0:row0 + n].rearrange("h p t -> p h t"),
                              anchor=kdb[bi], writes=(kdb[bi],))
                        S.dma('sp', hvb[bi].t[:, 0:n // 128, :],
                              HV[row0:row0 + n, :].rearrange("(s p) c -> p s c", p=128), anchor=hvb[bi],
                              writes=(hvb[bi],))
                        if kind == 'own':
                            S.dma('sp', qdb[bi].t[:, :, 0:n],
                                  QD[r, :, :, tok0:tok0 + n].rearrange("h p t -> p h t"), anchor=qdb[bi],
                                  writes=(qdb[bi],))
                            if r == 1:
                                S.dma('sp', o1b[bi].t[:, :, 0:n],
                                      O1[:, :, tok0:tok0 + n].rearrange("h p t -> p h t"), anchor=o1b[bi],
                                      writes=(o1b[bi],))

                    def hcompute(i):
                        kind, row0, n, tok0 = htiles[i]
                        bi = i % 2
                        own = kind == 'own'
                        nsub = n // 128
                        kd_, qd_, hv_, o1_ = kdb[bi], qdb[bi], hvb[bi], o1b[bi]
                        os_ = ost[i % 2]
                        hs_ = hst[i % 2]
                        subs = list(range(nsub)) if r == 0 else list(reversed(range(nsub)))
                        corder = (0, 1) if r == 0 else (1, 0)
                        def do_chunk(s_, hs, j, c):
                            ci = (row0 + s_ * 128 + c * 64) // 64
                            kdx = kdA if c == 0 else kdB
                            psUx = psU[j]
                            slots = psUa_slot if j == 0 else psUb_slot
                            csl = slice(s_ * 128 + c * 64, s_ * 128 + c * 64 + 64)
                            for h in hs:
                                q = h % 4
                                if own and HL >= 4:
                                    sb_ = Sbf[h][scur[h] % 2]
                                    S.op('pe', lambda e, h=h, q=q, sb_=sb_: e.matmul(
                                        psO[q].t[:, c * 64:c * 64 + 64], lhsT=sb_.t[:, :],
                                        rhs=qd_.t[:, h, csl], start=False, stop=(j == 1)),
                                         reads=(sb_, qd_), writes=(psO[q],))
                                S.op('pe', lambda e, h=h, q=q: e.matmul(
                                    psUx.t[:, q * 128:(q + 1) * 128], lhsT=kdx[h].t[:, :],
                                    rhs=hv_.t[:, s_, h * 128:(h + 1) * 128], start=True, stop=True),
                                     reads=(kdx[h], hv_), writes=(slots[q],))
                            if HL < 3:
                                return
                            for h in hs:
                                q = h % 4
                                dcur = dec.t[:, r, h, ci:ci + 1]
                                dp = dprev[h] if dprev[h] is not None else dcur
                                S.op('dve', lambda e, h=h, q=q, dp=dp: e.scalar_tensor_tensor(
                                    out=Tst[h].t[:, :], in0=Tst[h].t[:, :], scalar=dp,
                                    in1=psUx.t[:, q * 128:(q + 1) * 128], op0=ALU.mult, op1=ALU.add),
                                     reads=(Tst[h], slots[q], dec), writes=(Tst[h],))
                                scur[h] += 1
                                sb2 = Sbf[h][scur[h] % 2]
                                S.op('act', lambda e, h=h, sb2=sb2, dcur=dcur: e.activation(
                                    out=sb2.t[:, :], in_=Tst[h].t[:, :], func=AF.Copy, scale=dcur),
                                     reads=(Tst[h], dec), writes=(sb2,))
                                dprev[h] = dcur

                        HL = int(os.environ.get("MK_HLVL", "9"))

                        def do_group(s_, hs):
                            ssl = slice(s_ * 128, (s_ + 1) * 128)
                            if HL < 1:
                                return
                            for h in hs:
                                S.op('pe', lambda e, h=h: e.transpose(
                                    out=ptrv[:, h * 128:(h + 1) * 128], in_=kd_.t[:, h, ssl],
                                    identity=ident.t[:, :]), reads=(kd_, ident), writes=(ptr_slot[h],))
                            for h in hs:
                                S.op('act', lambda e, h=h: e.activation(
                                    out=kdA[h].t[0:64, :], in_=ptrv[0:64, h * 128:(h + 1) * 128], func=AF.Copy),
                                     reads=(ptr_slot[h],), writes=(kdA[h],))
                                S.op('act', lambda e, h=h: e.activation(
                                    out=kdB[h].t[64:128, :], in_=ptrv[64:128, h * 128:(h + 1) * 128],
                                    func=AF.Copy), reads=(ptr_slot[h],), writes=(kdB[h],))
                            if HL < 2:
                                return
                            if own and HL >= 4:
                                for h in hs:
                                    q = h % 4
                                    S.op('pe', lambda e, h=h, q=q: e.matmul(
                                        psS.t[:, q * 128:(q + 1) * 128], lhsT=kd_.t[:, h, ssl],
                                        rhs=qd_.t[:, h, ssl], start=True, stop=True), reads=(kd_, qd_),
                                         writes=(psS_slot[q],))
                                for h in hs:
                                    q = h % 4
                                    S.op('dve', lambda e, h=h, q=q: e.tensor_tensor(
                                        out=smk[h].t[:, :], in0=psS.t[:, q * 128:(q + 1) * 128],
                                        in1=masks[r].t[:, :], op=ALU.mult), reads=(psS_slot[q], masks[r]),
                                         writes=(smk[h],))
                                for h in hs:
                                    q = h % 4
                                    S.op('pe', lambda e, h=h, q=q: e.matmul(
                                        psO[q].t[:, 0:128], lhsT=hv_.t[:, s_, h * 128:(h + 1) * 128],
                                        rhs=smk[h].t[:, :], start=True, stop=False), reads=(hv_, smk[h]),
                                         writes=(psO[q],))
                            for j, c in enumerate(corder):
                                do_chunk(s_, hs, j, c)
                            if own and HL >= 4:
                                for h in hs:
                                    q = h % 4
                                    if r == 0:
                                        S.op('act', lambda e, h=h, q=q: e.activation(
                                            out=os_.t[:, h, ssl], in_=psO[q].t[:, 0:128], func=AF.Copy),
                                             reads=(psO[q],), writes=(os_,))
                                    else:
                                        S.op('dve', lambda e, h=h, q=q: e.tensor_tensor(
                                            out=hs_.t[:, h, ssl], in0=psO[q].t[:, 0:128], in1=o1_.t[:, h, ssl],
                                            op=ALU.add), reads=(psO[q], o1_), writes=(hs_,))

                        for s_ in subs:
                            for grp in range(2):
                                do_group(s_, list(range(grp * 4, grp * 4 + 4)))
                        if own and HL >= 4:
                            if r == 0:
                                S.dma('pool', O1[:, :, tok0:tok0 + n].rearrange("h p t -> p h t"), os_.t[:, :, 0:n],
                                      anchor=os_, reads=(os_,))
                            else:
                                S.dma('pool', HO[:, :, tok0:tok0 + n].rearrange("h p t -> p h t"), hs_.t[:, :, 0:n],
                                      anchor=hs_, reads=(hs_,))

                    dprev = [None] * 8

                    if debug and os.environ.get("MK_HT"):
                        htiles = htiles[:int(os.environ["MK_HT"])]
                    hload(0)
                    for i in range(len(htiles)):
                        if i + 1 < len(htiles):
                            hload(i + 1)
                        hcompute(i)
                    S.emit()
        stAH.close()
        if 'C' in PHASES:
            with ExitStack() as st:
                x4 = sbt(st, "x4", [128, 4, D], F32)
                raw = sbt(st, "raw", [128, 4, D], F32)
                act = sbt(st, "act", [128, 44, T], BF16)
                fT = sbt(st, "fT", [128, KC, T], BF16)
                xnc = [sbt(st, "xnc%d" % i, [128, D], BF16) for i in range(2)]
                wslc = Rot([sbt(st, "wslc%d" % i, [128, 8, 512], BF16) for i in range(4)])
                hob = Rot([sbt(st, "hob%d" % i, [128, T], BF16) for i in range(2)])
                ghb = Rot([sbt(st, "ghb%d" % i, [128, T], BF16) for i in range(2)])
                gab = Rot([sbt(st, "gab%d" % i, [128, T], BF16) for i in range(2)])
                gbb = Rot([sbt(st, "gbb%d" % i, [128, T], BF16) for i in range(2)])
                sqb = Rot([sbt(st, "sqb%d" % i, [128, T], BF16) for i in range(2)])
                Gb = sbt(st, "Gb", [128, D], F32)
                sa = sbt(st, "sa", [128, 4, T], BF16)
                tmpc = Rot([sbt(st, "tmpc%d" % i, [128, T], F32) for i in range(4)])
                ssq4 = sbt(st, "ssq4", [128, 16], F32)
                ssq1 = sbt(st, "ssq1", [128, 4], F32)
                rstdc = sbt(st, "rstdc", [128, 4], F32)
                crot = Rot([PB[0], PB[1], PB[2], PB[3]])
                c2rot = Rot([PB[4], PB[5]])
                ctr = Rot([PB[6], PB[7]])
                n_own = 8
                if debug and os.environ.get("MK_NT"):
                    n_own = int(os.environ["MK_NT"])
                S.drain('sp')

                def loadw(src, k0, nk):
                    w = wslc.next()
                    S.dma('sp', w.t[:, 0:nk, :], src.rearrange("p (k c) -> p k c", c=512)[:, k0:k0 + nk, :], anchor=w,
                          writes=(w,))
                    return w

                def loadw_cols(src, c0):
                    w = wslc.next()
                    wv = w.t[:, :, :].rearrange("p a b -> p (a b)").rearrange("p (k c) -> p k c", c=256)
                    S.dma('sp', wv, src.rearrange("p (k c) -> p k c", c=512)[:, :, c0:c0 + 256], anchor=w, writes=(w,))
                    return w, wv

                def finalize(gidx, addx, tok0, store):
                    S.dma('sp', Gb.t[:, :], GAB[gidx, :, :], anchor=Gb, writes=(Gb,))
                    for s_ in range(4):
                        S.op('dve', lambda e, s_=s_: e.tensor_reduce(
                            out=ssq1.t[:, s_:s_ + 1], in_=ssq4.t[:, s_ * 4:(s_ + 1) * 4], axis=mybir.AxisListType.X,
                            op=ALU.add), reads=(ssq4,), writes=(ssq1,))
                        rsqrt(rstdc.t[:, s_:s_ + 1], ssq1.t[:, s_:s_ + 1], D * EPS, (ssq1,), rstdc)
                        S.op('dve', lambda e, s_=s_: e.scalar_tensor_tensor(
                            out=raw.t[:, s_, :], in0=raw.t[:, s_, :], scalar=rstdc.t[:, s_:s_ + 1], in1=Gb.t[:, :],
                            op0=ALU.mult, op1=ALU.mult), reads=(raw, rstdc, Gb), writes=(raw,))
                        if store:
                            S.op('dve', lambda e, s_=s_: e.tensor_tensor(out=raw.t[:, s_, :], in0=raw.t[:, s_, :],
                                                                         in1=x4.t[:, s_, :], op=ALU.add),
                                 reads=(raw, x4), writes=(raw,))
                            S.dma('pool', y[tok0 + s_ * 128:tok0 + (s_ + 1) * 128, :], raw.t[:, s_, :], anchor=raw,
                                  reads=(raw,))
                        else:
                            S.op('dve', lambda e, s_=s_: e.tensor_tensor(out=x4.t[:, s_, :], in0=x4.t[:, s_, :],
                                                                         in1=raw.t[:, s_, :], op=ALU.add),
                                 reads=(raw, x4), writes=(x4,))

                def evac_raw(pb, s_, blk):
                    S.op('act', lambda e: e.activation(out=raw.t[:, s_, blk * 512:(blk + 1) * 512], in_=pb.t[:, :],
                                                       func=AF.Copy), reads=(pb,), writes=(raw,))
                    jb = sqb.next()
                    S.op('act', lambda e: e.activation(out=jb.t[:, :], in_=pb.t[:, :], func=AF.Square,
                                                       accum_out=ssq4.t[:, s_ * 4 + blk:s_ * 4 + blk + 1]),
                         reads=(pb,), writes=(jb, ssq4))

                def ctile(i):
                    tok0 = i * T
                    S.dma('sp', act.t[:, 0:8, :], AT[:, :, tok0:tok0 + T].rearrange("h p t -> p h t"), anchor=act,
                          writes=(act,))
                    S.dma('sp', x4.t[:, :, :], xo[tok0:tok0 + T, :].rearrange("(s p) c -> p s c", p=128), anchor=x4,
                          writes=(x4,))

                    def hnorm(h):
                        ho = hob.next()
                        gh = ghb.next()
                        S.dma('sp', ho.t[:, :], HO[h, :, tok0:tok0 + T], anchor=ho, writes=(ho,))
                        S.dma('sp', gh.t[:, :], GH[h, :, tok0:tok0 + T], anchor=gh, writes=(gh,))
                        sq_ = sqb.next()
                        S.op('act', lambda e: e.activation(out=sq_.t[:, :], in_=ho.t[:, :], func=AF.Square),
                             reads=(ho,), writes=(sq_,))
                        pb = c2rot.next()
                        S.op('pe', lambda e: e.matmul(pb.t[:, :], lhsT=ones_bf.t[:, :], rhs=sq_.t[:, :], start=True,
                                                      stop=True), reads=(ones_bf, sq_), writes=(pb,))
                        rt = tmpc.next()
                        rsqrt(rt.t[:, :], pb.t[:, :], 128 * EPS, (pb,), rt)
                        t_ = tmpc.next()
                        S.op('dve', lambda e: e.tensor_tensor(out=t_.t[:, :], in0=ho.t[:, :], in1=rt.t[:, :],
                                                              op=ALU.mult), reads=(ho, rt), writes=(t_,))
                        S.op('dve', lambda e: e.scalar_tensor_tensor(
                            out=act.t[:, 8 + h, :], in0=t_.t[:, :], scalar=ongs.t[:, 0:1], in1=gh.t[:, :],
                            op0=ALU.mult, op1=ALU.mult), reads=(t_, ongs, gh), writes=(act,))

                    CL = int(os.environ.get("MK_CLVL", "9")) if debug else 9
                    for h in range(8):
                        hnorm(h)
                    if CL < 2:
                        return

                    def merge_chunk(wa, wh, cc, c):
                        pbA = crot.next()
                        pbH = crot.next()
                        for k in range(8):
                            S.op('pe', lambda e, k=k: e.matmul(pbA.t[:, :], lhsT=wa.t[:, k, cc * 128:(cc + 1) * 128],
                                                               rhs=act.t[:, k, :], start=(k == 0), stop=(k == 7)),
                                 reads=(wa, act), writes=(pbA,))
                        for k in range(8):
                            S.op('pe', lambda e, k=k: e.matmul(pbH.t[:, :], lhsT=wh.t[:, k, cc * 128:(cc + 1) * 128],
                                                               rhs=act.t[:, 8 + k, :], start=(k == 0), stop=(k == 7)),
                                 reads=(wh, act), writes=(pbH,))
                        ga = gab.next()
                        gb = gbb.next()
                        S.dma('sp', ga.t[:, :], GG[c, :, tok0:tok0 + T], anchor=ga, writes=(ga,))
                        S.dma('sp', gb.t[:, :], GG[16 + c, :, tok0:tok0 + T], anchor=gb, writes=(gb,))
                        t1 = tmpc.next()
                        t2 = tmpc.next()
                        S.op('dve', lambda e: e.tensor_tensor(out=t1.t[:, :], in0=pbA.t[:, :], in1=ga.t[:, :],
                                                              op=ALU.mult), reads=(pbA, ga), writes=(t1,))
                        S.op('dve', lambda e: e.tensor_tensor(out=t2.t[:, :], in0=pbH.t[:, :], in1=gb.t[:, :],
                                                              op=ALU.mult), reads=(pbH, gb), writes=(t2,))
                        S.op('pool', lambda e: e.tensor_tensor(out=fT.t[:, c, :], in0=t1.t[:, :], in1=t2.t[:, :],
                                                               op=ALU.add), reads=(t1, t2), writes=(fT,))

                    for blk in range(4):
                        wa = loadw(WBA[blk], 0, 8)
                        wh = loadw(WBH[blk], 0, 8)
                        for cc in range(4):
                            merge_chunk(wa, wh, cc, blk * 4 + cc)

                    if CL < 3:
                        return

                    def tm_block(w, blk, nk, src, kofs, banks, first, last):
                        for s_ in range(4):
                            pb = banks[s_]
                            for k in range(nk):
                                S.op('pe', lambda e, k=k, pb=pb, s_=s_: e.matmul(
                                    pb.t[:, :], lhsT=src.t[:, kofs + k, s_ * 128:(s_ + 1) * 128], rhs=w.t[:, k, :],
                                    start=(first and k == 0), stop=(last and k == nk - 1)), reads=(src, w),
                                     writes=(pb,))

                    for blk in range(4):
                        banks = [PB[(blk % 2) * 4 + q] for q in range(4)]
                        for half in range(2):
                            w = loadw(WO[blk], half * 8, 8)
                            tm_block(w, blk, 8, fT, half * 8, banks, half == 0, half == 1)
                        for s_ in range(4):
                            evac_raw(banks[s_], s_, blk)
                    finalize(0, True, tok0, False)
                    if CL < 4:
                        return

                    def prep_sub(s_):
                        xn = xnc[s_ % 2]
                        S.op('act', lambda e: e.activation(out=xn.t[:, :], in_=x4.t[:, s_, :], func=AF.Square,
                                                           accum_out=ssq1.t[:, s_:s_ + 1]), reads=(x4,),
                             writes=(xn, ssq1))
                        rsqrt(rstdc.t[:, s_:s_ + 1], ssq1.t[:, s_:s_ + 1], D * EPS, (ssq1,), rstdc)
                        S.op('dve', lambda e: e.tensor_scalar(out=xn.t[:, :], in0=x4.t[:, s_, :],
                                                              scalar1=rstdc.t[:, s_:s_ + 1], scalar2=None,
                                                              op0=ALU.mult), reads=(x4, rstdc, xn), writes=(xn,))
                        PSL = int(os.environ.get("MK_PS", "9")) if debug else 9
                        if PSL < 1:
                            return
                        for g in range(2):
                            pb = ctr.next()
                            pbv = pb.t[:, :].bitcast(BF16)
                            for kk in range(8):
                                k = g * 8 + kk
                                S.op('pe', lambda e, kk=kk, k=k, pbv=pbv: e.transpose(
                                    out=pbv[:, kk * 128:(kk + 1) * 128], in_=xn.t[:, k * 128:(k + 1) * 128],
                                    identity=ident.t[:, :]), reads=(xn, ident), writes=(pb,))
                            if PSL < 2:
                                continue
                            for kk in range(8):
                                k = g * 8 + kk
                                S.op('act', lambda e, kk=kk, k=k, pbv=pbv: e.activation(
                                    out=fT.t[:, k, s_ * 128:(s_ + 1) * 128], in_=pbv[:, kk * 128:(kk + 1) * 128],
                                    func=AF.Identity, scale=gsF.t[:, k:k + 1], bias=shF.t[:, k:k + 1]),
                                     reads=(pb, gsF, shF), writes=(fT,))

                    for s_ in range(4):
                        prep_sub(s_)
                    if CL < 5:
                        return

                    def ffn_chunk(w, wv, c2, cc, c, is_a):
                        pb = crot.next()
                        for k in range(KC):
                            S.op('pe', lambda e, k=k: e.matmul(pb.t[:, :], lhsT=wv[:, k, c2 * 128:(c2 + 1) * 128],
                                                               rhs=fT.t[:, k, :], start=(k == 0), stop=(k == KC - 1)),
                                 reads=(w, fT), writes=(pb,))
                        if is_a:
                            S.op('act', lambda e: e.activation(out=sa.t[:, cc, :], in_=pb.t[:, :], func=AF.Silu),
                                 reads=(pb,), writes=(sa,))
                        else:
                            S.op('dve', lambda e: e.tensor_tensor(out=act.t[:, c, :], in0=pb.t[:, :],
                                                                  in1=sa.t[:, cc, :], op=ALU.mult), reads=(pb, sa),
                                 writes=(act,))

                    for blk in range(11):
                        for hc in range(2):
                            w, wv = loadw_cols(WF1[blk], hc * 256)
                            for c2 in range(2):
                                cc = hc * 2 + c2
                                ffn_chunk(w, wv, c2, cc, blk * 4 + cc, True)
                        for hc in range(2):
                            w, wv = loadw_cols(WF1[11 + blk], hc * 256)
                            for c2 in range(2):
                                cc = hc * 2 + c2
                                ffn_chunk(w, wv, c2, cc, blk * 4 + cc, False)

                    if CL < 6:
                        return
                    for blk in range(4):
                        banks = [PB[(blk % 2) * 4 + q] for q in range(4)]
                        for kp in range(4):
                            w = loadw(WF2[blk * 4 + kp], 0, 8)
                            tm_block(w, blk, 8, act, kp * 11, banks, kp == 0, False)
                            w = loadw(WF2[blk * 4 + kp], 8, 3)
                            tm_block(w, blk, 3, act, kp * 11 + 8, banks, False, kp == 3)
                        for s_ in range(4):
                            evac_raw(banks[s_], s_, blk)
                    finalize(1, True, tok0, True)

                for i in range(n_own):
                    ctile(i)
                S.drain('sp')
                S.emit()
        return nc, S, locals()


def _rope_tables():
    pairs = 16
    pos = np.arange(L)
    row = (pos // 64).astype(np.float32)
    col = (pos % 64).astype(np.float32)
    inv = (np.float32(10000.0) ** (-np.arange(pairs, dtype=np.float32) / np.float32(pairs))).astype(np.float32)
    ang = np.concatenate([row[:, None] * inv, col[:, None] * inv], axis=-1).astype(np.float32)
    cos = np.cos(ang).astype(np.float32)
    sin = np.sin(ang).astype(np.float32)
    return np.ascontiguousarray(np.tile(cos.T, (4, 1))), np.ascontiguousarray(np.tile(sin.T, (4, 1)))


def core_inputs(inp, core, shared):
    b = core // 2
    rev = core % 2
    f = lambda a: np.ascontiguousarray(a, dtype=np.float32)
    x = inp['x'][b]
    ctx = inp['ctx'][b]
    if rev:
        x = x[::-1]
        ctx = ctx[::-1]
    key = ('w', rev)
    if key not in shared:
        w_in = inp['w_in'][0]
        lb = inp['hgrn_lb']
        cos2, sin2 = shared['rope']
        if rev:
            w_in = np.concatenate([w_in[:, :1856], w_in[:, 2880:3904], w_in[:, 1856:2880], w_in[:, 3904:]], axis=1)
            lb = lb[:, ::-1]
            cos2 = cos2[:, ::-1]
            sin2 = sin2[:, ::-1]
        shared[key] = dict(w_in=f(w_in), lbT=f(lb.transpose(3, 0, 1, 2).reshape(128, 2, 16)), cos2=f(cos2),
                           sin2=f(sin2))
    if 'common' not in shared:
        ng = inp['norm_g'][0]
        shared['common'] = dict(
            w_mod=f(inp['w_mod'][0]), b_mod=f(inp['b_mod'][0].reshape(1, -1)),
            ng_fm=f(ng.reshape(4, KC, 128).transpose(2, 0, 1)), ng_row=f(ng.reshape(1, -1)),
            qng=f(inp['mla_q_norm'][0].reshape(4, 128).T), kvng=f(inp['mla_kv_norm'][0].reshape(2, 128).T),
            w_uq=f(inp['w_uq'][0]), w_ukv=f(inp['w_ukv'][0]), ong=f(inp['hgrn_o_norm'][0].reshape(128, 1)),
            w_br_mla=f(inp['w_br_mla'][0]), w_br_hgrn=f(inp['w_br_hgrn'][0]), w_out=f(inp['w_out'][0]),
            w_ffn_in=f(inp['w_ffn_in'][0]), w_ffn_out=f(inp['w_ffn_out'][0]))
    m = dict(shared['common'])
    m.update(shared[key])
    cc = np.stack([inp['c'][b], inp['c_ctx']], axis=-1)
    m['cc'] = f(cc.reshape(KC, 128, 2).transpose(1, 0, 2))
    m['xo'] = f(x[:LH])
    m['xt'] = f(x[LH:])
    m['cx'] = f(ctx)
    return m


_NC_CACHE = {}


def kernel(**inputs):
    inp = {k: np.asarray(v) for k, v in inputs.items()}
    if 'nc' not in _NC_CACHE:
        _NC_CACHE['nc'] = build_nc(False)[0]
    nc = _NC_CACHE['nc']
    shared = {'rope': _rope_tables()}
    in_maps = [core_inputs(inp, c, shared) for c in range(8)]
    res = run_bass_kernel_spmd(nc, in_maps, core_ids=list(range(8)))
    out = np.empty((4, L, D), np.float32)
    for c in range(8):
        yv = np.asarray(res.results[c]["y"], dtype=np.float32)
        b = c // 2
        if c % 2 == 0:
            out[b, :LH] = yv
        else:
            out[b, LH:] = yv[::-1]
    return out
```

```python
import os
import numpy as np
from contextlib import ExitStack
import concourse.bass as bass
import concourse.mybir as mybir
from concourse.bass_utils import run_bass_kernel_spmd

F32 = mybir.dt.float32
BF16 = mybir.dt.bfloat16
AF = mybir.ActivationFunctionType
ALU = mybir.AluOpType

D = 2048
KC = 16
T = 512
L = 8192
LH = 4096
CTX = 256
NKEY = L + CTX
NKT = NKEY // 128
DFF = 5632
EPS = 1e-6
NCH = NKEY // 64
PHASES = os.environ.get("MK_PHASES", "WABHC")


class Buf:
    __slots__ = ('name', 'w', 'r', 'dsem', 'dcnt', 'psum')

    def __init__(self, name):
        self.name = name
        self.w = None
        self.r = {}
        self.dsem = None
        self.dcnt = 0
        self.psum = False


class TB:
    def __init__(self, t, name):
        self.t = t
        self.b = Buf(name)


def _b(x):
    return x.b if isinstance(x, TB) else x


class Sched:
    def __init__(self, nc, stack):
        self.nc = nc
        self.stack = stack
        self.engs = ['pe', 'act', 'dve', 'pool', 'sp']
        self.sem = {e: stack.enter_context(nc.semaphore('s_' + e)) for e in self.engs}
        self.cnt = {e: 0 for e in self.engs}
        self.prog = {e: [] for e in self.engs}
        self.waited = {}
        self.anchors = []
        self.nsem = 0

    def _deps(self, eng, reads, writes, skip=None):
        deps = []
        for b in reads:
            b = _b(b)
            if b.w is not None:
                deps.append(b.w)
            if b.psum:
                for t in b.r.values():
                    if t[2] != eng:
                        deps.append(t)
        for b in writes:
            b = _b(b)
            if b.w is not None and b.w[2] != eng:
                deps.append(b.w)
            for t in b.r.values():
                if t[2] != eng:
                    deps.append(t)
        for (s, v, src) in deps:
            if src == eng and eng == 'pe':
                continue
            if skip is not None and s is skip:
                continue
            key = (eng, id(s))
            if self.waited.get(key, 0) >= v:
                continue
            self.waited[key] = v
            self.prog[eng].append(lambda e, s=s, v=v: e.wait_ge(s, v))

    def _mark(self, tok, reads, writes):
        for b in reads:
            _b(b).r[id(tok[0])] = tok
        for b in writes:
            b = _b(b)
            b.w = tok
            b.r = {}

    def op(self, eng, fn, reads=(), writes=()):
        self._deps(eng, reads, writes)
        if self.cnt[eng] >= 30000:
            self.nsem += 1
            self.sem[eng] = self.stack.enter_context(self.nc.semaphore('s_%s_%d' % (eng, self.nsem)))
            self.cnt[eng] = 0
        self.cnt[eng] += 1
        s = self.sem[eng]
        tok = (s, self.cnt[eng], eng)
        self.prog[eng].append(lambda e, fn=fn, s=s: fn(e).then_inc(s, 1))
        self._mark(tok, reads, writes)

    def dma(self, eng, out, in_, anchor, reads=(), writes=()):
        a = _b(anchor)
        if a.dsem is None:
            a.dsem = self.stack.enter_context(self.nc.semaphore('d_' + a.name))
            self.anchors.append(a)
        self._deps(eng, reads, writes, skip=a.dsem)
        a.dcnt += 16
        tok = (a.dsem, a.dcnt, 'dma')
        self.prog[eng].append(lambda e, s=a.dsem, o=out, i=in_: e.dma_start(out=o, in_=i).then_inc(s, 16))
        self._mark(tok, reads, writes)

    def drain(self, eng):
        for a in self.anchors:
            if a.dcnt > 0:
                key = (eng, id(a.dsem))
                if self.waited.get(key, 0) >= a.dcnt:
                    continue
                self.waited[key] = a.dcnt
                self.prog[eng].append(lambda e, s=a.dsem, v=a.dcnt: e.wait_ge(s, v))

    def emit(self):
        nc = self.nc
        prog = self.prog
        with nc.Block() as block:
            @block.tensor
            def _(e):
                for t in prog['pe']:
                    t(e)

            @block.scalar
            def _(e):
                for t in prog['act']:
                    t(e)

            @block.vector
            def _(e):
                for t in prog['dve']:
                    t(e)

            @block.gpsimd
            def _(e):
                for t in prog['pool']:
                    t(e)

            @block.sync
            def _(e):
                for t in prog['sp']:
                    t(e)
        self.prog = {e: [] for e in self.engs}


class _View:
    def __init__(self, t, idx):
        self.tt = t
        self.idx = idx

    def __getitem__(self, key):
        return self.tt[key[0], self.idx, key[1]]


class SV(TB):
    def __init__(self, tile, idx):
        self.b = tile.b
        self.t = _View(tile.t, idx)


class Grouper:
    def __init__(self, S, rot):
        self.S = S
        self.rot = rot
        self.cur = None
        self.cnt = 0

    def slot(self):
        if self.cur is None:
            self.cur = self.rot.next()
            self.cnt = 0
        sv = SV(self.cur, self.cnt)
        self.cnt += 1
        return sv

    def flush(self, dst, ncol, src=None):
        tile = self.cur
        if src is None:
            src = tile.t[:, 0:self.cnt, 0:ncol] if self.cnt > 1 else tile.t[:, 0, 0:ncol]
        self.S.dma('pool', dst, src, anchor=tile, reads=(tile,))
        self.cur = None


class Rot:
    def __init__(self, items):
        self.items = items
        self.i = 0

    def next(self):
        x = self.items[self.i % len(self.items)]
        self.i += 1
        return x


def build_nc(debug=False):
    nc = bass.Bass("TRN2", target_bir_lowering=False)

    def din(name, shape, dt=F32):
        return nc.dram_tensor(name, shape, dt, kind="ExternalInput").ap()

    dbg_out = set(os.environ.get("MK_DBG", "").split(",")) if debug else set()

    def dscr(name, shape, dt=BF16):
        ext = debug and (name in dbg_out or "ALL" in dbg_out)
        return nc.dram_tensor(name, shape, dt, kind=("ExternalOutput" if ext else "Internal")).ap()

    xo = din("xo", [LH, D])
    xt = din("xt", [LH, D])
    cx = din("cx", [CTX, D])
    cc = din("cc", [128, KC, 2])
    w_mod = din("w_mod", [D, 6 * D])
    b_mod = din("b_mod", [1, 6 * D])
    ng_fm = din("ng_fm", [128, 4, KC])
    ng_row = din("ng_row", [1, 4 * D])
    w_in = din("w_in", [D, 10048])
    qng = din("qng", [128, 4])
    kvng = din("kvng", [128, 2])
    w_uq = din("w_uq", [512, 1536])
    w_ukv = din("w_ukv", [256, 2048])
    lbT = din("lbT", [128, 2, 16])
    ong = din("ong", [128, 1])
    w_br_mla = din("w_br_mla", [1024, D])
    w_br_hgrn = din("w_br_hgrn", [1024, D])
    w_out = din("w_out", [D, D])
    w_ffn_in = din("w_ffn_in", [D, 2 * DFF])
    w_ffn_out = din("w_ffn_out", [DFF, D])
    cos2 = din("cos2", [128, L])
    sin2 = din("sin2", [128, L])
    y = nc.dram_tensor("y", [LH, D], F32, kind="ExternalOutput").ap()

    W1 = dscr("W1", [20, 128, KC * 512])
    WBA = dscr("WBA", [4, 128, 8 * 512])
    WBH = dscr("WBH", [4, 128, 8 * 512])
    WO = dscr("WO", [4, 128, KC * 512])
    WF1 = dscr("WF1", [22, 128, KC * 512])
    WF2 = dscr("WF2", [16, 128, 11 * 512])
    KTn = dscr("KTn", [8, 128, NKEY])
    KTr = dscr("KTr", [128, NKEY])
    VH = dscr("VH", [8, 128, NKT, 128])
    QTn = dscr("QTn", [8, 128, LH])
    QTr = dscr("QTr", [4, 128, LH])
    KD = dscr("KD", [2, 8, 128, NKEY])
    QD = dscr("QD", [2, 8, 128, LH])
    HV = dscr("HV", [NKEY, 1024])
    GH = dscr("GH", [8, 128, LH])
    GG = dscr("GG", [32, 128, LH])
    AT = dscr("AT", [8, 128, LH])
    O1 = dscr("O1", [8, 128, LH], F32)
    HO = dscr("HO", [8, 128, LH])
    GAB = dscr("GAB", [2, 128, D], F32)

    with ExitStack() as top:
        S = Sched(nc, top)

        def sbt(st, name, shape, dt):
            return TB(st.enter_context(nc.sbuf_tensor(name, shape, dt)), name)

        def pst(st, name, shape, dt):
            tb = TB(st.enter_context(nc.psum_tensor(name, shape, dt)), name)
            tb.b.psum = True
            return tb

        def rsqrt(out_ap, in_ap, const, reads, outb):
            S.op('act', lambda e: e.activation(out=out_ap, in_=in_ap, func=AF.Sqrt, bias=float(const), scale=1.0),
                 reads=reads, writes=(outb,))
            S.op('dve', lambda e: e.reciprocal(out=out_ap, in_=out_ap), reads=(outb,), writes=(outb,))

        ident = sbt(top, "ident", [128, 128], BF16)
        ones_bf = sbt(top, "ones_bf", [128, 128], BF16)
        ones_f = sbt(top, "ones_f", [1, 128], F32)
        mskF = sbt(top, "mskF", [128, 512], F32)
        mask1 = sbt(top, "mask1", [128, 128], F32)
        mask2 = sbt(top, "mask2", [128, 128], F32)
        modT = sbt(top, "modT", [128, 256], F32)
        gsA = sbt(top, "gsA", [128, KC], F32)
        shA = sbt(top, "shA", [128, KC], F32)
        gsC = sbt(top, "gsC", [128, KC], F32)
        shC = sbt(top, "shC", [128, KC], F32)
        gsF = sbt(top, "gsF", [128, KC], F32)
        shF = sbt(top, "shF", [128, KC], F32)
        ngf = sbt(top, "ngf", [128, 4, KC], F32)
        oml = sbt(top, "oml", [128, 16], F32)
        lbt = sbt(top, "lbt", [128, 2, 16], F32)
        ongs = sbt(top, "ongs", [128, 1], F32)
        stAH = ExitStack()
        dec = sbt(stAH, "dec", [128, 2, 8, NCH], F32)
        PB = [pst(top, "pb%d" % i, [128, 512], F32) for i in range(8)]

        S.op('pool', lambda e: e.memset(ident.t[:, :], 1.0), writes=(ident,))
        S.op('pool', lambda e: e.affine_select(out=ident.t[:, :], in_=ident.t[:, :], pattern=[[-1, 128]],
                                               compare_op=ALU.is_equal, fill=0.0, base=0, channel_multiplier=1),
             reads=(ident,), writes=(ident,))
        S.op('pool', lambda e: e.memset(ones_bf.t[:, :], 1.0), writes=(ones_bf,))
        S.op('pool', lambda e: e.memset(ones_f.t[:, :], 1.0), writes=(ones_f,))
        S.op('pool', lambda e: e.memset(mskF.t[:, :], 1.0), writes=(mskF,))
        mv = mskF.t[:, :].rearrange("p (c t) -> p c t", t=64)
        S.op('pool', lambda e: e.memset(mv[:, :, 0:1], 0.0), reads=(mskF,), writes=(mskF,))
        for (mk, sign) in ((mask1, 1), (mask2, -1)):
            S.op('pool', lambda e, mk=mk: e.memset(mk.t[:, :], 1.0), writes=(mk,))
            S.op('pool', lambda e, mk=mk, sign=sign: e.affine_select(
                out=mk.t[:, :], in_=mk.t[:, :], pattern=[[sign, 128]], compare_op=ALU.is_ge, fill=0.0,
                base=0, channel_multiplier=-sign), reads=(mk,), writes=(mk,))
            S.op('pool', lambda e, mk=mk: e.memset(mk.t[0:64, 64:128], 0.0), reads=(mk,), writes=(mk,))
            S.op('pool', lambda e, mk=mk: e.memset(mk.t[64:128, 0:64], 0.0), reads=(mk,), writes=(mk,))

        if 'W' in PHASES:
            with ExitStack() as st:
                stf = [sbt(st, "stf%d" % i, [128, KC, 512], F32) for i in range(2)]
                stb = [sbt(st, "stb%d" % i, [128, KC, 512], BF16) for i in range(2)]
                rowp = sbt(st, "rowp", [1, 512], F32)
                rowg = sbt(st, "rowg", [1, 512], F32)
                bmp = [sbt(st, "bmp%d" % i, [1, 512], F32) for i in range(2)]
                gpc = [sbt(st, "gpc%d" % i, [1, 512], F32) for i in range(2)]
                cct = sbt(st, "cct", [128, KC, 2], F32)
                scT = sbt(st, "scT", [128, KC, 2], BF16)
                one2 = sbt(st, "one2", [1, 2], F32)
                tmpm = sbt(st, "tmpm", [128, KC], F32)
                gtm = [sbt(st, "gtm%d" % i, [128, 512], F32) for i in range(2)]
                uidx = [0]
                cast_engs = ['act', 'dve']

                def cast(eng, o, i, rd, wr, scale=None):
                    if eng == 'act':
                        if scale is None:
                            S.op('act', lambda e: e.activation(out=o, in_=i, func=AF.Copy), reads=rd, writes=wr)
                        else:
                            S.op('act', lambda e: e.activation(out=o, in_=i, func=AF.Copy, scale=scale), reads=rd,
                                 writes=wr)
                    else:
                        if scale is None:
                            S.op(eng, lambda e: e.tensor_copy(out=o, in_=i), reads=rd, writes=wr)
                        else:
                            S.op(eng, lambda e: e.tensor_scalar(out=o, in0=i, scalar1=scale, scalar2=None,
                                                                 op0=ALU.mult), reads=rd, writes=wr)

                def precast(src, r0, nk, c0, ncol, dst):
                    u = uidx[0]
                    uidx[0] += 1
                    f = stf[u % 2]
                    b = stb[u % 2]
                    srcv = src[r0:r0 + nk * 128, c0:c0 + ncol].rearrange("(k p) c -> p k c", p=128)
                    S.dma('sp', f.t[:, 0:nk, 0:ncol], srcv, anchor=f, writes=(f,))
                    return u, f, b

                def simple_unit(src, r0, nk, c0, dst):
                    u, f, b = precast(src, r0, nk, c0, 512, dst)
                    h = nk // 2
                    cast('act', b.t[:, 0:h, :], f.t[:, 0:h, :], (f,), (b,))
                    cast('dve', b.t[:, h:nk, :], f.t[:, h:nk, :], (f,), (b,))
                    S.dma('pool', dst.rearrange("p (k c) -> p k c", c=512), b.t[:, 0:nk, :], anchor=b, reads=(b,))

                S.dma('sp', cct.t[:, :, :], cc[:, :, :], anchor=cct, writes=(cct,))
                S.op('act', lambda e: e.activation(out=scT.t[:, :, :], in_=cct.t[:, :, :], func=AF.Silu),
                     reads=(cct,), writes=(scT,))
                S.op('pool', lambda e: e.memset(one2.t[:, :], 1.0), writes=(one2,))
                S.dma('sp', ngf.t[:, :, :], ng_fm[:, :, :], anchor=ngf, writes=(ngf,))
                S.dma('sp', lbt.t[:, :, :], lbT[:, :, :], anchor=lbt, writes=(lbt,))
                S.dma('sp', ongs.t[:, :], ong[:, :], anchor=ongs, writes=(ongs,))
                modps = PB[7]
                mrot = Rot([PB[0], PB[1], PB[2], PB[3]])
                brot = Rot([PB[4], PB[5]])
                for blk in range(24):
                    which = blk // 4
                    u, f, b = precast(w_mod, 0, KC, blk * 512, 512, None)
                    cast('act', b.t[:, 0:8, :], f.t[:, 0:8, :], (f,), (b,))
                    cast('dve', b.t[:, 8:16, :], f.t[:, 8:16, :], (f,), (b,))
                    bp = bmp[blk % 2]
                    S.dma('sp', bp.t[:, :], b_mod[0:1, blk * 512:(blk + 1) * 512], anchor=bp, writes=(bp,))
                    for m in range(2):
                        if m == 1 and which >= 2:
                            continue
                        pb = mrot.next()
                        for k in range(KC):
                            S.op('pe', lambda e, pb=pb, k=k, m=m, b=b: e.matmul(
                                pb.t[0:1, :], lhsT=scT.t[:, k, m:m + 1], rhs=b.t[:, k, :], start=(k == 0),
                                stop=(k == KC - 1)), reads=(scT, b), writes=(pb,))
                        S.op('dve', lambda e, pb=pb, bp=bp: e.tensor_tensor(out=rowp.t[:, :], in0=pb.t[0:1, :],
                                                                            in1=bp.t[:, :], op=ALU.add),
                             reads=(pb, bp), writes=(rowp,))
                        if which in (0, 1, 3, 4):
                            for j in range(4):
                                k = (blk % 4) * 4 + j
                                col = ((m * 6 + which) * KC + k)
                                S.op('pe', lambda e, j=j, col=col: e.matmul(
                                    modps.t[:, 2 * col:2 * col + 2], lhsT=rowp.t[0:1, j * 128:(j + 1) * 128],
                                    rhs=one2.t[0:1, 0:2], start=True, stop=True), reads=(rowp, one2),
                                     writes=(modps,))
                        elif m == 0:
                            gi = 1 if which == 2 else 3
                            gp = gpc[blk % 2]
                            cb = (blk % 4) * 512
                            S.dma('sp', gp.t[:, :], ng_row[0:1, gi * D + cb: gi * D + cb + 512], anchor=gp,
                                  writes=(gp,))
                            S.op('dve', lambda e, gp=gp: e.scalar_tensor_tensor(
                                out=rowg.t[:, :], in0=rowp.t[:, :], scalar=float(np.sqrt(D)), in1=gp.t[:, :],
                                op0=ALU.mult, op1=ALU.mult), reads=(rowp, gp), writes=(rowg,))
                            pbb = brot.next()
                            S.op('pe', lambda e, pbb=pbb: e.matmul(pbb.t[:, :], lhsT=ones_f.t[0:1, :],
                                                                   rhs=rowg.t[0:1, :], start=True, stop=True),
                                 reads=(ones_f, rowg), writes=(pbb,))
                            gt_ = gtm[blk % 2]
                            S.op('act', lambda e, pbb=pbb, gt_=gt_: e.activation(
                                out=gt_.t[:, :], in_=pbb.t[:, :], func=AF.Copy), reads=(pbb,), writes=(gt_,))
                            S.dma('pool', GAB[0 if which == 2 else 1, :, cb:cb + 512], gt_.t[:, :], anchor=gt_,
                                  reads=(gt_,))
                S.op('dve', lambda e: e.tensor_copy(out=modT.t[:, 0:192], in_=modps.t[:, 0:384:2]), reads=(modps,),
                     writes=(modT,))

                def mcol(m, which):
                    c0 = (m * 6 + which) * KC
                    return modT.t[:, c0:c0 + KC]

                sq = float(np.sqrt(D))
                for (gs_, sh_, m, wsc, wsh, gi) in ((gsA, shA, 0, 1, 0, 0), (gsC, shC, 1, 1, 0, 0),
                                                    (gsF, shF, 0, 4, 3, 2)):
                    S.op('dve', lambda e, m=m, wsc=wsc: e.tensor_scalar(
                        out=tmpm.t[:, :], in0=mcol(m, wsc), scalar1=1.0, scalar2=sq, op0=ALU.add, op1=ALU.mult),
                         reads=(modT,), writes=(tmpm,))
                    S.op('dve', lambda e, gs_=gs_, gi=gi: e.tensor_tensor(out=gs_.t[:, :], in0=tmpm.t[:, :],
                                                                          in1=ngf.t[:, gi, :], op=ALU.mult),
                         reads=(tmpm, ngf), writes=(gs_,))
                    S.op('dve', lambda e, sh_=sh_, m=m, wsh=wsh: e.tensor_copy(out=sh_.t[:, :], in_=mcol(m, wsh)),
                         reads=(modT,), writes=(sh_,))
                S.op('dve', lambda e: e.tensor_tensor(out=oml.t[:, :], in0=lbt.t[:, 0, :], in1=lbt.t[:, 1, :],
                                                      op=ALU.subtract), reads=(lbt,), writes=(oml,))
                S.op('act', lambda e: e.activation(out=oml.t[:, :], in_=oml.t[:, :], func=AF.Sigmoid, scale=-1.0),
                     reads=(oml,), writes=(oml,))
                S.op('dve', lambda e: e.tensor_scalar(out=ongs.t[:, :], in0=ongs.t[:, :],
                                                      scalar1=float(np.sqrt(128.0)), scalar2=None, op0=ALU.mult),
                     reads=(ongs,), writes=(ongs,))

                simple_unit(w_in, 0, KC, 0, W1[0])
                u, f, b = precast(w_in, 0, KC, 512, 320, None)
                cast('act', b.t[:, :, 0:256], f.t[:, :, 0:256], (f,), (b,))
                cast('dve', b.t[:, :, 256:320], f.t[:, :, 256:320], (f,), (b,))
                cast('dve', b.t[:, :, 320:384], f.t[:, :, 256:320], (f,), (b,))
                for o in (384, 448):
                    cast('dve', b.t[:, :, o:o + 32], f.t[:, :, 288:320], (f,), (b,), scale=-1.0)
                    cast('dve', b.t[:, :, o + 32:o + 64], f.t[:, :, 256:288], (f,), (b,))
                S.dma('pool', W1[1].rearrange("p (k c) -> p k c", c=512), b.t[:, :, :], anchor=b, reads=(b,))
                for blk in range(2, 20):
                    simple_unit(w_in, 0, KC, 832 + (blk - 2) * 512, W1[blk])
                for blk in range(4):
                    simple_unit(w_br_mla, 0, 8, blk * 512, WBA[blk])
                    simple_unit(w_br_hgrn, 0, 8, blk * 512, WBH[blk])
                    simple_unit(w_out, 0, KC, blk * 512, WO[blk])
                for blk in range(22):
                    simple_unit(w_ffn_in, 0, KC, blk * 512, WF1[blk])
                for blk in range(4):
                    for kp in range(4):
                        simple_unit(w_ffn_out, kp * 11 * 128, 11, blk * 512, WF2[blk * 4 + kp])
                S.emit()

        if 'A' in PHASES:
            with ExitStack() as st:
                wq = sbt(st, "wq", [128, 4, 2048], BF16)
                wkv = sbt(st, "wkv", [128, 2, 8, 256], BF16)
                with ExitStack() as st2:
                    wqf = sbt(st2, "wqf", [128, 4, 8, 192], F32)
                    wkvf = sbt(st2, "wkvf", [128, 2, 2048], F32)
                    qg = sbt(st2, "qg", [128, 4], F32)
                    kvg = sbt(st2, "kvg", [128, 2], F32)
                    S.dma('sp', wqf.t[:, :, :, :], w_uq.rearrange("(k p) (h c) -> p k h c", p=128, c=192),
                          anchor=wqf, writes=(wqf,))
                    S.dma('sp', wkvf.t[:, :, :], w_ukv.rearrange("(k p) c -> p k c", p=128), anchor=wkvf,
                          writes=(wkvf,))
                    S.dma('sp', qg.t[:, :], qng[:, :], anchor=qg, writes=(qg,))
                    S.dma('sp', kvg.t[:, :], kvng[:, :], anchor=kvg, writes=(kvg,))
                    S.op('dve', lambda e: e.tensor_scalar(out=qg.t[:, :], in0=qg.t[:, :],
                                                          scalar1=float(np.sqrt(512.0)), scalar2=None,
                                                          op0=ALU.mult), reads=(qg,), writes=(qg,))
                    S.op('dve', lambda e: e.tensor_scalar(out=kvg.t[:, :], in0=kvg.t[:, :], scalar1=16.0,
                                                          scalar2=None, op0=ALU.mult), reads=(kvg,), writes=(kvg,))
                    wq4 = wq.t[:, :, :].rearrange("p k (j c) -> p k j c", c=128)
                    for k in range(4):
                        g = qg.t[:, k:k + 1]
                        S.op('dve', lambda e, k=k, g=g: e.tensor_scalar(
                            out=wq4[:, k, 0:8, :], in0=wqf.t[:, k, :, 0:128], scalar1=g, scalar2=None,
                            op0=ALU.mult), reads=(wqf, qg), writes=(wq,))
                        ropev = wq.t[:, k, 1024:1536].rearrange("p (h c) -> p h c", c=64)
                        S.op('dve', lambda e, k=k, g=g, ropev=ropev: e.tensor_scalar(
                            out=ropev, in0=wqf.t[:, k, :, 128:192], scalar1=g, scalar2=None, op0=ALU.mult),
                             reads=(wqf, qg), writes=(wq,))
                        rotv = wq.t[:, k, 1536:2048].rearrange("p (h c) -> p h c", c=64)
                        S.op('dve', lambda e, k=k, g=g, rotv=rotv: e.tensor_scalar(
                            out=rotv[:, :, 0:32], in0=wqf.t[:, k, :, 160:192], scalar1=g, scalar2=-1.0,
                            op0=ALU.mult, op1=ALU.mult), reads=(wqf, qg), writes=(wq,))
                        S.op('dve', lambda e, k=k, g=g, rotv=rotv: e.tensor_scalar(
                            out=rotv[:, :, 32:64], in0=wqf.t[:, k, :, 128:160], scalar1=g, scalar2=None,
                            op0=ALU.mult), reads=(wqf, qg), writes=(wq,))
                    for k in range(2):
                        g = kvg.t[:, k:k + 1]
                        S.op('dve', lambda e, k=k, g=g: e.tensor_scalar(
                            out=wkv.t[:, k, :, :], in0=wkvf.t[:, k, :].rearrange("p (h c) -> p h c", c=256),
                            scalar1=g, scalar2=None, op0=ALU.mult), reads=(wkvf, kvg), writes=(wkv,))
                    S.emit()
                wsl = [sbt(st, "wsl%d" % i, [128, KC, 512], BF16) for i in range(3)]
                wrot = Rot(wsl)
                hTs = [sbt(st, "hT%d" % i, [128, KC, T], BF16) for i in range(2)]
                xts = [sbt(st, "xt%d" % i, [128, D], F32) for i in range(2)]
                xnb = [sbt(st, "xn%d" % i, [128, D], BF16) for i in range(2)]
                ssq = sbt(st, "ssq", [128, 4], F32)
                rstd = sbt(st, "rstd", [128, 4], F32)
                cq = sbt(st, "cq", [128, 4, T], BF16)
                sqq = sbt(st, "sqq", [128, 4, T], BF16)
                ckv = sbt(st, "ckv", [128, 2, T], BF16)
                sqk = sbt(st, "sqk", [128, 2, T], BF16)
                rq = sbt(st, "rq", [128, T], F32)
                rkv = sbt(st, "rkv", [128, T], F32)
                rkvt = sbt(st, "rkvt", [128, 4], F32)
                crq = sbt(st, "crq", [128, T], F32)
                srq = sbt(st, "srq", [128, T], F32)
                cts = [sbt(st, "ct%d" % i, [128, T], F32) for i in range(1)]
                sts = [sbt(st, "st%d" % i, [128, T], F32) for i in range(1)]
                qs = sbt(st, "qs", [128, 8, T], BF16)
                stg4 = Rot([sbt(st, "stg%d" % i, [128, 4, T], BF16) for i in range(4)])
                tmp = Rot([sbt(st, "tmp%d" % i, [128, T], F32) for i in range(8)])
                trb = [PB[6], PB[7]]
                trrot = Rot(trb)
                arot = Rot([PB[0], PB[1], PB[2], PB[3]])
                srot = Rot([PB[4], PB[5]])

                tiles = [('ctx', cx, 0, CTX, 0, None)]
                for i in range(8):
                    tiles.append(('oth', xt, i * T, T, CTX + LH + i * T, LH + i * T))
                for i in range(8):
                    tiles.append(('own', xo, i * T, T, CTX + i * T, i * T))
                if debug and os.environ.get("MK_NT"):
                    nt_ = int(os.environ["MK_NT"])
                    no_ = int(os.environ.get("MK_NO", "1"))
                    tiles = tiles[:1] + tiles[1:1 + no_] + tiles[9:9 + nt_]

                def load_sub(ti, s):
                    kind, src, r0, n, kr0, tp = tiles[ti]
                    xtile = xts[s % 2]
                    S.dma('act', xtile.t[:, :], src[r0 + s * 128:r0 + (s + 1) * 128, :], anchor=xtile,
                          writes=(xtile,))

                def prep_sub(ti, hT, s):
                    kind, src, r0, n, kr0, tp = tiles[ti]
                    gs_ = gsC if kind == 'ctx' else gsA
                    sh_ = shC if kind == 'ctx' else shA
                    nsub = n // 128
                    if True:
                        xtile = xts[s % 2]
                        xn = xnb[s % 2]
                        S.op('act', lambda e, xtile=xtile, xn=xn, s=s: e.activation(
                            out=xn.t[:, :], in_=xtile.t[:, :], func=AF.Square, accum_out=ssq.t[:, s:s + 1]),
                             reads=(xtile,), writes=(xn, ssq))
                        rsqrt(rstd.t[:, s:s + 1], ssq.t[:, s:s + 1], D * EPS, (ssq,), rstd)
                        S.op('dve', lambda e, xtile=xtile, xn=xn, s=s: e.tensor_scalar(
                            out=xn.t[:, :], in0=xtile.t[:, :], scalar1=rstd.t[:, s:s + 1], scalar2=None,
                            op0=ALU.mult), reads=(xtile, rstd, xn), writes=(xn,))
                        if s + 2 < nsub:
                            load_sub(ti, s + 2)
                        elif ti + 1 < len(tiles):
                            s2 = s + 2 - nsub
                            if s2 < tiles[ti + 1][3] // 128:
                                load_sub(ti + 1, s2)
                        for g in range(2):
                            pb = trrot.next()
                            pbv = pb.t[:, :].bitcast(BF16)
                            for kk in range(8):
                                k = g * 8 + kk
                                S.op('pe', lambda e, pbv=pbv, kk=kk, k=k, xn=xn: e.transpose(
                                    out=pbv[:, kk * 128:(kk + 1) * 128], in_=xn.t[:, k * 128:(k + 1) * 128],
                                    identity=ident.t[:, :]), reads=(xn, ident), writes=(pb,))
                            for kk in range(8):
                                k = g * 8 + kk
                                S.op('act', lambda e, pbv=pbv, kk=kk, k=k, s=s, hT=hT, gs_=gs_, sh_=sh_: e.activation(
                                    out=hT.t[:, k, s * 128:(s + 1) * 128], in_=pbv[:, kk * 128:(kk + 1) * 128],
                                    func=AF.Identity, scale=gs_.t[:, k:k + 1], bias=sh_.t[:, k:k + 1]),
                                     reads=(pb, gs_, sh_), writes=(hT,))

                def load_w(blk):
                    w = wrot.next()
                    S.dma('sp', w.t[:, :, :], W1[blk].rearrange("p (k c) -> p k c", c=512), anchor=w, writes=(w,))
                    return w

                def mm_fm(w, c, hT, n, pb):
                    for k in range(KC):
                        S.op('pe', lambda e, k=k: e.matmul(pb.t[:, 0:n], lhsT=w.t[:, k, c * 128:(c + 1) * 128],
                                                           rhs=hT.t[:, k, 0:n], start=(k == 0), stop=(k == KC - 1)),
                             reads=(w, hT), writes=(pb,))

                def store(sg, dst, n):
                    S.dma('pool', dst, sg.t[:, 0:n], anchor=sg, reads=(sg,))

                def act_evac(func, pb, n, out, outb, scale=None):
                    if scale is None:
                        S.op('act', lambda e: e.activation(out=out, in_=pb.t[:, 0:n], func=func), reads=(pb,),
                             writes=(outb,))
                    else:
                        S.op('act', lambda e: e.activation(out=out, in_=pb.t[:, 0:n], func=func, scale=scale),
                             reads=(pb,), writes=(outb,))

                def do_tile(ti, hT, hook):
                    kind, src, r0, n, kr0, tp = tiles[ti]
                    nsub = n // 128
                    own = kind == 'own'
                    ct = cts[0]
                    stt = sts[0]
                    if kind != 'ctx':
                        S.dma('sp', ct.t[:, 0:n], cos2[:, tp:tp + n], anchor=ct, writes=(ct,))
                        S.dma('sp', stt.t[:, 0:n], sin2[:, tp:tp + n], anchor=stt, writes=(stt,))
                    w = load_w(1)
                    for i in range(2):
                        pb = arot.next()
                        mm_fm(w, i, hT, n, pb)
                        act_evac(AF.Copy, pb, n, ckv.t[:, i, 0:n], ckv)
                        act_evac(AF.Square, pb, n, sqk.t[:, i, 0:n], sqk)
                    pbA = arot.next()
                    mm_fm(w, 2, hT, n, pbA)
                    if kind == 'ctx':
                        gk = Grouper(S, stg4)
                        sg = gk.slot()
                        act_evac(AF.Copy, pbA, n, sg.t[:, 0:n], sg)
                        gk.flush(KTr[:, kr0:kr0 + n], n)
                    else:
                        pbB = arot.next()
                        mm_fm(w, 3, hT, n, pbB)
                        t1 = tmp.next()
                        t2 = tmp.next()
                        S.op('dve', lambda e, t1=t1, pbA=pbA: e.tensor_tensor(out=t1.t[:, 0:n], in0=pbA.t[:, 0:n], in1=ct.t[:, 0:n],
                                                              op=ALU.mult), reads=(pbA, ct), writes=(t1,))
                        S.op('dve', lambda e, t2=t2, pbB=pbB: e.tensor_tensor(out=t2.t[:, 0:n], in0=pbB.t[:, 0:n], in1=stt.t[:, 0:n],
                                                              op=ALU.mult), reads=(pbB, stt), writes=(t2,))
                        gk = Grouper(S, stg4)
                        sg = gk.slot()
                        S.op('pool', lambda e, sg=sg, t1=t1, t2=t2: e.tensor_tensor(out=sg.t[:, 0:n], in0=t1.t[:, 0:n], in1=t2.t[:, 0:n],
                                                               op=ALU.add), reads=(t1, t2), writes=(sg,))
                        gk.flush(KTr[:, kr0:kr0 + n], n)
                    pbs = srot.next()
                    for i in range(2):
                        S.op('pe', lambda e, i=i, pbs=pbs: e.matmul(pbs.t[:, 0:n], lhsT=ones_bf.t[:, :], rhs=sqk.t[:, i, 0:n],
                                                           start=(i == 0), stop=(i == 1)), reads=(ones_bf, sqk),
                             writes=(pbs,))
                    rsqrt(rkv.t[:, 0:n], pbs.t[:, 0:n], 256 * EPS, (pbs,), rkv)
                    pbt = srot.next()
                    for s in range(nsub):
                        for i in range(2):
                            S.op('pe', lambda e, i=i, s=s, pbt=pbt: e.matmul(
                                pbt.t[:, 2 * s:2 * s + 2], lhsT=sqk.t[:, i, s * 128:(s + 1) * 128],
                                rhs=ones_bf.t[:, 0:2], start=(i == 0), stop=(i == 1)), reads=(ones_bf, sqk),
                                 writes=(pbt,))
                    rsqrt(rkvt.t[:, 0:nsub], pbt.t[:, 0:2 * nsub:2], 256 * EPS, (pbt,), rkvt)
                    gk = Grouper(S, stg4)
                    for h in range(8):
                        pb = arot.next()
                        for k in range(2):
                            S.op('pe', lambda e, k=k, h=h, pb=pb: e.matmul(
                                pb.t[:, 0:n], lhsT=wkv.t[:, k, h, 0:128], rhs=ckv.t[:, k, 0:n], start=(k == 0),
                                stop=(k == 1)), reads=(wkv, ckv), writes=(pb,))
                        sg = gk.slot()
                        S.op('dve', lambda e, pb=pb, sg=sg: e.tensor_tensor(out=sg.t[:, 0:n], in0=pb.t[:, 0:n],
                                                                            in1=rkv.t[:, 0:n], op=ALU.mult),
                             reads=(pb, rkv), writes=(sg,))
                        if h % 4 == 3:
                            gk.flush(KTn[h - 3:h + 1, :, kr0:kr0 + n].rearrange("h p t -> p h t"), n)
                    for s in range(nsub):
                        kt = (kr0 + s * 128) // 128
                        gv = Grouper(S, stg4)
                        for j in range(2):
                            pb = arot.next()
                            for k in range(2):
                                S.op('pe', lambda e, k=k, j=j, s=s, pb=pb: e.matmul(
                                    pb.t[:, :].rearrange("p (h c) -> p h c", c=128),
                                    lhsT=ckv.t[:, k, s * 128:(s + 1) * 128], rhs=wkv.t[:, k, 4 * j:4 * j + 4, 128:256],
                                    start=(k == 0), stop=(k == 1)), reads=(wkv, ckv), writes=(pb,))
                            sg = gv.slot()
                            S.op('act', lambda e, pb=pb, sg=sg, s=s: e.activation(
                                out=sg.t[:, 0:T], in_=pb.t[:, :], func=AF.Copy, scale=rkvt.t[:, s:s + 1]),
                                 reads=(pb, rkvt), writes=(sg,))
                        gv.flush(VH[:, :, kt, :].rearrange("h p c -> p h c"), T,
                                 src=gv.cur.t[:, 0:2, :].rearrange("p j (h c) -> p (j h) c", c=128))
                    hook('b1')
                    if own:
                        w = load_w(0)
                        for i in range(4):
                            pb = arot.next()
                            mm_fm(w, i, hT, n, pb)
                            act_evac(AF.Copy, pb, n, cq.t[:, i, 0:n], cq)
                            act_evac(AF.Square, pb, n, sqq.t[:, i, 0:n], sqq)
                        pbs = srot.next()
                        for i in range(4):
                            S.op('pe', lambda e, i=i, pbs=pbs: e.matmul(pbs.t[:, 0:n], lhsT=ones_bf.t[:, :],
                                                               rhs=sqq.t[:, i, 0:n], start=(i == 0), stop=(i == 3)),
                                 reads=(ones_bf, sqq), writes=(pbs,))
                        rsqrt(rq.t[:, 0:n], pbs.t[:, 0:n], 512 * EPS, (pbs,), rq)
                        S.op('dve', lambda e: e.tensor_tensor(out=crq.t[:, 0:n], in0=ct.t[:, 0:n], in1=rq.t[:, 0:n],
                                                              op=ALU.mult), reads=(ct, rq), writes=(crq,))
                        S.op('dve', lambda e: e.tensor_tensor(out=srq.t[:, 0:n], in0=stt.t[:, 0:n], in1=rq.t[:, 0:n],
                                                              op=ALU.mult), reads=(stt, rq), writes=(srq,))

                        def qmm(j, pb):
                            for k in range(4):
                                S.op('pe', lambda e, k=k: e.matmul(pb.t[:, 0:n], lhsT=wq.t[:, k, j * 128:(j + 1) * 128],
                                                                   rhs=cq.t[:, k, 0:n], start=(k == 0), stop=(k == 3)),
                                     reads=(wq, cq), writes=(pb,))

                        gq = Grouper(S, stg4)
                        for j in range(8):
                            pb = arot.next()
                            qmm(j, pb)
                            sg = gq.slot()
                            S.op('dve', lambda e, pb=pb, sg=sg: e.tensor_tensor(out=sg.t[:, 0:n], in0=pb.t[:, 0:n],
                                                                                in1=rq.t[:, 0:n], op=ALU.mult),
                                 reads=(pb, rq), writes=(sg,))
                            if j % 4 == 3:
                                gq.flush(QTn[j - 3:j + 1, :, tp:tp + n].rearrange("h p t -> p h t"), n)
                        gq = Grouper(S, stg4)
                        for j in range(4):
                            pbA = arot.next()
                            qmm(8 + j, pbA)
                            pbB = arot.next()
                            qmm(12 + j, pbB)
                            t1 = tmp.next()
                            t2 = tmp.next()
                            S.op('dve', lambda e, pbA=pbA, t1=t1: e.tensor_tensor(
                                out=t1.t[:, 0:n], in0=pbA.t[:, 0:n], in1=crq.t[:, 0:n], op=ALU.mult),
                                 reads=(pbA, crq), writes=(t1,))
                            S.op('dve', lambda e, pbB=pbB, t2=t2: e.tensor_tensor(
                                out=t2.t[:, 0:n], in0=pbB.t[:, 0:n], in1=srq.t[:, 0:n], op=ALU.mult),
                                 reads=(pbB, srq), writes=(t2,))
                            sg = gq.slot()
                            S.op('pool', lambda e, sg=sg, t1=t1, t2=t2: e.tensor_tensor(
                                out=sg.t[:, 0:n], in0=t1.t[:, 0:n], in1=t2.t[:, 0:n], op=ALU.add), reads=(t1, t2),
                                 writes=(sg,))
                        gq.flush(QTr[0:4, :, tp:tp + n].rearrange("h p t -> p h t"), n)
                        for blk in (2, 3):
                            w = load_w(blk)
                            for c in range(4):
                                h = (blk - 2) * 4 + c
                                pb = arot.next()
                                mm_fm(w, c, hT, n, pb)
                                act_evac(AF.Silu, pb, n, qs.t[:, h, 0:n], qs)
                    fblks = (4, 5, 6, 7) if kind != 'oth' else (6, 7)
                    for blk in fblks:
                        w = load_w(blk)
                        r = (blk - 4) // 2
                        gkd = Grouper(S, stg4)
                        gqd = Grouper(S, stg4)
                        for c in range(4):
                            h = (blk % 2) * 4 + c
                            pb = arot.next()
                            mm_fm(w, c, hT, n, pb)
                            sgm = tmp.next()
                            act_evac(AF.Sigmoid, pb, n, sgm.t[:, 0:n], sgm, scale=-1.0)
                            kk_ = tmp.next()
                            S.op('dve', lambda e, sgm=sgm, kk_=kk_, r=r, h=h: e.tensor_scalar(
                                out=kk_.t[:, 0:n], in0=sgm.t[:, 0:n], scalar1=oml.t[:, r * 8 + h:r * 8 + h + 1],
                                scalar2=None, op0=ALU.mult), reads=(sgm, oml), writes=(kk_,))
                            lf = tmp.next()
                            S.op('act', lambda e, kk_=kk_, lf=lf: e.activation(
                                out=lf.t[:, 0:n], in_=kk_.t[:, 0:n], func=AF.Ln, scale=-1.0, bias=1.0),
                                 reads=(kk_,), writes=(lf,))
                            bb = tmp.next()
                            S.op('dve', lambda e, lf=lf, bb=bb: e.tensor_tensor_scan(
                                out=bb.t[:, 0:n], data0=mskF.t[:, 0:n], data1=lf.t[:, 0:n], initial=0.0,
                                op0=ALU.mult, op1=ALU.add), reads=(lf, mskF), writes=(bb,))
                            if r == 1:
                                d1 = tmp.next()
                                S.op('pool', lambda e, lf=lf, bb=bb, d1=d1: e.tensor_tensor(
                                    out=d1.t[:, 0:n], in0=lf.t[:, 0:n], in1=bb.t[:, 0:n], op=ALU.subtract),
                                     reads=(lf, bb), writes=(d1,))
                                b2 = tmp.next()
                                bb3 = bb.t[:, 0:n].rearrange("p (c t) -> p c t", t=64)
                                S.op('dve', lambda e, d1=d1, b2=b2, bb3=bb3: e.tensor_tensor(
                                    out=b2.t[:, 0:n].rearrange("p (c t) -> p c t", t=64),
                                    in0=d1.t[:, 0:n].rearrange("p (c t) -> p c t", t=64),
                                    in1=bb3[:, :, 63:64].to_broadcast([128, n // 64, 64]), op=ALU.add),
                                     reads=(d1, bb), writes=(b2,))
                                bb = b2
                            E = tmp.next()
                            S.op('act', lambda e, bb=bb, E=E: e.activation(out=E.t[:, 0:n], in_=bb.t[:, 0:n],
                                                                           func=AF.Exp), reads=(bb,), writes=(E,))
                            Ei = tmp.next()
                            S.op('act', lambda e, bb=bb, Ei=Ei: e.activation(out=Ei.t[:, 0:n], in_=bb.t[:, 0:n],
                                                                             func=AF.Exp, scale=-1.0), reads=(bb,),
                                 writes=(Ei,))
                            sg = gkd.slot()
                            S.op('dve', lambda e, kk_=kk_, Ei=Ei, sg=sg: e.tensor_tensor(
                                out=sg.t[:, 0:n], in0=kk_.t[:, 0:n], in1=Ei.t[:, 0:n], op=ALU.mult),
                                 reads=(kk_, Ei), writes=(sg,))
                            if c == 3:
                                gkd.flush(KD[r, h - 3:h + 1, :, kr0:kr0 + n].rearrange("h p t -> p h t"), n)
                                hook('f%d' % blk)
                            if own:
                                sg2 = gqd.slot()
                                S.op('dve', lambda e, E=E, sg2=sg2, h=h: e.scalar_tensor_tensor(
                                    out=sg2.t[:, 0:n], in0=qs.t[:, h, 0:n], scalar=float(128.0 ** -0.5),
                                    in1=E.t[:, 0:n], op0=ALU.mult, op1=ALU.mult), reads=(qs, E), writes=(sg2,))
                                if c == 3:
                                    gqd.flush(QD[r, h - 3:h + 1, :, tp:tp + n].rearrange("h p t -> p h t"), n)
                            E3 = E.t[:, 0:n].rearrange("p (c t) -> p c t", t=64)
                            ci0 = kr0 // 64
                            col = 63 if r == 0 else 0
                            S.op('pool', lambda e, E3=E3, r=r, h=h, ci0=ci0, col=col: e.tensor_copy(
                                out=dec.t[:, r, h, ci0:ci0 + n // 64], in_=E3[:, :, col]), reads=(E,), writes=(dec,))
                    for blk in (8, 9):
                        w = load_w(blk)
                        j = blk - 8
                        ghv = Grouper(S, stg4)
                        for s in range(nsub):
                            pb = arot.next()
                            for k in range(KC):
                                S.op('pe', lambda e, k=k, s=s, pb=pb, w=w: e.matmul(
                                    pb.t[:, :], lhsT=hT.t[:, k, s * 128:(s + 1) * 128], rhs=w.t[:, k, :],
                                    start=(k == 0), stop=(k == KC - 1)), reads=(w, hT), writes=(pb,))
                            sg = ghv.slot()
                            S.op('act', lambda e, pb=pb, sg=sg: e.activation(out=sg.t[:, 0:T], in_=pb.t[:, :],
                                                                             func=AF.Copy), reads=(pb,), writes=(sg,))
                        ghv.flush(HV[kr0:kr0 + n, j * 512:(j + 1) * 512].rearrange("(s p) c -> p s c", p=128), T,
                                  src=ghv.cur.t[:, 0:nsub, :])
                        hook('v%d' % blk)
                    if own:
                        for blk in (10, 11):
                            w = load_w(blk)
                            gg_ = Grouper(S, stg4)
                            for c in range(4):
                                h = (blk - 10) * 4 + c
                                pb = arot.next()
                                mm_fm(w, c, hT, n, pb)
                                sg = gg_.slot()
                                act_evac(AF.Silu, pb, n, sg.t[:, 0:n], sg)
                            gg_.flush(GH[h - 3:h + 1, :, tp:tp + n].rearrange("h p t -> p h t"), n)
                        for blk in range(12, 20):
                            w = load_w(blk)
                            gg_ = Grouper(S, stg4)
                            for c in range(4):
                                gi = (blk - 12) * 4 + c
                                pb = arot.next()
                                mm_fm(w, c, hT, n, pb)
                                sg = gg_.slot()
                                act_evac(AF.Sigmoid, pb, n, sg.t[:, 0:n], sg)
                            gg_.flush(GG[gi - 3:gi + 1, :, tp:tp + n].rearrange("h p t -> p h t"), n)
                            hook('g%d' % blk)

                load_sub(0, 0)
                load_sub(0, 1)
                for s0 in range(tiles[0][3] // 128):
                    prep_sub(0, hTs[0], s0)
                HOOKS = {'own': ('g12', 'g13', 'g14', 'g15'), 'oth': ('b1', 'f6', 'f7', 'v8'),
                         'ctx': ('b1', 'f4', 'f5', 'f6')}
                for ti in range(len(tiles)):
                    pending = []
                    if ti + 1 < len(tiles):
                        pending = [(lambda s1=s1, ti=ti: prep_sub(ti + 1, hTs[(ti + 1) % 2], s1))
                                   for s1 in range(tiles[ti + 1][3] // 128)]
                    where = HOOKS[tiles[ti][0]]

                    def hook(tag, pending=pending, where=where):
                        if tag in where and pending:
                            pending.pop(0)()

                    do_tile(ti, hTs[ti % 2], hook)
                    while pending:
                        pending.pop(0)()
                S.emit()
        if 'B' in PHASES:
            with ExitStack() as st:
                kn = [sbt(st, "kn%d" % i, [128, NKEY], BF16) for i in range(2)]
                vh = [sbt(st, "vh%d" % i, [128, NKT, 128], BF16) for i in range(2)]
                qn = [sbt(st, "qn%d" % i, [128, LH], BF16) for i in range(2)]
                qr = [sbt(st, "qr%d" % i, [128, LH], BF16) for i in range(2)]
                kre = sbt(st, "kre", [128, NKEY], BF16)
                kro = sbt(st, "kro", [128, NKEY], BF16)
                NPT = 6
                pT = [sbt(st, "pT%d" % i, [128, T], BF16) for i in range(NPT)]
                osb = Rot([sbt(st, "osb%d" % i, [128, T], BF16) for i in range(2)])
                rinv = Rot([sbt(st, "rinv%d" % i, [128, T], F32) for i in range(2)])
                accD = [sbt(st, "accD%d" % i, [128, T], F32) for i in range(2)]
                accP = [sbt(st, "accP%d" % i, [128, T], F32) for i in range(2)]
                ones_ff = sbt(st, "ones_ff", [128, 128], F32)
                S.op('pool', lambda e: e.memset(ones_ff.t[:, :], 1.0), writes=(ones_ff,))
                sbank = [PB[0], PB[1], PB[2]]
                obank = [PB[3], PB[4]]
                rbank = [PB[5], PB[6]]
                nqt = LH // T
                if debug and os.environ.get("MK_NT"):
                    nqt = int(os.environ["MK_NT"])
                heads = list(range(8))
                if debug and os.environ.get("MK_NH"):
                    heads = list(range(int(os.environ["MK_NH"])))
                S.drain('sp')
                S.op('pool', lambda e: e.memset(kre.t[64:128, :], 0.0), writes=(kre,))
                S.op('pool', lambda e: e.memset(kro.t[0:64, :], 0.0), writes=(kro,))
                S.dma('sp', kre.t[0:64, :], KTr[0:64, :], anchor=kre, reads=(), writes=(kre,))
                S.dma('sp', kro.t[64:128, :], KTr[64:128, :], anchor=kro, reads=(), writes=(kro,))
                steps = [(h, qt, kt) for h in heads for qt in range(nqt) for kt in range(NKT)]
                per_head = nqt * NKT
                sc_att = float(192.0 ** -0.5)

                def load_head(h):
                    S.dma('sp', kn[h % 2].t[:, :], KTn[h, :, :], anchor=kn[h % 2], writes=(kn[h % 2],))
                    S.dma('sp', vh[h % 2].t[:, :, :], VH[h, :, :, :], anchor=vh[h % 2], writes=(vh[h % 2],))
                    S.dma('sp', qn[h % 2].t[:, 0:nqt * T], QTn[h, :, 0:nqt * T], anchor=qn[h % 2],
                          writes=(qn[h % 2],))
                    if h % 2 == 0:
                        p = h // 2
                        S.dma('sp', qr[p % 2].t[:, 0:nqt * T], QTr[p, :, 0:nqt * T], anchor=qr[p % 2],
                              writes=(qr[p % 2],))

                def rs_mode(kt):
                    if kt % 11 == 10:
                        return 'pe'
                    i = kt - kt // 11
                    return 'pool' if (i % 2) == 1 else 'dve'

                def emit_qk(g):
                    h, qt, kt = steps[g]
                    sb = sbank[g % 3]
                    krx = kre if h % 2 == 0 else kro
                    qrx = qr[(h // 2) % 2]
                    knx = kn[h % 2]
                    qnx = qn[h % 2]
                    S.op('pe', lambda e: e.matmul(sb.t[:, :], lhsT=knx.t[:, kt * 128:(kt + 1) * 128],
                                                  rhs=qnx.t[:, qt * T:(qt + 1) * T], start=True, stop=False),
                         reads=(knx, qnx), writes=(sb,))
                    S.op('pe', lambda e: e.matmul(sb.t[:, :], lhsT=krx.t[:, kt * 128:(kt + 1) * 128],
                                                  rhs=qrx.t[:, qt * T:(qt + 1) * T], start=False, stop=True),
                         reads=(krx, qrx), writes=(sb,))
                    p = pT[g % NPT]
                    S.op('act', lambda e: e.activation(out=p.t[:, :], in_=sb.t[:, :], func=AF.Exp, scale=sc_att),
                         reads=(sb,), writes=(p,))
                    idx = (g // NKT) % 2
                    mode = rs_mode(kt)
                    if mode != 'pe':
                        acc = accP[idx] if mode == 'pool' else accD[idx]
                        firstk = 1 if mode == 'pool' else 0
                        if kt == firstk:
                            S.op(mode, lambda e: e.tensor_copy(out=acc.t[:, :], in_=p.t[:, :]), reads=(p,),
                                 writes=(acc,))
                        else:
                            S.op(mode, lambda e: e.tensor_tensor(out=acc.t[:, :], in0=acc.t[:, :], in1=p.t[:, :],
                                                                 op=ALU.add), reads=(acc, p), writes=(acc,))

                def emit_pv(g):
                    h, qt, kt = steps[g]
                    idx = (g // NKT) % 2
                    ob = obank[idx]
                    rb = rbank[idx]
                    p = pT[g % NPT]
                    vx = vh[h % 2]
                    S.op('pe', lambda e: e.matmul(ob.t[:, :], lhsT=vx.t[:, kt, :], rhs=p.t[:, :], start=(kt == 0),
                                                  stop=(kt == NKT - 1)), reads=(vx, p), writes=(ob,))
                    if rs_mode(kt) == 'pe':
                        S.op('pe', lambda e: e.matmul(rb.t[:, :], lhsT=ones_bf.t[:, :], rhs=p.t[:, :], start=(kt == 10),
                                                      stop=False), reads=(ones_bf, p), writes=(rb,))
                    if kt == NKT - 1:
                        aD, aP = accD[idx], accP[idx]
                        S.op('pe', lambda e: e.matmul(rb.t[:, :], lhsT=ones_ff.t[:, :], rhs=aD.t[:, :], start=False,
                                                      stop=False), reads=(ones_ff, aD), writes=(rb,))
                        S.op('pe', lambda e: e.matmul(rb.t[:, :], lhsT=ones_ff.t[:, :], rhs=aP.t[:, :], start=False,
                                                      stop=True), reads=(ones_ff, aP), writes=(rb,))
                        ri = rinv.next()
                        os_ = osb.next()
                        S.op('dve', lambda e: e.reciprocal(out=ri.t[:, :], in_=rb.t[:, :]), reads=(rb,), writes=(ri,))
                        S.op('dve', lambda e: e.tensor_tensor(out=os_.t[:, :], in0=ob.t[:, :], in1=ri.t[:, :],
                                                              op=ALU.mult), reads=(ob, ri), writes=(os_,))
                        S.dma('pool', AT[h, :, qt * T:(qt + 1) * T], os_.t[:, :], anchor=os_, reads=(os_,))

                load_head(heads[0])
                G = len(steps)
                for g in range(G + 2):
                    if g >= 2 and (g - 2) % per_head == 0:
                        hn = (g - 2) // per_head + 1
                        if hn < len(heads):
                            load_head(heads[hn])
                    if g < G:
                        emit_qk(g)
                    if g >= 2:
                        emit_pv(g - 2)
                S.emit()
        if 'H' in PHASES:
            with ExitStack() as st:
                kdb = [sbt(st, "kdb%d" % i, [128, 8, T], BF16) for i in range(2)]
                qdb = [sbt(st, "qdb%d" % i, [128, 8, T], BF16) for i in range(2)]
                hvb = [sbt(st, "hvb%d" % i, [128, 4, 1024], BF16) for i in range(2)]
                o1b = [sbt(st, "o1b%d" % i, [128, 8, T], F32) for i in range(2)]
                ost = [sbt(st, "ost%d" % i, [128, 8, T], F32) for i in range(2)]
                hst = [sbt(st, "hst%d" % i, [128, 8, T], BF16) for i in range(2)]
                Tst = [sbt(st, "Tst%d" % h, [128, 128], F32) for h in range(8)]
                Sbf = [[sbt(st, "Sbf%d_%d" % (h, i), [128, 128], BF16) for i in range(2)] for h in range(8)]
                kdA = [sbt(st, "kdA%d" % i, [128, 128], BF16) for i in range(8)]
                kdB = [sbt(st, "kdB%d" % i, [128, 128], BF16) for i in range(8)]
                smk = [sbt(st, "smk%d" % i, [128, 128], BF16) for i in range(8)]
                ptrb = PB[7]
                ptrv = ptrb.t[:, :].bitcast(BF16)
                ptr_slot = [ptrb] * 8
                psS_slot = [PB[4]] * 4
                psUa_slot = [PB[5]] * 4
                psUb_slot = [PB[6]] * 4
                psO = [PB[0], PB[1], PB[2], PB[3]]
                psS = PB[4]
                psU = [PB[5], PB[6]]
                for i in range(8):
                    S.op('pool', lambda e, i=i: e.memset(kdA[i].t[:, :], 0.0), writes=(kdA[i],))
                    S.op('pool', lambda e, i=i: e.memset(kdB[i].t[:, :], 0.0), writes=(kdB[i],))
                masks = (mask1, mask2)
                n_own = 8
                n_oth = 8
                if debug and os.environ.get("MK_NT"):
                    n_own = int(os.environ["MK_NT"])
                    n_oth = 1
                for r in range(int(os.environ.get("MK_HR", "2")) if debug else 2):
                    S.drain('sp')
                    for h in range(8):
                        S.op('pool', lambda e, h=h: e.memset(Tst[h].t[:, :], 0.0), writes=(Tst[h],))
                        S.op('pool', lambda e, h=h: e.memset(Sbf[h][0].t[:, :], 0.0), writes=(Sbf[h][0],))
                    scur = [0] * 8
                    if r == 0:
                        htiles = [('ctx', 0, CTX, None)] + [('own', CTX + i * T, T, i * T) for i in range(n_own)]
                    else:
                        htiles = [('ctx', 0, CTX, None)]
                        htiles += [('oth', CTX + LH + i * T, T, None) for i in reversed(range(n_oth))]
                        htiles += [('own', CTX + i * T, T, i * T) for i in reversed(range(n_own))]
                    first = [True]

                    def hload(i):
                        kind, row0, n, tok0 = htiles[i]
                        bi = i % 2
                        S.dma('sp', kdb[bi].t[:, :, 0:n], KD[r, :, :, row0:row0 + n].rearrange("h p t -> p h t"),
                              anchor=kdb[bi], writes=(kdb[bi],))
                        S.dma('sp', hvb[bi].t[:, 0:n // 128, :],
                              HV[row0:row0 + n, :].rearrange("(s p) c -> p s c", p=128), anchor=hvb[bi],
                              writes=(hvb[bi],))
                        if kind == 'own':
                            S.dma('sp', qdb[bi].t[:, :, 0:n],
                                  QD[r, :, :, tok0:tok0 + n].rearrange("h p t -> p h t"), anchor=qdb[bi],
                                  writes=(qdb[bi],))
                            if r == 1:
                                S.dma('sp', o1b[bi].t[:, :, 0:n],
                                      O1[:, :, tok0:tok0 + n].rearrange("h p t -> p h t"), anchor=o1b[bi],
                                      writes=(o1b[bi],))

                    def hcompute(i):
                        kind, row0, n, tok0 = htiles[i]
                        bi = i % 2
                        own = kind == 'own'
                        nsub = n // 128
                        kd_, qd_, hv_, o1_ = kdb[bi], qdb[bi], hvb[bi], o1b[bi]
                        os_ = ost[i % 2]
                        hs_ = hst[i % 2]
                        subs = list(range(nsub)) if r == 0 else list(reversed(range(nsub)))
                        corder = (0, 1) if r == 0 else (1, 0)
                        def do_chunk(s_, hs, j, c):
                            ci = (row0 + s_ * 128 + c * 64) // 64
                            kdx = kdA if c == 0 else kdB
                            psUx = psU[j]
                            slots = psUa_slot if j == 0 else psUb_slot
                            csl = slice(s_ * 128 + c * 64, s_ * 128 + c * 64 + 64)
                            for h in hs:
                                q = h % 4
                                if own and HL >= 4:
                                    sb_ = Sbf[h][scur[h] % 2]
                                    S.op('pe', lambda e, h=h, q=q, sb_=sb_: e.matmul(
                                        psO[q].t[:, c * 64:c * 64 + 64], lhsT=sb_.t[:, :],
                                        rhs=qd_.t[:, h, csl], start=False, stop=(j == 1)),
                                         reads=(sb_, qd_), writes=(psO[q],))
                                S.op('pe', lambda e, h=h, q=q: e.matmul(
                                    psUx.t[:, q * 128:(q + 1) * 128], lhsT=kdx[h].t[:, :],
                                    rhs=hv_.t[:, s_, h * 128:(h + 1) * 128], start=True, stop=True),
                                     reads=(kdx[h], hv_), writes=(slots[q],))
                            if HL < 3:
                                return
                            for h in hs:
                                q = h % 4
                                dcur = dec.t[:, r, h, ci:ci + 1]
                                dp = dprev[h] if dprev[h] is not None else dcur
                                S.op('dve', lambda e, h=h, q=q, dp=dp: e.scalar_tensor_tensor(
                                    out=Tst[h].t[:, :], in0=Tst[h].t[:, :], scalar=dp,
                                    in1=psUx.t[:, q * 128:(q + 1) * 128], op0=ALU.mult, op1=ALU.add),
                                     reads=(Tst[h], slots[q], dec), writes=(Tst[h],))
                                scur[h] += 1
                                sb2 = Sbf[h][scur[h] % 2]
                                S.op('act', lambda e, h=h, sb2=sb2, dcur=dcur: e.activation(
                                    out=sb2.t[:, :], in_=Tst[h].t[:, :], func=AF.Copy, scale=dcur),
                                     reads=(Tst[h], dec), writes=(sb2,))
                                dprev[h] = dcur

                        HL = int(os.environ.get("MK_HLVL", "9"))

                        def do_group(s_, hs):
                            ssl = slice(s_ * 128, (s_ + 1) * 128)
                            if HL < 1:
                                return
                            for h in hs:
                                S.op('pe', lambda e, h=h: e.transpose(
                                    out=ptrv[:, h * 128:(h + 1) * 128], in_=kd_.t[:, h, ssl],
                                    identity=ident.t[:, :]), reads=(kd_, ident), writes=(ptr_slot[h],))
                            for h in hs:
                                S.op('act', lambda e, h=h: e.activation(
                                    out=kdA[h].t[0:64, :], in_=ptrv[0:64, h * 128:(h + 1) * 128], func=AF.Copy),
                                     reads=(ptr_slot[h],), writes=(kdA[h],))
                                S.op('act', lambda e, h=h: e.activation(
                                    out=kdB[h].t[64:128, :], in_=ptrv[64:128, h * 128:(h + 1) * 128],
                                    func=AF.Copy), reads=(ptr_slot[h],), writes=(kdB[h],))
                            if HL < 2:
                                return
                            if own and HL >= 4:
                                for h in hs:
                                    q = h % 4
                                    S.op('pe', lambda e, h=h, q=q: e.matmul(
                                        psS.t[:, q * 128:(q + 1) * 128], lhsT=kd_.t[:, h, ssl],
                                        rhs=qd_.t[:, h, ssl], start=True, stop=True), reads=(kd_, qd_),
                                         writes=(psS_slot[q],))
                                for h in hs:
                                    q = h % 4
                                    S.op('dve', lambda e, h=h, q=q: e.tensor_tensor(
                                        out=smk[h].t[:, :], in0=psS.t[:, q * 128:(q + 1) * 128],
                                        in1=masks[r].t[:, :], op=ALU.mult), reads=(psS_slot[q], masks[r]),
                                         writes=(smk[h],))
                                for h in hs:
                                    q = h % 4
                                    S.op('pe', lambda e, h=h, q=q: e.matmul(
                                        psO[q].t[:, 0:128], lhsT=hv_.t[:, s_, h * 128:(h + 1) * 128],
                                        rhs=smk[h].t[:, :], start=True, stop=False), reads=(hv_, smk[h]),
                                         writes=(psO[q],))
                            for j, c in enumerate(corder):
                                do_chunk(s_, hs, j, c)
                            if own and HL >= 4:
                                for h in hs:
                                    q = h % 4
                                    if r == 0:
                                        S.op('act', lambda e, h=h, q=q: e.activation(
                                            out=os_.t[:, h, ssl], in_=psO[q].t[:, 0:128], func=AF.Copy),
                                             reads=(psO[q],), writes=(os_,))
                                    else:
                                        S.op('dve', lambda e, h=h, q=q: e.tensor_tensor(
                                            out=hs_.t[:, h, ssl], in0=psO[q].t[:, 0:128], in1=o1_.t[:, h, ssl],
                                            op=ALU.add), reads=(psO[q], o1_), writes=(hs_,))

                        for s_ in subs:
                            for grp in range(2):
                                do_group(s_, list(range(grp * 4, grp * 4 + 4)))
                        if own and HL >= 4:
                            if r == 0:
                                S.dma('pool', O1[:, :, tok0:tok0 + n].rearrange("h p t -> p h t"), os_.t[:, :, 0:n],
                                      anchor=os_, reads=(os_,))
                            else:
                                S.dma('pool', HO[:, :, tok0:tok0 + n].rearrange("h p t -> p h t"), hs_.t[:, :, 0:n],
                                      anchor=hs_, reads=(hs_,))

                    dprev = [None] * 8

                    if debug and os.environ.get("MK_HT"):
                        htiles = htiles[:int(os.environ["MK_HT"])]
                    hload(0)
                    for i in range(len(htiles)):
                        if i + 1 < len(htiles):
                            hload(i + 1)
                        hcompute(i)
                    S.emit()
        stAH.close()
        if 'C' in PHASES:
            with ExitStack() as st:
                x4 = sbt(st, "x4", [128, 4, D], F32)
                raw = sbt(st, "raw", [128, 4, D], F32)
                act = sbt(st, "act", [128, 44, T], BF16)
                fT = sbt(st, "fT", [128, KC, T], BF16)
                xnc = [sbt(st, "xnc%d" % i, [128, D], BF16) for i in range(2)]
                wslc = Rot([sbt(st, "wslc%d" % i, [128, 8, 512], BF16) for i in range(4)])
                hob = Rot([sbt(st, "hob%d" % i, [128, T], BF16) for i in range(2)])
                ghb = Rot([sbt(st, "ghb%d" % i, [128, T], BF16) for i in range(2)])
                gab = Rot([sbt(st, "gab%d" % i, [128, T], BF16) for i in range(2)])
                gbb = Rot([sbt(st, "gbb%d" % i, [128, T], BF16) for i in range(2)])
                sqb = Rot([sbt(st, "sqb%d" % i, [128, T], BF16) for i in range(2)])
                Gb = sbt(st, "Gb", [128, D], F32)
                sa = sbt(st, "sa", [128, 4, T], BF16)
                tmpc = Rot([sbt(st, "tmpc%d" % i, [128, T], F32) for i in range(4)])
                ssq4 = sbt(st, "ssq4", [128, 16], F32)
                ssq1 = sbt(st, "ssq1", [128, 4], F32)
                rstdc = sbt(st, "rstdc", [128, 4], F32)
                crot = Rot([PB[0], PB[1], PB[2], PB[3]])
                c2rot = Rot([PB[4], PB[5]])
                ctr = Rot([PB[6], PB[7]])
                n_own = 8
                if debug and os.environ.get("MK_NT"):
                    n_own = int(os.environ["MK_NT"])
                S.drain('sp')
                S.drain('act')

                def loadw(src, k0, nk):
                    w = wslc.next()
                    S.dma('sp', w.t[:, 0:nk, :], src.rearrange("p (k c) -> p k c", c=512)[:, k0:k0 + nk, :], anchor=w,
                          writes=(w,))
                    return w

                def loadw_cols(src, c0):
                    w = wslc.next()
                    wv = w.t[:, :, :].rearrange("p a b -> p (a b)").rearrange("p (k c) -> p k c", c=256)
                    S.dma('sp', wv, src.rearrange("p (k c) -> p k c", c=512)[:, :, c0:c0 + 256], anchor=w, writes=(w,))
                    return w, wv

                def finalize(gidx, addx, tok0, store):
                    S.dma('act', Gb.t[:, :], GAB[gidx, :, :], anchor=Gb, writes=(Gb,))
                    for s_ in range(4):
                        S.op('dve', lambda e, s_=s_: e.tensor_reduce(
                            out=ssq1.t[:, s_:s_ + 1], in_=ssq4.t[:, s_ * 4:(s_ + 1) * 4], axis=mybir.AxisListType.X,
                            op=ALU.add), reads=(ssq4,), writes=(ssq1,))
                        rsqrt(rstdc.t[:, s_:s_ + 1], ssq1.t[:, s_:s_ + 1], D * EPS, (ssq1,), rstdc)
                        S.op('dve', lambda e, s_=s_: e.scalar_tensor_tensor(
                            out=raw.t[:, s_, :], in0=raw.t[:, s_, :], scalar=rstdc.t[:, s_:s_ + 1], in1=Gb.t[:, :],
                            op0=ALU.mult, op1=ALU.mult), reads=(raw, rstdc, Gb), writes=(raw,))
                        if store:
                            S.op('dve', lambda e, s_=s_: e.tensor_tensor(out=raw.t[:, s_, :], in0=raw.t[:, s_, :],
                                                                         in1=x4.t[:, s_, :], op=ALU.add),
                                 reads=(raw, x4), writes=(raw,))
                            S.dma('pool', y[tok0 + s_ * 128:tok0 + (s_ + 1) * 128, :], raw.t[:, s_, :], anchor=raw,
                                  reads=(raw,))
                        else:
                            S.op('dve', lambda e, s_=s_: e.tensor_tensor(out=x4.t[:, s_, :], in0=x4.t[:, s_, :],
                                                                         in1=raw.t[:, s_, :], op=ALU.add),
                                 reads=(raw, x4), writes=(x4,))

                def evac_raw(pb, s_, blk):
                    S.op('act', lambda e: e.activation(out=raw.t[:, s_, blk * 512:(blk + 1) * 512], in_=pb.t[:, :],
                                                       func=AF.Copy), reads=(pb,), writes=(raw,))
                    jb = sqb.next()
                    S.op('act', lambda e: e.activation(out=jb.t[:, :], in_=pb.t[:, :], func=AF.Square,
                                                       accum_out=ssq4.t[:, s_ * 4 + blk:s_ * 4 + blk + 1]),
                         reads=(pb,), writes=(jb, ssq4))

                def ctile(i):
                    tok0 = i * T
                    S.dma('sp', act.t[:, 0:8, :], AT[:, :, tok0:tok0 + T].rearrange("h p t -> p h t"), anchor=act,
                          writes=(act,))
                    S.dma('sp', x4.t[:, :, :], xo[tok0:tok0 + T, :].rearrange("(s p) c -> p s c", p=128), anchor=x4,
                          writes=(x4,))

                    def hnorm(h):
                        ho = hob.next()
                        gh = ghb.next()
                        S.dma('act', ho.t[:, :], HO[h, :, tok0:tok0 + T], anchor=ho, writes=(ho,))
                        S.dma('act', gh.t[:, :], GH[h, :, tok0:tok0 + T], anchor=gh, writes=(gh,))
                        sq_ = sqb.next()
                        S.op('act', lambda e: e.activation(out=sq_.t[:, :], in_=ho.t[:, :], func=AF.Square),
                             reads=(ho,), writes=(sq_,))
                        pb = c2rot.next()
                        S.op('pe', lambda e: e.matmul(pb.t[:, :], lhsT=ones_bf.t[:, :], rhs=sq_.t[:, :], start=True,
                                                      stop=True), reads=(ones_bf, sq_), writes=(pb,))
                        rt = tmpc.next()
                        rsqrt(rt.t[:, :], pb.t[:, :], 128 * EPS, (pb,), rt)
                        t_ = tmpc.next()
                        S.op('dve', lambda e: e.tensor_tensor(out=t_.t[:, :], in0=ho.t[:, :], in1=rt.t[:, :],
                                                              op=ALU.mult), reads=(ho, rt), writes=(t_,))
                        S.op('dve', lambda e: e.scalar_tensor_tensor(
                            out=act.t[:, 8 + h, :], in0=t_.t[:, :], scalar=ongs.t[:, 0:1], in1=gh.t[:, :],
                            op0=ALU.mult, op1=ALU.mult), reads=(t_, ongs, gh), writes=(act,))

                    CL = int(os.environ.get("MK_CLVL", "9")) if debug else 9
                    for h in range(8):
                        hnorm(h)
                    if CL < 2:
                        return

                    def merge_chunk(wa, wh, cc, c):
                        pbA = crot.next()
                        pbH = crot.next()
                        for k in range(8):
                            S.op('pe', lambda e, k=k: e.matmul(pbA.t[:, :], lhsT=wa.t[:, k, cc * 128:(cc + 1) * 128],
                                                               rhs=act.t[:, k, :], start=(k == 0), stop=(k == 7)),
                                 reads=(wa, act), writes=(pbA,))
                        for k in range(8):
                            S.op('pe', lambda e, k=k: e.matmul(pbH.t[:, :], lhsT=wh.t[:, k, cc * 128:(cc + 1) * 128],
                                                               rhs=act.t[:, 8 + k, :], start=(k == 0), stop=(k == 7)),
                                 reads=(wh, act), writes=(pbH,))
                        ga = gab.next()
                        gb = gbb.next()
                        S.dma('act', ga.t[:, :], GG[c, :, tok0:tok0 + T], anchor=ga, writes=(ga,))
                        S.dma('act', gb.t[:, :], GG[16 + c, :, tok0:tok0 + T], anchor=gb, writes=(gb,))
                        t1 = tmpc.next()
                        t2 = tmpc.next()
                        S.op('dve', lambda e: e.tensor_tensor(out=t1.t[:, :], in0=pbA.t[:, :], in1=ga.t[:, :],
                                                              op=ALU.mult), reads=(pbA, ga), writes=(t1,))
                        S.op('dve', lambda e: e.tensor_tensor(out=t2.t[:, :], in0=pbH.t[:, :], in1=gb.t[:, :],
                                                              op=ALU.mult), reads=(pbH, gb), writes=(t2,))
                        S.op('pool', lambda e: e.tensor_tensor(out=fT.t[:, c, :], in0=t1.t[:, :], in1=t2.t[:, :],
                                                               op=ALU.add), reads=(t1, t2), writes=(fT,))

                    for blk in range(4):
                        wa = loadw(WBA[blk], 0, 8)
                        wh = loadw(WBH[blk], 0, 8)
                        for cc in range(4):
                            merge_chunk(wa, wh, cc, blk * 4 + cc)

                    if CL < 3:
                        return

                    def tm_block(w, blk, nk, src, kofs, banks, first, last):
                        for s_ in range(4):
                            pb = banks[s_]
                            for k in range(nk):
                                S.op('pe', lambda e, k=k, pb=pb, s_=s_: e.matmul(
                                    pb.t[:, :], lhsT=src.t[:, kofs + k, s_ * 128:(s_ + 1) * 128], rhs=w.t[:, k, :],
                                    start=(first and k == 0), stop=(last and k == nk - 1)), reads=(src, w),
                                     writes=(pb,))

                    for blk in range(4):
                        banks = [PB[(blk % 2) * 4 + q] for q in range(4)]
                        for half in range(2):
                            w = loadw(WO[blk], half * 8, 8)
                            tm_block(w, blk, 8, fT, half * 8, banks, half == 0, half == 1)
                        for s_ in range(4):
                            evac_raw(banks[s_], s_, blk)
                    finalize(0, True, tok0, False)
                    if CL < 4:
                        return

                    def prep_sub(s_):
                        xn = xnc[s_ % 2]
                        S.op('act', lambda e: e.activation(out=xn.t[:, :], in_=x4.t[:, s_, :], func=AF.Square,
                                                           accum_out=ssq1.t[:, s_:s_ + 1]), reads=(x4,),
                             writes=(xn, ssq1))
                        rsqrt(rstdc.t[:, s_:s_ + 1], ssq1.t[:, s_:s_ + 1], D * EPS, (ssq1,), rstdc)
                        S.op('dve', lambda e: e.tensor_scalar(out=xn.t[:, :], in0=x4.t[:, s_, :],
                                                              scalar1=rstdc.t[:, s_:s_ + 1], scalar2=None,
                                                              op0=ALU.mult), reads=(x4, rstdc, xn), writes=(xn,))
                        PSL = int(os.environ.get("MK_PS", "9")) if debug else 9
                        if PSL < 1:
                            return
                        for g in range(2):
                            pb = ctr.next()
                            pbv = pb.t[:, :].bitcast(BF16)
                            for kk in range(8):
                                k = g * 8 + kk
                                S.op('pe', lambda e, kk=kk, k=k, pbv=pbv: e.transpose(
                                    out=pbv[:, kk * 128:(kk + 1) * 128], in_=xn.t[:, k * 128:(k + 1) * 128],
                                    identity=ident.t[:, :]), reads=(xn, ident), writes=(pb,))
                            if PSL < 2:
                                continue
                            for kk in range(8):
                                k = g * 8 + kk
                                S.op('act', lambda e, kk=kk, k=k, pbv=pbv: e.activation(
                                    out=fT.t[:, k, s_ * 128:(s_ + 1) * 128], in_=pbv[:, kk * 128:(kk + 1) * 128],
                                    func=AF.Identity, scale=gsF.t[:, k:k + 1], bias=shF.t[:, k:k + 1]),
                                     reads=(pb, gsF, shF), writes=(fT,))

                    for s_ in range(4):
                        prep_sub(s_)
                    if CL < 5:
                        return

                    def ffn_chunk(w, wv, c2, cc, c, is_a):
                        pb = crot.next()
                        for k in range(KC):
                            S.op('pe', lambda e, k=k: e.matmul(pb.t[:, :], lhsT=wv[:, k, c2 * 128:(c2 + 1) * 128],
                                                               rhs=fT.t[:, k, :], start=(k == 0), stop=(k == KC - 1)),
                                 reads=(w, fT), writes=(pb,))
                        if is_a:
                            S.op('act', lambda e: e.activation(out=sa.t[:, cc, :], in_=pb.t[:, :], func=AF.Silu),
                                 reads=(pb,), writes=(sa,))
                        else:
                            S.op('dve', lambda e: e.tensor_tensor(out=act.t[:, c, :], in0=pb.t[:, :],
                                                                  in1=sa.t[:, cc, :], op=ALU.mult), reads=(pb, sa),
                                 writes=(act,))

                    for blk in range(11):
                        for hc in range(2):
                            w, wv = loadw_cols(WF1[blk], hc * 256)
                            for c2 in range(2):
                                cc = hc * 2 + c2
                                ffn_chunk(w, wv, c2, cc, blk * 4 + cc, True)
                        for hc in range(2):
                            w, wv = loadw_cols(WF1[11 + blk], hc * 256)
                            for c2 in range(2):
                                cc = hc * 2 + c2
                                ffn_chunk(w, wv, c2, cc, blk * 4 + cc, False)

                    if CL < 6:
                        return
                    for blk in range(4):
                        banks = [PB[(blk % 2) * 4 + q] for q in range(4)]
                        for kp in range(4):
                            w = loadw(WF2[blk * 4 + kp], 0, 8)
                            tm_block(w, blk, 8, act, kp * 11, banks, kp == 0, False)
                            w = loadw(WF2[blk * 4 + kp], 8, 3)
                            tm_block(w, blk, 3, act, kp * 11 + 8, banks, False, kp == 3)
                        for s_ in range(4):
                            evac_raw(banks[s_], s_, blk)
                    finalize(1, True, tok0, True)

                for i in range(n_own):
                    ctile(i)
                S.drain('sp')
                S.emit()
        return nc, S, locals()


def _rope_tables():
    pairs = 16
    pos = np.arange(L)
    row = (pos // 64).astype(np.float32)
    col = (pos % 64).astype(np.float32)
    inv = (np.float32(10000.0) ** (-np.arange(pairs, dtype=np.float32) / np.float32(pairs))).astype(np.float32)
    ang = np.concatenate([row[:, None] * inv, col[:, None] * inv], axis=-1).astype(np.float32)
    cos = np.cos(ang).astype(np.float32)
    sin = np.sin(ang).astype(np.float32)
    return np.ascontiguousarray(np.tile(cos.T, (4, 1))), np.ascontiguousarray(np.tile(sin.T, (4, 1)))


def core_inputs(inp, core, shared):
    b = core // 2
    rev = core % 2
    f = lambda a: np.ascontiguousarray(a, dtype=np.float32)
    x = inp['x'][b]
    ctx = inp['ctx'][b]
    if rev:
        x = x[::-1]
        ctx = ctx[::-1]
    key = ('w', rev)
    if key not in shared:
        w_in = inp['w_in'][0]
        lb = inp['hgrn_lb']
        cos2, sin2 = shared['rope']
        if rev:
            w_in = np.concatenate([w_in[:, :1856], w_in[:, 2880:3904], w_in[:, 1856:2880], w_in[:, 3904:]], axis=1)
            lb = lb[:, ::-1]
            cos2 = cos2[:, ::-1]
            sin2 = sin2[:, ::-1]
        shared[key] = dict(w_in=f(w_in), lbT=f(lb.transpose(3, 0, 1, 2).reshape(128, 2, 16)), cos2=f(cos2),
                           sin2=f(sin2))
    if 'common' not in shared:
        ng = inp['norm_g'][0]
        shared['common'] = dict(
            w_mod=f(inp['w_mod'][0]), b_mod=f(inp['b_mod'][0].reshape(1, -1)),
            ng_fm=f(ng.reshape(4, KC, 128).transpose(2, 0, 1)), ng_row=f(ng.reshape(1, -1)),
            qng=f(inp['mla_q_norm'][0].reshape(4, 128).T), kvng=f(inp['mla_kv_norm'][0].reshape(2, 128).T),
            w_uq=f(inp['w_uq'][0]), w_ukv=f(inp['w_ukv'][0]), ong=f(inp['hgrn_o_norm'][0].reshape(128, 1)),
            w_br_mla=f(inp['w_br_mla'][0]), w_br_hgrn=f(inp['w_br_hgrn'][0]), w_out=f(inp['w_out'][0]),
            w_ffn_in=f(inp['w_ffn_in'][0]), w_ffn_out=f(inp['w_ffn_out'][0]))
    m = dict(shared['common'])
    m.update(shared[key])
    cc = np.stack([inp['c'][b], inp['c_ctx']], axis=-1)
    m['cc'] = f(cc.reshape(KC, 128, 2).transpose(1, 0, 2))
    m['xo'] = f(x[:LH])
    m['xt'] = f(x[LH:])
    m['cx'] = f(ctx)
    return m


_NC_CACHE = {}


def kernel(**inputs):
    inp = {k: np.asarray(v) for k, v in inputs.items()}
    if 'nc' not in _NC_CACHE:
        _NC_CACHE['nc'] = build_nc(False)[0]
    nc = _NC_CACHE['nc']
    shared = {'rope': _rope_tables()}
    in_maps = [core_inputs(inp, c, shared) for c in range(8)]
    res = run_bass_kernel_spmd(nc, in_maps, core_ids=list(range(8)))
    out = np.empty((4, L, D), np.float32)
    for c in range(8):
        yv = np.asarray(res.results[c]["y"], dtype=np.float32)
        b = c // 2
        if c % 2 == 0:
            out[b, :LH] = yv
        else:
            out[b, LH:] = yv[::-1]
    return out
```

```python
import os
import numpy as np
from contextlib import ExitStack
import concourse.bass as bass
import concourse.mybir as mybir
from concourse.bass_utils import run_bass_kernel_spmd

F32 = mybir.dt.float32
BF16 = mybir.dt.bfloat16
AF = mybir.ActivationFunctionType
ALU = mybir.AluOpType

D = 2048
KC = 16
T = 512
L = 8192
LH = 4096
CTX = 256
NKEY = L + CTX
NKT = NKEY // 128
DFF = 5632
EPS = 1e-6
NCH = NKEY // 64
PHASES = os.environ.get("MK_PHASES", "WABHC")


class Buf:
    __slots__ = ('name', 'w', 'r', 'dsem', 'dcnt', 'psum')

    def __init__(self, name):
        self.name = name
        self.w = None
        self.r = {}
        self.dsem = None
        self.dcnt = 0
        self.psum = False


class TB:
    def __init__(self, t, name):
        self.t = t
        self.b = Buf(name)


def _b(x):
    return x.b if isinstance(x, TB) else x


class Sched:
    def __init__(self, nc, stack):
        self.nc = nc
        self.stack = stack
        self.engs = ['pe', 'act', 'dve', 'pool', 'sp']
        self.sem = {e: stack.enter_context(nc.semaphore('s_' + e)) for e in self.engs}
        self.cnt = {e: 0 for e in self.engs}
        self.prog = {e: [] for e in self.engs}
        self.waited = {}
        self.anchors = []
        self.nsem = 0

    def _deps(self, eng, reads, writes, skip=None):
        deps = []
        for b in reads:
            b = _b(b)
            if b.w is not None:
                deps.append(b.w)
            if b.psum:
                for t in b.r.values():
                    if t[2] != eng:
                        deps.append(t)
        for b in writes:
            b = _b(b)
            if b.w is not None and b.w[2] != eng:
                deps.append(b.w)
            for t in b.r.values():
                if t[2] != eng:
                    deps.append(t)
        for (s, v, src) in deps:
            if src == eng and eng == 'pe':
                continue
            if skip is not None and s is skip:
                continue
            key = (eng, id(s))
            if self.waited.get(key, 0) >= v:
                continue
            self.waited[key] = v
            self.prog[eng].append(lambda e, s=s, v=v: e.wait_ge(s, v))

    def _mark(self, tok, reads, writes):
        for b in reads:
            _b(b).r[id(tok[0])] = tok
        for b in writes:
            b = _b(b)
            b.w = tok
            b.r = {}

    def op(self, eng, fn, reads=(), writes=()):
        self._deps(eng, reads, writes)
        if self.cnt[eng] >= 30000:
            self.nsem += 1
            self.sem[eng] = self.stack.enter_context(self.nc.semaphore('s_%s_%d' % (eng, self.nsem)))
            self.cnt[eng] = 0
        self.cnt[eng] += 1
        s = self.sem[eng]
        tok = (s, self.cnt[eng], eng)
        self.prog[eng].append(lambda e, fn=fn, s=s: fn(e).then_inc(s, 1))
        self._mark(tok, reads, writes)

    def dma(self, eng, out, in_, anchor, reads=(), writes=()):
        a = _b(anchor)
        if a.dsem is None:
            a.dsem = self.stack.enter_context(self.nc.semaphore('d_' + a.name))
            self.anchors.append(a)
        self._deps(eng, reads, writes, skip=a.dsem)
        a.dcnt += 16
        tok = (a.dsem, a.dcnt, 'dma')
        self.prog[eng].append(lambda e, s=a.dsem, o=out, i=in_: e.dma_start(out=o, in_=i).then_inc(s, 16))
        self._mark(tok, reads, writes)

    def drain(self, eng):
        for a in self.anchors:
            if a.dcnt > 0:
                key = (eng, id(a.dsem))
                if self.waited.get(key, 0) >= a.dcnt:
                    continue
                self.waited[key] = a.dcnt
                self.prog[eng].append(lambda e, s=a.dsem, v=a.dcnt: e.wait_ge(s, v))

    def emit(self):
        nc = self.nc
        prog = self.prog
        with nc.Block() as block:
            @block.tensor
            def _(e):
                for t in prog['pe']:
                    t(e)

            @block.scalar
            def _(e):
                for t in prog['act']:
                    t(e)

            @block.vector
            def _(e):
                for t in prog['dve']:
                    t(e)

            @block.gpsimd
            def _(e):
                for t in prog['pool']:
                    t(e)

            @block.sync
            def _(e):
                for t in prog['sp']:
                    t(e)
        self.prog = {e: [] for e in self.engs}


class _View:
    def __init__(self, t, idx):
        self.tt = t
        self.idx = idx

    def __getitem__(self, key):
        return self.tt[key[0], self.idx, key[1]]


class SV(TB):
    def __init__(self, tile, idx):
        self.b = tile.b
        self.t = _View(tile.t, idx)


class Grouper:
    def __init__(self, S, rot):
        self.S = S
        self.rot = rot
        self.cur = None
        self.cnt = 0

    def slot(self):
        if self.cur is None:
            self.cur = self.rot.next()
            self.cnt = 0
        sv = SV(self.cur, self.cnt)
        self.cnt += 1
        return sv

    def flush(self, dst, ncol, src=None):
        tile = self.cur
        if src is None:
            src = tile.t[:, 0:self.cnt, 0:ncol] if self.cnt > 1 else tile.t[:, 0, 0:ncol]
        self.S.dma('pool', dst, src, anchor=tile, reads=(tile,))
        self.cur = None


class Rot:
    def __init__(self, items):
        self.items = items
        self.i = 0

    def next(self):
        x = self.items[self.i % len(self.items)]
        self.i += 1
        return x


def build_nc(debug=False):
    nc = bass.Bass("TRN2", target_bir_lowering=False)

    def din(name, shape, dt=F32):
        return nc.dram_tensor(name, shape, dt, kind="ExternalInput").ap()

    dbg_out = set(os.environ.get("MK_DBG", "").split(",")) if debug else set()

    def dscr(name, shape, dt=BF16):
        ext = debug and (name in dbg_out or "ALL" in dbg_out)
        return nc.dram_tensor(name, shape, dt, kind=("ExternalOutput" if ext else "Internal")).ap()

    xo = din("xo", [LH, D])
    xt = din("xt", [LH, D])
    cx = din("cx", [CTX, D])
    cc = din("cc", [128, KC, 2])
    w_mod = din("w_mod", [D, 6 * D])
    b_mod = din("b_mod", [1, 6 * D])
    ng_fm = din("ng_fm", [128, 4, KC])
    ng_row = din("ng_row", [1, 4 * D])
    w_in = din("w_in", [D, 10048])
    qng = din("qng", [128, 4])
    kvng = din("kvng", [128, 2])
    w_uq = din("w_uq", [512, 1536])
    w_ukv = din("w_ukv", [256, 2048])
    lbT = din("lbT", [128, 2, 16])
    ong = din("ong", [128, 1])
    w_br_mla = din("w_br_mla", [1024, D])
    w_br_hgrn = din("w_br_hgrn", [1024, D])
    w_out = din("w_out", [D, D])
    w_ffn_in = din("w_ffn_in", [D, 2 * DFF])
    w_ffn_out = din("w_ffn_out", [DFF, D])
    cos2 = din("cos2", [128, L])
    sin2 = din("sin2", [128, L])
    y = nc.dram_tensor("y", [LH, D], F32, kind="ExternalOutput").ap()

    W1 = dscr("W1", [20, 128, KC * 512])
    WBA = dscr("WBA", [4, 128, 8 * 512])
    WBH = dscr("WBH", [4, 128, 8 * 512])
    WO = dscr("WO", [4, 128, KC * 512])
    WF1 = dscr("WF1", [22, 128, KC * 512])
    WF2 = dscr("WF2", [16, 128, 11 * 512])
    KTn = dscr("KTn", [8, 128, NKEY])
    KTr = dscr("KTr", [128, NKEY])
    VH = dscr("VH", [8, 128, NKT, 128])
    QTn = dscr("QTn", [8, 128, LH])
    QTr = dscr("QTr", [4, 128, LH])
    KD = dscr("KD", [2, 8, 128, NKEY])
    QD = dscr("QD", [2, 8, 128, LH])
    HV = dscr("HV", [NKEY, 1024])
    GH = dscr("GH", [8, 128, LH])
    GG = dscr("GG", [32, 128, LH])
    AT = dscr("AT", [8, 128, LH])
    O1 = dscr("O1", [8, 128, LH], F32)
    HO = dscr("HO", [8, 128, LH])
    GAB = dscr("GAB", [2, 128, D], F32)

    with ExitStack() as top:
        S = Sched(nc, top)

        def sbt(st, name, shape, dt):
            return TB(st.enter_context(nc.sbuf_tensor(name, shape, dt)), name)

        def pst(st, name, shape, dt):
            tb = TB(st.enter_context(nc.psum_tensor(name, shape, dt)), name)
            tb.b.psum = True
            return tb

        def rsqrt(out_ap, in_ap, const, reads, outb):
            S.op('act', lambda e: e.activation(out=out_ap, in_=in_ap, func=AF.Sqrt, bias=float(const), scale=1.0),
                 reads=reads, writes=(outb,))
            S.op('dve', lambda e: e.reciprocal(out=out_ap, in_=out_ap), reads=(outb,), writes=(outb,))

        ident = sbt(top, "ident", [128, 128], BF16)
        ones_bf = sbt(top, "ones_bf", [128, 128], BF16)
        ones_f = sbt(top, "ones_f", [1, 128], F32)
        mskF = sbt(top, "mskF", [128, 512], F32)
        mask1 = sbt(top, "mask1", [128, 128], F32)
        mask2 = sbt(top, "mask2", [128, 128], F32)
        modT = sbt(top, "modT", [128, 256], F32)
        gsA = sbt(top, "gsA", [128, KC], F32)
        shA = sbt(top, "shA", [128, KC], F32)
        gsC = sbt(top, "gsC", [128, KC], F32)
        shC = sbt(top, "shC", [128, KC], F32)
        gsF = sbt(top, "gsF", [128, KC], F32)
        shF = sbt(top, "shF", [128, KC], F32)
        ngf = sbt(top, "ngf", [128, 4, KC], F32)
        oml = sbt(top, "oml", [128, 16], F32)
        lbt = sbt(top, "lbt", [128, 2, 16], F32)
        ongs = sbt(top, "ongs", [128, 1], F32)
        stAH = ExitStack()
        dec = sbt(stAH, "dec", [128, 2, 8, NCH], F32)
        PB = [pst(top, "pb%d" % i, [128, 512], F32) for i in range(8)]

        S.op('pool', lambda e: e.memset(ident.t[:, :], 1.0), writes=(ident,))
        S.op('pool', lambda e: e.affine_select(out=ident.t[:, :], in_=ident.t[:, :], pattern=[[-1, 128]],
                                               compare_op=ALU.is_equal, fill=0.0, base=0, channel_multiplier=1),
             reads=(ident,), writes=(ident,))
        S.op('pool', lambda e: e.memset(ones_bf.t[:, :], 1.0), writes=(ones_bf,))
        S.op('pool', lambda e: e.memset(ones_f.t[:, :], 1.0), writes=(ones_f,))
        S.op('pool', lambda e: e.memset(mskF.t[:, :], 1.0), writes=(mskF,))
        mv = mskF.t[:, :].rearrange("p (c t) -> p c t", t=64)
        S.op('pool', lambda e: e.memset(mv[:, :, 0:1], 0.0), reads=(mskF,), writes=(mskF,))
        for (mk, sign) in ((mask1, 1), (mask2, -1)):
            S.op('pool', lambda e, mk=mk: e.memset(mk.t[:, :], 1.0), writes=(mk,))
            S.op('pool', lambda e, mk=mk, sign=sign: e.affine_select(
                out=mk.t[:, :], in_=mk.t[:, :], pattern=[[sign, 128]], compare_op=ALU.is_ge, fill=0.0,
                base=0, channel_multiplier=-sign), reads=(mk,), writes=(mk,))
            S.op('pool', lambda e, mk=mk: e.memset(mk.t[0:64, 64:128], 0.0), reads=(mk,), writes=(mk,))
            S.op('pool', lambda e, mk=mk: e.memset(mk.t[64:128, 0:64], 0.0), reads=(mk,), writes=(mk,))

        if 'W' in PHASES:
            with ExitStack() as st:
                stf = [sbt(st, "stf%d" % i, [128, KC, 512], F32) for i in range(2)]
                stb = [sbt(st, "stb%d" % i, [128, KC, 512], BF16) for i in range(2)]
                rowp = sbt(st, "rowp", [1, 512], F32)
                rowg = sbt(st, "rowg", [1, 512], F32)
                bmp = [sbt(st, "bmp%d" % i, [1, 512], F32) for i in range(2)]
                gpc = [sbt(st, "gpc%d" % i, [1, 512], F32) for i in range(2)]
                cct = sbt(st, "cct", [128, KC, 2], F32)
                scT = sbt(st, "scT", [128, KC, 2], BF16)
                one2 = sbt(st, "one2", [1, 2], F32)
                tmpm = sbt(st, "tmpm", [128, KC], F32)
                gtm = [sbt(st, "gtm%d" % i, [128, 512], F32) for i in range(2)]
                uidx = [0]
                cast_engs = ['act', 'dve']

                def cast(eng, o, i, rd, wr, scale=None):
                    if eng == 'act':
                        if scale is None:
                            S.op('act', lambda e: e.activation(out=o, in_=i, func=AF.Copy), reads=rd, writes=wr)
                        else:
                            S.op('act', lambda e: e.activation(out=o, in_=i, func=AF.Copy, scale=scale), reads=rd,
                                 writes=wr)
                    else:
                        if scale is None:
                            S.op(eng, lambda e: e.tensor_copy(out=o, in_=i), reads=rd, writes=wr)
                        else:
                            S.op(eng, lambda e: e.tensor_scalar(out=o, in0=i, scalar1=scale, scalar2=None,
                                                                 op0=ALU.mult), reads=rd, writes=wr)

                def precast(src, r0, nk, c0, ncol, dst):
                    u = uidx[0]
                    uidx[0] += 1
                    f = stf[u % 2]
                    b = stb[u % 2]
                    srcv = src[r0:r0 + nk * 128, c0:c0 + ncol].rearrange("(k p) c -> p k c", p=128)
                    S.dma('sp', f.t[:, 0:nk, 0:ncol], srcv, anchor=f, writes=(f,))
                    return u, f, b

                def simple_unit(src, r0, nk, c0, dst):
                    u, f, b = precast(src, r0, nk, c0, 512, dst)
                    h = nk // 2
                    cast('act', b.t[:, 0:h, :], f.t[:, 0:h, :], (f,), (b,))
                    cast('dve', b.t[:, h:nk, :], f.t[:, h:nk, :], (f,), (b,))
                    S.dma('pool', dst.rearrange("p (k c) -> p k c", c=512), b.t[:, 0:nk, :], anchor=b, reads=(b,))

                S.dma('sp', cct.t[:, :, :], cc[:, :, :], anchor=cct, writes=(cct,))
                S.op('act', lambda e: e.activation(out=scT.t[:, :, :], in_=cct.t[:, :, :], func=AF.Silu),
                     reads=(cct,), writes=(scT,))
                S.op('pool', lambda e: e.memset(one2.t[:, :], 1.0), writes=(one2,))
                S.dma('sp', ngf.t[:, :, :], ng_fm[:, :, :], anchor=ngf, writes=(ngf,))
                S.dma('sp', lbt.t[:, :, :], lbT[:, :, :], anchor=lbt, writes=(lbt,))
                S.dma('sp', ongs.t[:, :], ong[:, :], anchor=ongs, writes=(ongs,))
                modps = PB[7]
                mrot = Rot([PB[0], PB[1], PB[2], PB[3]])
                brot = Rot([PB[4], PB[5]])
                for blk in range(24):
                    which = blk // 4
                    u, f, b = precast(w_mod, 0, KC, blk * 512, 512, None)
                    cast('act', b.t[:, 0:8, :], f.t[:, 0:8, :], (f,), (b,))
                    cast('dve', b.t[:, 8:16, :], f.t[:, 8:16, :], (f,), (b,))
                    bp = bmp[blk % 2]
                    S.dma('sp', bp.t[:, :], b_mod[0:1, blk * 512:(blk + 1) * 512], anchor=bp, writes=(bp,))
                    for m in range(2):
                        if m == 1 and which >= 2:
                            continue
                        pb = mrot.next()
                        for k in range(KC):
                            S.op('pe', lambda e, pb=pb, k=k, m=m, b=b: e.matmul(
                                pb.t[0:1, :], lhsT=scT.t[:, k, m:m + 1], rhs=b.t[:, k, :], start=(k == 0),
                                stop=(k == KC - 1)), reads=(scT, b), writes=(pb,))
                        S.op('dve', lambda e, pb=pb, bp=bp: e.tensor_tensor(out=rowp.t[:, :], in0=pb.t[0:1, :],
                                                                            in1=bp.t[:, :], op=ALU.add),
                             reads=(pb, bp), writes=(rowp,))
                        if which in (0, 1, 3, 4):
                            for j in range(4):
                                k = (blk % 4) * 4 + j
                                col = ((m * 6 + which) * KC + k)
                                S.op('pe', lambda e, j=j, col=col: e.matmul(
                                    modps.t[:, 2 * col:2 * col + 2], lhsT=rowp.t[0:1, j * 128:(j + 1) * 128],
                                    rhs=one2.t[0:1, 0:2], start=True, stop=True), reads=(rowp, one2),
                                     writes=(modps,))
                        elif m == 0:
                            gi = 1 if which == 2 else 3
                            gp = gpc[blk % 2]
                            cb = (blk % 4) * 512
                            S.dma('sp', gp.t[:, :], ng_row[0:1, gi * D + cb: gi * D + cb + 512], anchor=gp,
                                  writes=(gp,))
                            S.op('dve', lambda e, gp=gp: e.scalar_tensor_tensor(
                                out=rowg.t[:, :], in0=rowp.t[:, :], scalar=float(np.sqrt(D)), in1=gp.t[:, :],
                                op0=ALU.mult, op1=ALU.mult), reads=(rowp, gp), writes=(rowg,))
                            pbb = brot.next()
                            S.op('pe', lambda e, pbb=pbb: e.matmul(pbb.t[:, :], lhsT=ones_f.t[0:1, :],
                                                                   rhs=rowg.t[0:1, :], start=True, stop=True),
                                 reads=(ones_f, rowg), writes=(pbb,))
                            gt_ = gtm[blk % 2]
                            S.op('act', lambda e, pbb=pbb, gt_=gt_: e.activation(
                                out=gt_.t[:, :], in_=pbb.t[:, :], func=AF.Copy), reads=(pbb,), writes=(gt_,))
                            S.dma('pool', GAB[0 if which == 2 else 1, :, cb:cb + 512], gt_.t[:, :], anchor=gt_,
                                  reads=(gt_,))
                for (c0, c1) in ((0, 32), (48, 80), (96, 128)):
                    S.op('dve', lambda e, c0=c0, c1=c1: e.tensor_copy(out=modT.t[:, c0:c1],
                                                                      in_=modps.t[:, 2 * c0:2 * c1:2]),
                         reads=(modps, modT), writes=(modT,))

                def mcol(m, which):
                    c0 = (m * 6 + which) * KC
                    return modT.t[:, c0:c0 + KC]

                sq = float(np.sqrt(D))
                for (gs_, sh_, m, wsc, wsh, gi) in ((gsA, shA, 0, 1, 0, 0), (gsC, shC, 1, 1, 0, 0),
                                                    (gsF, shF, 0, 4, 3, 2)):
                    S.op('dve', lambda e, m=m, wsc=wsc: e.tensor_scalar(
                        out=tmpm.t[:, :], in0=mcol(m, wsc), scalar1=1.0, scalar2=sq, op0=ALU.add, op1=ALU.mult),
                         reads=(modT,), writes=(tmpm,))
                    S.op('dve', lambda e, gs_=gs_, gi=gi: e.tensor_tensor(out=gs_.t[:, :], in0=tmpm.t[:, :],
                                                                          in1=ngf.t[:, gi, :], op=ALU.mult),
                         reads=(tmpm, ngf), writes=(gs_,))
                    S.op('dve', lambda e, sh_=sh_, m=m, wsh=wsh: e.tensor_copy(out=sh_.t[:, :], in_=mcol(m, wsh)),
                         reads=(modT,), writes=(sh_,))
                S.op('dve', lambda e: e.tensor_tensor(out=oml.t[:, :], in0=lbt.t[:, 0, :], in1=lbt.t[:, 1, :],
                                                      op=ALU.subtract), reads=(lbt,), writes=(oml,))
                S.op('act', lambda e: e.activation(out=oml.t[:, :], in_=oml.t[:, :], func=AF.Sigmoid, scale=-1.0),
                     reads=(oml,), writes=(oml,))
                S.op('dve', lambda e: e.tensor_scalar(out=ongs.t[:, :], in0=ongs.t[:, :],
                                                      scalar1=float(np.sqrt(128.0)), scalar2=None, op0=ALU.mult),
                     reads=(ongs,), writes=(ongs,))

                simple_unit(w_in, 0, KC, 0, W1[0])
                u, f, b = precast(w_in, 0, KC, 512, 320, None)
                cast('act', b.t[:, :, 0:256], f.t[:, :, 0:256], (f,), (b,))
                cast('dve', b.t[:, :, 256:320], f.t[:, :, 256:320], (f,), (b,))
                cast('dve', b.t[:, :, 320:384], f.t[:, :, 256:320], (f,), (b,))
                for o in (384, 448):
                    cast('dve', b.t[:, :, o:o + 32], f.t[:, :, 288:320], (f,), (b,), scale=-1.0)
                    cast('dve', b.t[:, :, o + 32:o + 64], f.t[:, :, 256:288], (f,), (b,))
                S.dma('pool', W1[1].rearrange("p (k c) -> p k c", c=512), b.t[:, :, :], anchor=b, reads=(b,))
                for blk in range(2, 20):
                    simple_unit(w_in, 0, KC, 832 + (blk - 2) * 512, W1[blk])
                for blk in range(4):
                    simple_unit(w_br_mla, 0, 8, blk * 512, WBA[blk])
                    simple_unit(w_br_hgrn, 0, 8, blk * 512, WBH[blk])
                    simple_unit(w_out, 0, KC, blk * 512, WO[blk])
                for blk in range(22):
                    simple_unit(w_ffn_in, 0, KC, blk * 512, WF1[blk])
                for blk in range(4):
                    for kp in range(4):
                        simple_unit(w_ffn_out, kp * 11 * 128, 11, blk * 512, WF2[blk * 4 + kp])
                S.emit()

        if 'A' in PHASES:
            with ExitStack() as st:
                wq = sbt(st, "wq", [128, 4, 2048], BF16)
                wkv = sbt(st, "wkv", [128, 2, 8, 256], BF16)
                with ExitStack() as st2:
                    wqf = sbt(st2, "wqf", [128, 4, 8, 192], F32)
                    wkvf = sbt(st2, "wkvf", [128, 2, 2048], F32)
                    qg = sbt(st2, "qg", [128, 4], F32)
                    kvg = sbt(st2, "kvg", [128, 2], F32)
                    S.dma('sp', wqf.t[:, :, :, :], w_uq.rearrange("(k p) (h c) -> p k h c", p=128, c=192),
                          anchor=wqf, writes=(wqf,))
                    S.dma('sp', wkvf.t[:, :, :], w_ukv.rearrange("(k p) c -> p k c", p=128), anchor=wkvf,
                          writes=(wkvf,))
                    S.dma('sp', qg.t[:, :], qng[:, :], anchor=qg, writes=(qg,))
                    S.dma('sp', kvg.t[:, :], kvng[:, :], anchor=kvg, writes=(kvg,))
                    S.op('dve', lambda e: e.tensor_scalar(out=qg.t[:, :], in0=qg.t[:, :],
                                                          scalar1=float(np.sqrt(512.0)), scalar2=None,
                                                          op0=ALU.mult), reads=(qg,), writes=(qg,))
                    S.op('dve', lambda e: e.tensor_scalar(out=kvg.t[:, :], in0=kvg.t[:, :], scalar1=16.0,
                                                          scalar2=None, op0=ALU.mult), reads=(kvg,), writes=(kvg,))
                    wq4 = wq.t[:, :, :].rearrange("p k (j c) -> p k j c", c=128)
                    for k in range(4):
                        g = qg.t[:, k:k + 1]
                        S.op('dve', lambda e, k=k, g=g: e.tensor_scalar(
                            out=wq4[:, k, 0:8, :], in0=wqf.t[:, k, :, 0:128], scalar1=g, scalar2=None,
                            op0=ALU.mult), reads=(wqf, qg), writes=(wq,))
                        ropev = wq.t[:, k, 1024:1536].rearrange("p (h c) -> p h c", c=64)
                        S.op('dve', lambda e, k=k, g=g, ropev=ropev: e.tensor_scalar(
                            out=ropev, in0=wqf.t[:, k, :, 128:192], scalar1=g, scalar2=None, op0=ALU.mult),
                             reads=(wqf, qg), writes=(wq,))
                        rotv = wq.t[:, k, 1536:2048].rearrange("p (h c) -> p h c", c=64)
                        S.op('dve', lambda e, k=k, g=g, rotv=rotv: e.tensor_scalar(
                            out=rotv[:, :, 0:32], in0=wqf.t[:, k, :, 160:192], scalar1=g, scalar2=-1.0,
                            op0=ALU.mult, op1=ALU.mult), reads=(wqf, qg), writes=(wq,))
                        S.op('dve', lambda e, k=k, g=g, rotv=rotv: e.tensor_scalar(
                            out=rotv[:, :, 32:64], in0=wqf.t[:, k, :, 128:160], scalar1=g, scalar2=None,
                            op0=ALU.mult), reads=(wqf, qg), writes=(wq,))
                    for k in range(2):
                        g = kvg.t[:, k:k + 1]
                        S.op('dve', lambda e, k=k, g=g: e.tensor_scalar(
                            out=wkv.t[:, k, :, :], in0=wkvf.t[:, k, :].rearrange("p (h c) -> p h c", c=256),
                            scalar1=g, scalar2=None, op0=ALU.mult), reads=(wkvf, kvg), writes=(wkv,))
                    S.emit()
                wsl = [sbt(st, "wsl%d" % i, [128, KC, 512], BF16) for i in range(3)]
                wrot = Rot(wsl)
                hTs = [sbt(st, "hT%d" % i, [128, KC, T], BF16) for i in range(2)]
                xts = [sbt(st, "xt%d" % i, [128, D], F32) for i in range(2)]
                xnb = [sbt(st, "xn%d" % i, [128, D], BF16) for i in range(2)]
                ssq = sbt(st, "ssq", [128, 4], F32)
                rstd = sbt(st, "rstd", [128, 4], F32)
                cq = sbt(st, "cq", [128, 4, T], BF16)
                sqq = sbt(st, "sqq", [128, 4, T], BF16)
                ckv = sbt(st, "ckv", [128, 2, T], BF16)
                sqk = sbt(st, "sqk", [128, 2, T], BF16)
                rq = sbt(st, "rq", [128, T], F32)
                rkv = sbt(st, "rkv", [128, T], F32)
                rkvt = sbt(st, "rkvt", [128, 4], F32)
                crq = sbt(st, "crq", [128, T], F32)
                srq = sbt(st, "srq", [128, T], F32)
                cts = [sbt(st, "ct%d" % i, [128, T], F32) for i in range(1)]
                sts = [sbt(st, "st%d" % i, [128, T], F32) for i in range(1)]
                qs = sbt(st, "qs", [128, 8, T], BF16)
                stg4 = Rot([sbt(st, "stg%d" % i, [128, 4, T], BF16) for i in range(4)])
                tmp = Rot([sbt(st, "tmp%d" % i, [128, T], F32) for i in range(8)])
                trb = [PB[6], PB[7]]
                trrot = Rot(trb)
                arot = Rot([PB[0], PB[1], PB[2], PB[3]])
                srot = Rot([PB[4], PB[5]])

                tiles = [('ctx', cx, 0, CTX, 0, None)]
                for i in range(8):
                    tiles.append(('oth', xt, i * T, T, CTX + LH + i * T, LH + i * T))
                for i in range(8):
                    tiles.append(('own', xo, i * T, T, CTX + i * T, i * T))
                if debug and os.environ.get("MK_NT"):
                    nt_ = int(os.environ["MK_NT"])
                    no_ = int(os.environ.get("MK_NO", "1"))
                    tiles = tiles[:1] + tiles[1:1 + no_] + tiles[9:9 + nt_]

                def load_sub(ti, s):
                    kind, src, r0, n, kr0, tp = tiles[ti]
                    xtile = xts[s % 2]
                    S.dma('act', xtile.t[:, :], src[r0 + s * 128:r0 + (s + 1) * 128, :], anchor=xtile,
                          writes=(xtile,))

                def prep_sub(ti, hT, s):
                    kind, src, r0, n, kr0, tp = tiles[ti]
                    gs_ = gsC if kind == 'ctx' else gsA
                    sh_ = shC if kind == 'ctx' else shA
                    nsub = n // 128
                    if True:
                        xtile = xts[s % 2]
                        xn = xnb[s % 2]
                        S.op('act', lambda e, xtile=xtile, xn=xn, s=s: e.activation(
                            out=xn.t[:, :], in_=xtile.t[:, :], func=AF.Square, accum_out=ssq.t[:, s:s + 1]),
                             reads=(xtile,), writes=(xn, ssq))
                        rsqrt(rstd.t[:, s:s + 1], ssq.t[:, s:s + 1], D * EPS, (ssq,), rstd)
                        S.op('dve', lambda e, xtile=xtile, xn=xn, s=s: e.tensor_scalar(
                            out=xn.t[:, :], in0=xtile.t[:, :], scalar1=rstd.t[:, s:s + 1], scalar2=None,
                            op0=ALU.mult), reads=(xtile, rstd, xn), writes=(xn,))
                        if s + 2 < nsub:
                            load_sub(ti, s + 2)
                        elif ti + 1 < len(tiles):
                            s2 = s + 2 - nsub
                            if s2 < tiles[ti + 1][3] // 128:
                                load_sub(ti + 1, s2)
                        for g in range(2):
                            pb = trrot.next()
                            pbv = pb.t[:, :].bitcast(BF16)
                            for kk in range(8):
                                k = g * 8 + kk
                                S.op('pe', lambda e, pbv=pbv, kk=kk, k=k, xn=xn: e.transpose(
                                    out=pbv[:, kk * 128:(kk + 1) * 128], in_=xn.t[:, k * 128:(k + 1) * 128],
                                    identity=ident.t[:, :]), reads=(xn, ident), writes=(pb,))
                            for kk in range(8):
                                k = g * 8 + kk
                                S.op('act', lambda e, pbv=pbv, kk=kk, k=k, s=s, hT=hT, gs_=gs_, sh_=sh_: e.activation(
                                    out=hT.t[:, k, s * 128:(s + 1) * 128], in_=pbv[:, kk * 128:(kk + 1) * 128],
                                    func=AF.Identity, scale=gs_.t[:, k:k + 1], bias=sh_.t[:, k:k + 1]),
                                     reads=(pb, gs_, sh_), writes=(hT,))

                def load_w(blk):
                    w = wrot.next()
                    S.dma('sp', w.t[:, :, :], W1[blk].rearrange("p (k c) -> p k c", c=512), anchor=w, writes=(w,))
                    return w

                def mm_fm(w, c, hT, n, pb):
                    for k in range(KC):
                        S.op('pe', lambda e, k=k: e.matmul(pb.t[:, 0:n], lhsT=w.t[:, k, c * 128:(c + 1) * 128],
                                                           rhs=hT.t[:, k, 0:n], start=(k == 0), stop=(k == KC - 1)),
                             reads=(w, hT), writes=(pb,))

                def store(sg, dst, n):
                    S.dma('pool', dst, sg.t[:, 0:n], anchor=sg, reads=(sg,))

                def act_evac(func, pb, n, out, outb, scale=None):
                    if scale is None:
                        S.op('act', lambda e: e.activation(out=out, in_=pb.t[:, 0:n], func=func), reads=(pb,),
                             writes=(outb,))
                    else:
                        S.op('act', lambda e: e.activation(out=out, in_=pb.t[:, 0:n], func=func, scale=scale),
                             reads=(pb,), writes=(outb,))

                def do_tile(ti, hT, hook):
                    kind, src, r0, n, kr0, tp = tiles[ti]
                    nsub = n // 128
                    own = kind == 'own'
                    ct = cts[0]
                    stt = sts[0]
                    if kind != 'ctx':
                        S.dma('sp', ct.t[:, 0:n], cos2[:, tp:tp + n], anchor=ct, writes=(ct,))
                        S.dma('sp', stt.t[:, 0:n], sin2[:, tp:tp + n], anchor=stt, writes=(stt,))
                    w = load_w(1)
                    for i in range(2):
                        pb = arot.next()
                        mm_fm(w, i, hT, n, pb)
                        act_evac(AF.Copy, pb, n, ckv.t[:, i, 0:n], ckv)
                        act_evac(AF.Square, pb, n, sqk.t[:, i, 0:n], sqk)
                    pbA = arot.next()
                    mm_fm(w, 2, hT, n, pbA)
                    if kind == 'ctx':
                        gk = Grouper(S, stg4)
                        sg = gk.slot()
                        act_evac(AF.Copy, pbA, n, sg.t[:, 0:n], sg)
                        gk.flush(KTr[:, kr0:kr0 + n], n)
                    else:
                        pbB = arot.next()
                        mm_fm(w, 3, hT, n, pbB)
                        t1 = tmp.next()
                        t2 = tmp.next()
                        S.op('dve', lambda e, t1=t1, pbA=pbA: e.tensor_tensor(out=t1.t[:, 0:n], in0=pbA.t[:, 0:n], in1=ct.t[:, 0:n],
                                                              op=ALU.mult), reads=(pbA, ct), writes=(t1,))
                        S.op('dve', lambda e, t2=t2, pbB=pbB: e.tensor_tensor(out=t2.t[:, 0:n], in0=pbB.t[:, 0:n], in1=stt.t[:, 0:n],
                                                              op=ALU.mult), reads=(pbB, stt), writes=(t2,))
                        gk = Grouper(S, stg4)
                        sg = gk.slot()
                        S.op('pool', lambda e, sg=sg, t1=t1, t2=t2: e.tensor_tensor(out=sg.t[:, 0:n], in0=t1.t[:, 0:n], in1=t2.t[:, 0:n],
                                                               op=ALU.add), reads=(t1, t2), writes=(sg,))
                        gk.flush(KTr[:, kr0:kr0 + n], n)
                    pbs = srot.next()
                    for i in range(2):
                        S.op('pe', lambda e, i=i, pbs=pbs: e.matmul(pbs.t[:, 0:n], lhsT=ones_bf.t[:, :], rhs=sqk.t[:, i, 0:n],
                                                           start=(i == 0), stop=(i == 1)), reads=(ones_bf, sqk),
                             writes=(pbs,))
                    rsqrt(rkv.t[:, 0:n], pbs.t[:, 0:n], 256 * EPS, (pbs,), rkv)
                    pbt = srot.next()
                    for s in range(nsub):
                        for i in range(2):
                            S.op('pe', lambda e, i=i, s=s, pbt=pbt: e.matmul(
                                pbt.t[:, 2 * s:2 * s + 2], lhsT=sqk.t[:, i, s * 128:(s + 1) * 128],
                                rhs=ones_bf.t[:, 0:2], start=(i == 0), stop=(i == 1)), reads=(ones_bf, sqk),
                                 writes=(pbt,))
                    rsqrt(rkvt.t[:, 0:nsub], pbt.t[:, 0:2 * nsub:2], 256 * EPS, (pbt,), rkvt)
                    gk = Grouper(S, stg4)
                    for h in range(8):
                        pb = arot.next()
                        for k in range(2):
                            S.op('pe', lambda e, k=k, h=h, pb=pb: e.matmul(
                                pb.t[:, 0:n], lhsT=wkv.t[:, k, h, 0:128], rhs=ckv.t[:, k, 0:n], start=(k == 0),
                                stop=(k == 1)), reads=(wkv, ckv), writes=(pb,))
                        sg = gk.slot()
                        S.op('dve', lambda e, pb=pb, sg=sg: e.tensor_tensor(out=sg.t[:, 0:n], in0=pb.t[:, 0:n],
                                                                            in1=rkv.t[:, 0:n], op=ALU.mult),
                             reads=(pb, rkv), writes=(sg,))
                        if h % 4 == 3:
                            gk.flush(KTn[h - 3:h + 1, :, kr0:kr0 + n].rearrange("h p t -> p h t"), n)
                    for s in range(nsub):
                        kt = (kr0 + s * 128) // 128
                        gv = Grouper(S, stg4)
                        for j in range(2):
                            pb = arot.next()
                            for k in range(2):
                                S.op('pe', lambda e, k=k, j=j, s=s, pb=pb: e.matmul(
                                    pb.t[:, :].rearrange("p (h c) -> p h c", c=128),
                                    lhsT=ckv.t[:, k, s * 128:(s + 1) * 128], rhs=wkv.t[:, k, 4 * j:4 * j + 4, 128:256],
                                    start=(k == 0), stop=(k == 1)), reads=(wkv, ckv), writes=(pb,))
                            sg = gv.slot()
                            S.op('act', lambda e, pb=pb, sg=sg, s=s: e.activation(
                                out=sg.t[:, 0:T], in_=pb.t[:, :], func=AF.Copy, scale=rkvt.t[:, s:s + 1]),
                                 reads=(pb, rkvt), writes=(sg,))
                        gv.flush(VH[:, :, kt, :].rearrange("h p c -> p h c"), T,
                                 src=gv.cur.t[:, 0:2, :].rearrange("p j (h c) -> p (j h) c", c=128))
                    hook('b1')
                    if own:
                        w = load_w(0)
                        for i in range(4):
                            pb = arot.next()
                            mm_fm(w, i, hT, n, pb)
                            act_evac(AF.Copy, pb, n, cq.t[:, i, 0:n], cq)
                            act_evac(AF.Square, pb, n, sqq.t[:, i, 0:n], sqq)
                        pbs = srot.next()
                        for i in range(4):
                            S.op('pe', lambda e, i=i, pbs=pbs: e.matmul(pbs.t[:, 0:n], lhsT=ones_bf.t[:, :],
                                                               rhs=sqq.t[:, i, 0:n], start=(i == 0), stop=(i == 3)),
                                 reads=(ones_bf, sqq), writes=(pbs,))
                        rsqrt(rq.t[:, 0:n], pbs.t[:, 0:n], 512 * EPS, (pbs,), rq)
                        S.op('dve', lambda e: e.tensor_tensor(out=crq.t[:, 0:n], in0=ct.t[:, 0:n], in1=rq.t[:, 0:n],
                                                              op=ALU.mult), reads=(ct, rq), writes=(crq,))
                        S.op('dve', lambda e: e.tensor_tensor(out=srq.t[:, 0:n], in0=stt.t[:, 0:n], in1=rq.t[:, 0:n],
                                                              op=ALU.mult), reads=(stt, rq), writes=(srq,))

                        def qmm(j, pb):
                            for k in range(4):
                                S.op('pe', lambda e, k=k: e.matmul(pb.t[:, 0:n], lhsT=wq.t[:, k, j * 128:(j + 1) * 128],
                                                                   rhs=cq.t[:, k, 0:n], start=(k == 0), stop=(k == 3)),
                                     reads=(wq, cq), writes=(pb,))

                        gq = Grouper(S, stg4)
                        for j in range(8):
                            pb = arot.next()
                            qmm(j, pb)
                            sg = gq.slot()
                            S.op('dve', lambda e, pb=pb, sg=sg: e.tensor_tensor(out=sg.t[:, 0:n], in0=pb.t[:, 0:n],
                                                                                in1=rq.t[:, 0:n], op=ALU.mult),
                                 reads=(pb, rq), writes=(sg,))
                            if j % 4 == 3:
                                gq.flush(QTn[j - 3:j + 1, :, tp:tp + n].rearrange("h p t -> p h t"), n)
                        gq = Grouper(S, stg4)
                        for j in range(4):
                            pbA = arot.next()
                            qmm(8 + j, pbA)
                            pbB = arot.next()
                            qmm(12 + j, pbB)
                            t1 = tmp.next()
                            t2 = tmp.next()
                            S.op('dve', lambda e, pbA=pbA, t1=t1: e.tensor_tensor(
                                out=t1.t[:, 0:n], in0=pbA.t[:, 0:n], in1=crq.t[:, 0:n], op=ALU.mult),
                                 reads=(pbA, crq), writes=(t1,))
                            S.op('dve', lambda e, pbB=pbB, t2=t2: e.tensor_tensor(
                                out=t2.t[:, 0:n], in0=pbB.t[:, 0:n], in1=srq.t[:, 0:n], op=ALU.mult),
                                 reads=(pbB, srq), writes=(t2,))
                            sg = gq.slot()
                            S.op('pool', lambda e, sg=sg, t1=t1, t2=t2: e.tensor_tensor(
                                out=sg.t[:, 0:n], in0=t1.t[:, 0:n], in1=t2.t[:, 0:n], op=ALU.add), reads=(t1, t2),
                                 writes=(sg,))
                        gq.flush(QTr[0:4, :, tp:tp + n].rearrange("h p t -> p h t"), n)
                        for blk in (2, 3):
                            w = load_w(blk)
                            for c in range(4):
                                h = (blk - 2) * 4 + c
                                pb = arot.next()
                                mm_fm(w, c, hT, n, pb)
                                act_evac(AF.Silu, pb, n, qs.t[:, h, 0:n], qs)
                    fblks = (4, 5, 6, 7) if kind != 'oth' else (6, 7)
                    for blk in fblks:
                        w = load_w(blk)
                        r = (blk - 4) // 2
                        gkd = Grouper(S, stg4)
                        gqd = Grouper(S, stg4)
                        for c in range(4):
                            h = (blk % 2) * 4 + c
                            pb = arot.next()
                            mm_fm(w, c, hT, n, pb)
                            sgm = tmp.next()
                            act_evac(AF.Sigmoid, pb, n, sgm.t[:, 0:n], sgm, scale=-1.0)
                            kk_ = tmp.next()
                            S.op('dve', lambda e, sgm=sgm, kk_=kk_, r=r, h=h: e.tensor_scalar(
                                out=kk_.t[:, 0:n], in0=sgm.t[:, 0:n], scalar1=oml.t[:, r * 8 + h:r * 8 + h + 1],
                                scalar2=None, op0=ALU.mult), reads=(sgm, oml), writes=(kk_,))
                            lf = tmp.next()
                            S.op('act', lambda e, kk_=kk_, lf=lf: e.activation(
                                out=lf.t[:, 0:n], in_=kk_.t[:, 0:n], func=AF.Ln, scale=-1.0, bias=1.0),
                                 reads=(kk_,), writes=(lf,))
                            bb = tmp.next()
                            S.op('dve', lambda e, lf=lf, bb=bb: e.tensor_tensor_scan(
                                out=bb.t[:, 0:n], data0=mskF.t[:, 0:n], data1=lf.t[:, 0:n], initial=0.0,
                                op0=ALU.mult, op1=ALU.add), reads=(lf, mskF), writes=(bb,))
                            if r == 1:
                                d1 = tmp.next()
                                S.op('pool', lambda e, lf=lf, bb=bb, d1=d1: e.tensor_tensor(
                                    out=d1.t[:, 0:n], in0=lf.t[:, 0:n], in1=bb.t[:, 0:n], op=ALU.subtract),
                                     reads=(lf, bb), writes=(d1,))
                                b2 = tmp.next()
                                bb3 = bb.t[:, 0:n].rearrange("p (c t) -> p c t", t=64)
                                S.op('dve', lambda e, d1=d1, b2=b2, bb3=bb3: e.tensor_tensor(
                                    out=b2.t[:, 0:n].rearrange("p (c t) -> p c t", t=64),
                                    in0=d1.t[:, 0:n].rearrange("p (c t) -> p c t", t=64),
                                    in1=bb3[:, :, 63:64].to_broadcast([128, n // 64, 64]), op=ALU.add),
                                     reads=(d1, bb), writes=(b2,))
                                bb = b2
                            E = tmp.next()
                            S.op('act', lambda e, bb=bb, E=E: e.activation(out=E.t[:, 0:n], in_=bb.t[:, 0:n],
                                                                           func=AF.Exp), reads=(bb,), writes=(E,))
                            Ei = tmp.next()
                            S.op('act', lambda e, bb=bb, Ei=Ei: e.activation(out=Ei.t[:, 0:n], in_=bb.t[:, 0:n],
                                                                             func=AF.Exp, scale=-1.0), reads=(bb,),
                                 writes=(Ei,))
                            sg = gkd.slot()
                            S.op('dve', lambda e, kk_=kk_, Ei=Ei, sg=sg: e.tensor_tensor(
                                out=sg.t[:, 0:n], in0=kk_.t[:, 0:n], in1=Ei.t[:, 0:n], op=ALU.mult),
                                 reads=(kk_, Ei), writes=(sg,))
                            if c == 3:
                                gkd.flush(KD[r, h - 3:h + 1, :, kr0:kr0 + n].rearrange("h p t -> p h t"), n)
                                hook('f%d' % blk)
                            if own:
                                sg2 = gqd.slot()
                                S.op('dve', lambda e, E=E, sg2=sg2, h=h: e.scalar_tensor_tensor(
                                    out=sg2.t[:, 0:n], in0=qs.t[:, h, 0:n], scalar=float(128.0 ** -0.5),
                                    in1=E.t[:, 0:n], op0=ALU.mult, op1=ALU.mult), reads=(qs, E), writes=(sg2,))
                                if c == 3:
                                    gqd.flush(QD[r, h - 3:h + 1, :, tp:tp + n].rearrange("h p t -> p h t"), n)
                            E3 = E.t[:, 0:n].rearrange("p (c t) -> p c t", t=64)
                            ci0 = kr0 // 64
                            col = 63 if r == 0 else 0
                            S.op('pool', lambda e, E3=E3, r=r, h=h, ci0=ci0, col=col: e.tensor_copy(
                                out=dec.t[:, r, h, ci0:ci0 + n // 64], in_=E3[:, :, col]), reads=(E,), writes=(dec,))
                    for blk in (8, 9):
                        w = load_w(blk)
                        j = blk - 8
                        ghv = Grouper(S, stg4)
                        for s in range(nsub):
                            pb = arot.next()
                            for k in range(KC):
                                S.op('pe', lambda e, k=k, s=s, pb=pb, w=w: e.matmul(
                                    pb.t[:, :], lhsT=hT.t[:, k, s * 128:(s + 1) * 128], rhs=w.t[:, k, :],
                                    start=(k == 0), stop=(k == KC - 1)), reads=(w, hT), writes=(pb,))
                            sg = ghv.slot()
                            S.op('act', lambda e, pb=pb, sg=sg: e.activation(out=sg.t[:, 0:T], in_=pb.t[:, :],
                                                                             func=AF.Copy), reads=(pb,), writes=(sg,))
                        ghv.flush(HV[kr0:kr0 + n, j * 512:(j + 1) * 512].rearrange("(s p) c -> p s c", p=128), T,
                                  src=ghv.cur.t[:, 0:nsub, :])
                        hook('v%d' % blk)
                    if own:
                        for blk in (10, 11):
                            w = load_w(blk)
                            gg_ = Grouper(S, stg4)
                            for c in range(4):
                                h = (blk - 10) * 4 + c
                                pb = arot.next()
                                mm_fm(w, c, hT, n, pb)
                                sg = gg_.slot()
                                act_evac(AF.Silu, pb, n, sg.t[:, 0:n], sg)
                            gg_.flush(GH[h - 3:h + 1, :, tp:tp + n].rearrange("h p t -> p h t"), n)
                        for blk in range(12, 20):
                            w = load_w(blk)
                            gg_ = Grouper(S, stg4)
                            for c in range(4):
                                gi = (blk - 12) * 4 + c
                                pb = arot.next()
                                mm_fm(w, c, hT, n, pb)
                                sg = gg_.slot()
                                act_evac(AF.Sigmoid, pb, n, sg.t[:, 0:n], sg)
                            gg_.flush(GG[gi - 3:gi + 1, :, tp:tp + n].rearrange("h p t -> p h t"), n)
                            hook('g%d' % blk)

                load_sub(0, 0)
                load_sub(0, 1)
                for s0 in range(tiles[0][3] // 128):
                    prep_sub(0, hTs[0], s0)
                HOOKS = {'own': ('g12', 'g13', 'g14', 'g15'), 'oth': ('b1', 'f6', 'f7', 'v8'),
                         'ctx': ('b1', 'f4', 'f5', 'f6')}
                for ti in range(len(tiles)):
                    pending = []
                    if ti + 1 < len(tiles):
                        pending = [(lambda s1=s1, ti=ti: prep_sub(ti + 1, hTs[(ti + 1) % 2], s1))
                                   for s1 in range(tiles[ti + 1][3] // 128)]
                    where = HOOKS[tiles[ti][0]]

                    def hook(tag, pending=pending, where=where):
                        if tag in where and pending:
                            pending.pop(0)()

                    do_tile(ti, hTs[ti % 2], hook)
                    while pending:
                        pending.pop(0)()
                S.emit()
        if 'B' in PHASES:
            with ExitStack() as st:
                kn = [sbt(st, "kn%d" % i, [128, NKEY], BF16) for i in range(2)]
                vh = [sbt(st, "vh%d" % i, [128, NKT, 128], BF16) for i in range(2)]
                qn = [sbt(st, "qn%d" % i, [128, LH], BF16) for i in range(2)]
                qr = [sbt(st, "qr%d" % i, [128, LH], BF16) for i in range(2)]
                kre = sbt(st, "kre", [128, NKEY], BF16)
                kro = sbt(st, "kro", [128, NKEY], BF16)
                NPT = 6
                pT = [sbt(st, "pT%d" % i, [128, T], BF16) for i in range(NPT)]
                osb = Rot([sbt(st, "osb%d" % i, [128, T], BF16) for i in range(2)])
                rinv = Rot([sbt(st, "rinv%d" % i, [128, T], F32) for i in range(2)])
                accD = [sbt(st, "accD%d" % i, [128, T], F32) for i in range(2)]
                accP = [sbt(st, "accP%d" % i, [128, T], F32) for i in range(2)]
                ones_ff = sbt(st, "ones_ff", [128, 128], F32)
                S.op('pool', lambda e: e.memset(ones_ff.t[:, :], 1.0), writes=(ones_ff,))
                sbank = [PB[0], PB[1], PB[2]]
                obank = [PB[3], PB[4]]
                rbank = [PB[5], PB[6]]
                nqt = LH // T
                if debug and os.environ.get("MK_NT"):
                    nqt = int(os.environ["MK_NT"])
                heads = list(range(8))
                if debug and os.environ.get("MK_NH"):
                    heads = list(range(int(os.environ["MK_NH"])))
                S.drain('sp')
                S.op('pool', lambda e: e.memset(kre.t[64:128, :], 0.0), writes=(kre,))
                S.op('pool', lambda e: e.memset(kro.t[0:64, :], 0.0), writes=(kro,))
                S.dma('sp', kre.t[0:64, :], KTr[0:64, :], anchor=kre, reads=(), writes=(kre,))
                S.dma('sp', kro.t[64:128, :], KTr[64:128, :], anchor=kro, reads=(), writes=(kro,))
                steps = [(h, qt, kt) for h in heads for qt in range(nqt) for kt in range(NKT)]
                per_head = nqt * NKT
                sc_att = float(192.0 ** -0.5)

                def load_head(h):
                    S.dma('sp', kn[h % 2].t[:, :], KTn[h, :, :], anchor=kn[h % 2], writes=(kn[h % 2],))
                    S.dma('sp', vh[h % 2].t[:, :, :], VH[h, :, :, :], anchor=vh[h % 2], writes=(vh[h % 2],))
                    S.dma('sp', qn[h % 2].t[:, 0:nqt * T], QTn[h, :, 0:nqt * T], anchor=qn[h % 2],
                          writes=(qn[h % 2],))
                    if h % 2 == 0:
                        p = h // 2
                        S.dma('sp', qr[p % 2].t[:, 0:nqt * T], QTr[p, :, 0:nqt * T], anchor=qr[p % 2],
                              writes=(qr[p % 2],))

                def rs_mode(kt):
                    if kt % 11 == 10:
                        return 'pe'
                    i = kt - kt // 11
                    return 'pool' if (i % 2) == 1 else 'dve'

                def emit_qk(g):
                    h, qt, kt = steps[g]
                    sb = sbank[g % 3]
                    krx = kre if h % 2 == 0 else kro
                    qrx = qr[(h // 2) % 2]
                    knx = kn[h % 2]
                    qnx = qn[h % 2]
                    S.op('pe', lambda e: e.matmul(sb.t[:, :], lhsT=knx.t[:, kt * 128:(kt + 1) * 128],
                                                  rhs=qnx.t[:, qt * T:(qt + 1) * T], start=True, stop=False),
                         reads=(knx, qnx), writes=(sb,))
                    S.op('pe', lambda e: e.matmul(sb.t[:, :], lhsT=krx.t[:, kt * 128:(kt + 1) * 128],
                                                  rhs=qrx.t[:, qt * T:(qt + 1) * T], start=False, stop=True),
                         reads=(krx, qrx), writes=(sb,))
                    p = pT[g % NPT]
                    S.op('act', lambda e: e.activation(out=p.t[:, :], in_=sb.t[:, :], func=AF.Exp, scale=sc_att),
                         reads=(sb,), writes=(p,))
                    idx = (g // NKT) % 2
                    mode = rs_mode(kt)
                    if mode != 'pe':
                        acc = accP[idx] if mode == 'pool' else accD[idx]
                        firstk = 1 if mode == 'pool' else 0
                        if kt == firstk:
                            S.op(mode, lambda e: e.tensor_copy(out=acc.t[:, :], in_=p.t[:, :]), reads=(p,),
                                 writes=(acc,))
                        else:
                            S.op(mode, lambda e: e.tensor_tensor(out=acc.t[:, :], in0=acc.t[:, :], in1=p.t[:, :],
                                                                 op=ALU.add), reads=(acc, p), writes=(acc,))

                def emit_pv(g):
                    h, qt, kt = steps[g]
                    idx = (g // NKT) % 2
                    ob = obank[idx]
                    rb = rbank[idx]
                    p = pT[g % NPT]
                    vx = vh[h % 2]
                    S.op('pe', lambda e: e.matmul(ob.t[:, :], lhsT=vx.t[:, kt, :], rhs=p.t[:, :], start=(kt == 0),
                                                  stop=(kt == NKT - 1)), reads=(vx, p), writes=(ob,))
                    if rs_mode(kt) == 'pe':
                        S.op('pe', lambda e: e.matmul(rb.t[:, :], lhsT=ones_bf.t[:, :], rhs=p.t[:, :], start=(kt == 10),
                                                      stop=False), reads=(ones_bf, p), writes=(rb,))
                    if kt == NKT - 1:
                        aD, aP = accD[idx], accP[idx]
                        S.op('pe', lambda e: e.matmul(rb.t[:, :], lhsT=ones_ff.t[:, :], rhs=aD.t[:, :], start=False,
                                                      stop=False), reads=(ones_ff, aD), writes=(rb,))
                        S.op('pe', lambda e: e.matmul(rb.t[:, :], lhsT=ones_ff.t[:, :], rhs=aP.t[:, :], start=False,
                                                      stop=True), reads=(ones_ff, aP), writes=(rb,))
                        ri = rinv.next()
                        os_ = osb.next()
                        S.op('dve', lambda e: e.reciprocal(out=ri.t[:, :], in_=rb.t[:, :]), reads=(rb,), writes=(ri,))
                        S.op('dve', lambda e: e.tensor_tensor(out=os_.t[:, :], in0=ob.t[:, :], in1=ri.t[:, :],
                                                              op=ALU.mult), reads=(ob, ri), writes=(os_,))
                        S.dma('pool', AT[h, :, qt * T:(qt + 1) * T], os_.t[:, :], anchor=os_, reads=(os_,))

                load_head(heads[0])
                G = len(steps)
                for g in range(G + 2):
                    if g >= 2 and (g - 2) % per_head == 0:
                        hn = (g - 2) // per_head + 1
                        if hn < len(heads):
                            load_head(heads[hn])
                    if g < G:
                        emit_qk(g)
                    if g >= 2:
                        emit_pv(g - 2)
                S.emit()
        if 'H' in PHASES:
            with ExitStack() as st:
                kdb = [sbt(st, "kdb%d" % i, [128, 8, T], BF16) for i in range(2)]
                qdb = [sbt(st, "qdb%d" % i, [128, 8, T], BF16) for i in range(2)]
                hvb = [sbt(st, "hvb%d" % i, [128, 4, 1024], BF16) for i in range(2)]
                o1b = [sbt(st, "o1b%d" % i, [128, 8, T], F32) for i in range(2)]
                ost = [sbt(st, "ost%d" % i, [128, 8, T], F32) for i in range(2)]
                hst = [sbt(st, "hst%d" % i, [128, 8, T], BF16) for i in range(2)]
                Tst = [sbt(st, "Tst%d" % h, [128, 128], F32) for h in range(8)]
                Sbf = [[sbt(st, "Sbf%d_%d" % (h, i), [128, 128], BF16) for i in range(2)] for h in range(8)]
                kdA = [sbt(st, "kdA%d" % i, [128, 128], BF16) for i in range(8)]
                kdB = [sbt(st, "kdB%d" % i, [128, 128], BF16) for i in range(8)]
                smk = [sbt(st, "smk%d" % i, [128, 128], BF16) for i in range(8)]
                ptrb = PB[7]
                ptrv = ptrb.t[:, :].bitcast(BF16)
                ptr_slot = [ptrb] * 8
                psS_slot = [PB[4]] * 4
                psUa_slot = [PB[5]] * 4
                psUb_slot = [PB[6]] * 4
                psO = [PB[0], PB[1], PB[2], PB[3]]
                psS = PB[4]
                psU = [PB[5], PB[6]]
                for i in range(8):
                    S.op('pool', lambda e, i=i: e.memset(kdA[i].t[:, :], 0.0), writes=(kdA[i],))
                    S.op('pool', lambda e, i=i: e.memset(kdB[i].t[:, :], 0.0), writes=(kdB[i],))
                masks = (mask1, mask2)
                n_own = 8
                n_oth = 8
                if debug and os.environ.get("MK_NT"):
                    n_own = int(os.environ["MK_NT"])
                    n_oth = 1
                for r in range(int(os.environ.get("MK_HR", "2")) if debug else 2):
                    S.drain('sp')
                    for h in range(8):
                        S.op('pool', lambda e, h=h: e.memset(Tst[h].t[:, :], 0.0), writes=(Tst[h],))
                        S.op('pool', lambda e, h=h: e.memset(Sbf[h][0].t[:, :], 0.0), writes=(Sbf[h][0],))
                    scur = [0] * 8
                    if r == 0:
                        htiles = [('ctx', 0, CTX, None)] + [('own', CTX + i * T, T, i * T) for i in range(n_own)]
                    else:
                        htiles = [('ctx', 0, CTX, None)]
                        htiles += [('oth', CTX + LH + i * T, T, None) for i in reversed(range(n_oth))]
                        htiles += [('own', CTX + i * T, T, i * T) for i in reversed(range(n_own))]
                    first = [True]

                    def hload(i):
                        kind, row0, n, tok0 = htiles[i]
                        bi = i % 2
                        S.dma('sp', kdb[bi].t[:, :, 0:n], KD[r, :, :, row0:row0 + n].rearrange("h p t -> p h t"),
                              anchor=kdb[bi], writes=(kdb[bi],))
                        S.dma('sp', hvb[bi].t[:, 0:n // 128, :],
                              HV[row0:row0 + n, :].rearrange("(s p) c -> p s c", p=128), anchor=hvb[bi],
                              writes=(hvb[bi],))
                        if kind == 'own':
                            S.dma('sp', qdb[bi].t[:, :, 0:n],
                                  QD[r, :, :, tok0:tok0 + n].rearrange("h p t -> p h t"), anchor=qdb[bi],
                                  writes=(qdb[bi],))
                            if r == 1:
                                S.dma('sp', o1b[bi].t[:, :, 0:n],
                                      O1[:, :, tok0:tok0 + n].rearrange("h p t -> p h t"), anchor=o1b[bi],
                                      writes=(o1b[bi],))

                    def hcompute(i):
                        kind, row0, n, tok0 = htiles[i]
                        bi = i % 2
                        own = kind == 'own'
                        nsub = n // 128
                        kd_, qd_, hv_, o1_ = kdb[bi], qdb[bi], hvb[bi], o1b[bi]
                        os_ = ost[i % 2]
                        hs_ = hst[i % 2]
                        subs = list(range(nsub)) if r == 0 else list(reversed(range(nsub)))
                        corder = (0, 1) if r == 0 else (1, 0)
                        def do_chunk(s_, hs, j, c):
                            ci = (row0 + s_ * 128 + c * 64) // 64
                            kdx = kdA if c == 0 else kdB
                            psUx = psU[j]
                            slots = psUa_slot if j == 0 else psUb_slot
                            csl = slice(s_ * 128 + c * 64, s_ * 128 + c * 64 + 64)
                            for h in hs:
                                q = h % 4
                                if own and HL >= 4:
                                    sb_ = Sbf[h][scur[h] % 2]
                                    S.op('pe', lambda e, h=h, q=q, sb_=sb_: e.matmul(
                                        psO[q].t[:, c * 64:c * 64 + 64], lhsT=sb_.t[:, :],
                                        rhs=qd_.t[:, h, csl], start=False, stop=(j == 1)),
                                         reads=(sb_, qd_), writes=(psO[q],))
                                S.op('pe', lambda e, h=h, q=q: e.matmul(
                                    psUx.t[:, q * 128:(q + 1) * 128], lhsT=kdx[h].t[:, :],
                                    rhs=hv_.t[:, s_, h * 128:(h + 1) * 128], start=True, stop=True),
                                     reads=(kdx[h], hv_), writes=(slots[q],))
                            if HL < 3:
                                return
                            for h in hs:
                                q = h % 4
                                dcur = dec.t[:, r, h, ci:ci + 1]
                                dp = dprev[h] if dprev[h] is not None else dcur
                                S.op('dve', lambda e, h=h, q=q, dp=dp: e.scalar_tensor_tensor(
                                    out=Tst[h].t[:, :], in0=Tst[h].t[:, :], scalar=dp,
                                    in1=psUx.t[:, q * 128:(q + 1) * 128], op0=ALU.mult, op1=ALU.add),
                                     reads=(Tst[h], slots[q], dec), writes=(Tst[h],))
                                scur[h] += 1
                                sb2 = Sbf[h][scur[h] % 2]
                                S.op('act', lambda e, h=h, sb2=sb2, dcur=dcur: e.activation(
                                    out=sb2.t[:, :], in_=Tst[h].t[:, :], func=AF.Copy, scale=dcur),
                                     reads=(Tst[h], dec), writes=(sb2,))
                                dprev[h] = dcur

                        HL = int(os.environ.get("MK_HLVL", "9"))

                        def do_group(s_, hs):
                            ssl = slice(s_ * 128, (s_ + 1) * 128)
                            if HL < 1:
                                return
                            for h in hs:
                                S.op('pe', lambda e, h=h: e.transpose(
                                    out=ptrv[:, h * 128:(h + 1) * 128], in_=kd_.t[:, h, ssl],
                                    identity=ident.t[:, :]), reads=(kd_, ident), writes=(ptr_slot[h],))
                            for h in hs:
                                S.op('act', lambda e, h=h: e.activation(
                                    out=kdA[h].t[0:64, :], in_=ptrv[0:64, h * 128:(h + 1) * 128], func=AF.Copy),
                                     reads=(ptr_slot[h],), writes=(kdA[h],))
                                S.op('act', lambda e, h=h: e.activation(
                                    out=kdB[h].t[64:128, :], in_=ptrv[64:128, h * 128:(h + 1) * 128],
                                    func=AF.Copy), reads=(ptr_slot[h],), writes=(kdB[h],))
                            if HL < 2:
                                return
                            if own and HL >= 4:
                                for h in hs:
                                    q = h % 4
                                    S.op('pe', lambda e, h=h, q=q: e.matmul(
                                        psS.t[:, q * 128:(q + 1) * 128], lhsT=kd_.t[:, h, ssl],
                                        rhs=qd_.t[:, h, ssl], start=True, stop=True), reads=(kd_, qd_),
                                         writes=(psS_slot[q],))
                                for h in hs:
                                    q = h % 4
                                    S.op('dve', lambda e, h=h, q=q: e.tensor_tensor(
                                        out=smk[h].t[:, :], in0=psS.t[:, q * 128:(q + 1) * 128],
                                        in1=masks[r].t[:, :], op=ALU.mult), reads=(psS_slot[q], masks[r]),
                                         writes=(smk[h],))
                                for h in hs:
                                    q = h % 4
                                    S.op('pe', lambda e, h=h, q=q: e.matmul(
                                        psO[q].t[:, 0:128], lhsT=hv_.t[:, s_, h * 128:(h + 1) * 128],
                                        rhs=smk[h].t[:, :], start=True, stop=False), reads=(hv_, smk[h]),
                                         writes=(psO[q],))
                            for j, c in enumerate(corder):
                                do_chunk(s_, hs, j, c)
                            if own and HL >= 4:
                                for h in hs:
                                    q = h % 4
                                    if r == 0:
                                        S.op('act', lambda e, h=h, q=q: e.activation(
                                            out=os_.t[:, h, ssl], in_=psO[q].t[:, 0:128], func=AF.Copy),
                                             reads=(psO[q],), writes=(os_,))
                                    else:
                                        S.op('dve', lambda e, h=h, q=q: e.tensor_tensor(
                                            out=hs_.t[:, h, ssl], in0=psO[q].t[:, 0:128], in1=o1_.t[:, h, ssl],
                                            op=ALU.add), reads=(psO[q], o1_), writes=(hs_,))

                        for s_ in subs:
                            for grp in range(2):
                                do_group(s_, list(range(grp * 4, grp * 4 + 4)))
                        if own and HL >= 4:
                            if r == 0:
                                S.dma('pool', O1[:, :, tok0:tok0 + n].rearrange("h p t -> p h t"), os_.t[:, :, 0:n],
                                      anchor=os_, reads=(os_,))
                            else:
                                S.dma('pool', HO[:, :, tok0:tok0 + n].rearrange("h p t -> p h t"), hs_.t[:, :, 0:n],
                                      anchor=hs_, reads=(hs_,))

                    dprev = [None] * 8

                    if debug and os.environ.get("MK_HT"):
                        htiles = htiles[:int(os.environ["MK_HT"])]
                    hload(0)
                    for i in range(len(htiles)):
                        if i + 1 < len(htiles):
                            hload(i + 1)
                        hcompute(i)
                    S.emit()
        stAH.close()
        if 'C' in PHASES:
            with ExitStack() as st:
                x4 = sbt(st, "x4", [128, 4, D], F32)
                raw = sbt(st, "raw", [128, 4, D], F32)
                act = sbt(st, "act", [128, 44, T], BF16)
                fT = sbt(st, "fT", [128, KC, T], BF16)
                xnc = [sbt(st, "xnc%d" % i, [128, D], BF16) for i in range(2)]
                wslc = Rot([sbt(st, "wslc%d" % i, [128, 8, 512], BF16) for i in range(4)])
                hob = Rot([sbt(st, "hob%d" % i, [128, T], BF16) for i in range(2)])
                ghb = Rot([sbt(st, "ghb%d" % i, [128, T], BF16) for i in range(2)])
                gab = Rot([sbt(st, "gab%d" % i, [128, T], BF16) for i in range(2)])
                gbb = Rot([sbt(st, "gbb%d" % i, [128, T], BF16) for i in range(2)])
                sqb = Rot([sbt(st, "sqb%d" % i, [128, T], BF16) for i in range(2)])
                Gb = sbt(st, "Gb", [128, D], F32)
                sa = sbt(st, "sa", [128, 4, T], BF16)
                tmpc = Rot([sbt(st, "tmpc%d" % i, [128, T], F32) for i in range(4)])
                ssq4 = sbt(st, "ssq4", [128, 16], F32)
                ssq1 = sbt(st, "ssq1", [128, 4], F32)
                rstdc = sbt(st, "rstdc", [128, 4], F32)
                crot = Rot([PB[0], PB[1], PB[2], PB[3]])
                c2rot = Rot([PB[4], PB[5]])
                ctr = Rot([PB[6], PB[7]])
                n_own = 8
                if debug and os.environ.get("MK_NT"):
                    n_own = int(os.environ["MK_NT"])
                S.drain('sp')
                S.drain('act')

                def loadw(src, k0, nk):
                    w = wslc.next()
                    S.dma('sp', w.t[:, 0:nk, :], src.rearrange("p (k c) -> p k c", c=512)[:, k0:k0 + nk, :], anchor=w,
                          writes=(w,))
                    return w

                def loadw_cols(src, c0):
                    w = wslc.next()
                    wv = w.t[:, :, :].rearrange("p a b -> p (a b)").rearrange("p (k c) -> p k c", c=256)
                    S.dma('sp', wv, src.rearrange("p (k c) -> p k c", c=512)[:, :, c0:c0 + 256], anchor=w, writes=(w,))
                    return w, wv

                def finalize(gidx, addx, tok0, store):
                    S.dma('act', Gb.t[:, :], GAB[gidx, :, :], anchor=Gb, writes=(Gb,))
                    for s_ in range(4):
                        S.op('dve', lambda e, s_=s_: e.tensor_reduce(
                            out=ssq1.t[:, s_:s_ + 1], in_=ssq4.t[:, s_ * 4:(s_ + 1) * 4], axis=mybir.AxisListType.X,
                            op=ALU.add), reads=(ssq4,), writes=(ssq1,))
                        rsqrt(rstdc.t[:, s_:s_ + 1], ssq1.t[:, s_:s_ + 1], D * EPS, (ssq1,), rstdc)
                        S.op('dve', lambda e, s_=s_: e.scalar_tensor_tensor(
                            out=raw.t[:, s_, :], in0=raw.t[:, s_, :], scalar=rstdc.t[:, s_:s_ + 1], in1=Gb.t[:, :],
                            op0=ALU.mult, op1=ALU.mult), reads=(raw, rstdc, Gb), writes=(raw,))
                        if store:
                            S.op('dve', lambda e, s_=s_: e.tensor_tensor(out=raw.t[:, s_, :], in0=raw.t[:, s_, :],
                                                                         in1=x4.t[:, s_, :], op=ALU.add),
                                 reads=(raw, x4), writes=(raw,))
                            S.dma('pool', y[tok0 + s_ * 128:tok0 + (s_ + 1) * 128, :], raw.t[:, s_, :], anchor=raw,
                                  reads=(raw,))
                        else:
                            S.op('dve', lambda e, s_=s_: e.tensor_tensor(out=x4.t[:, s_, :], in0=x4.t[:, s_, :],
                                                                         in1=raw.t[:, s_, :], op=ALU.add),
                                 reads=(raw, x4), writes=(x4,))

                def evac_raw(pb, s_, blk):
                    S.op('act', lambda e: e.activation(out=raw.t[:, s_, blk * 512:(blk + 1) * 512], in_=pb.t[:, :],
                                                       func=AF.Copy), reads=(pb,), writes=(raw,))
                    jb = sqb.next()
                    S.op('act', lambda e: e.activation(out=jb.t[:, :], in_=pb.t[:, :], func=AF.Square,
                                                       accum_out=ssq4.t[:, s_ * 4 + blk:s_ * 4 + blk + 1]),
                         reads=(pb,), writes=(jb, ssq4))

                def ctile(i):
                    tok0 = i * T
                    S.dma('sp', act.t[:, 0:8, :], AT[:, :, tok0:tok0 + T].rearrange("h p t -> p h t"), anchor=act,
                          writes=(act,))
                    S.dma('sp', x4.t[:, :, :], xo[tok0:tok0 + T, :].rearrange("(s p) c -> p s c", p=128), anchor=x4,
                          writes=(x4,))

                    def hnorm(h):
                        ho = hob.next()
                        gh = ghb.next()
                        S.dma('act', ho.t[:, :], HO[h, :, tok0:tok0 + T], anchor=ho, writes=(ho,))
                        S.dma('act', gh.t[:, :], GH[h, :, tok0:tok0 + T], anchor=gh, writes=(gh,))
                        sq_ = sqb.next()
                        S.op('act', lambda e: e.activation(out=sq_.t[:, :], in_=ho.t[:, :], func=AF.Square),
                             reads=(ho,), writes=(sq_,))
                        pb = c2rot.next()
                        S.op('pe', lambda e: e.matmul(pb.t[:, :], lhsT=ones_bf.t[:, :], rhs=sq_.t[:, :], start=True,
                                                      stop=True), reads=(ones_bf, sq_), writes=(pb,))
                        rt = tmpc.next()
                        rsqrt(rt.t[:, :], pb.t[:, :], 128 * EPS, (pb,), rt)
                        t_ = tmpc.next()
                        S.op('dve', lambda e: e.tensor_tensor(out=t_.t[:, :], in0=ho.t[:, :], in1=rt.t[:, :],
                                                              op=ALU.mult), reads=(ho, rt), writes=(t_,))
                        S.op('dve', lambda e: e.scalar_tensor_tensor(
                            out=act.t[:, 8 + h, :], in0=t_.t[:, :], scalar=ongs.t[:, 0:1], in1=gh.t[:, :],
                            op0=ALU.mult, op1=ALU.mult), reads=(t_, ongs, gh), writes=(act,))

                    CL = int(os.environ.get("MK_CLVL", "9")) if debug else 9
                    for h in range(8):
                        hnorm(h)
                    if CL < 2:
                        return

                    def merge_chunk(wa, wh, cc, c):
                        pbA = crot.next()
                        pbH = crot.next()
                        for k in range(8):
                            S.op('pe', lambda e, k=k: e.matmul(pbA.t[:, :], lhsT=wa.t[:, k, cc * 128:(cc + 1) * 128],
                                                               rhs=act.t[:, k, :], start=(k == 0), stop=(k == 7)),
                                 reads=(wa, act), writes=(pbA,))
                        for k in range(8):
                            S.op('pe', lambda e, k=k: e.matmul(pbH.t[:, :], lhsT=wh.t[:, k, cc * 128:(cc + 1) * 128],
                                                               rhs=act.t[:, 8 + k, :], start=(k == 0), stop=(k == 7)),
                                 reads=(wh, act), writes=(pbH,))
                        ga = gab.next()
                        gb = gbb.next()
                        S.dma('act', ga.t[:, :], GG[c, :, tok0:tok0 + T], anchor=ga, writes=(ga,))
                        S.dma('act', gb.t[:, :], GG[16 + c, :, tok0:tok0 + T], anchor=gb, writes=(gb,))
                        t1 = tmpc.next()
                        t2 = tmpc.next()
                        S.op('dve', lambda e: e.tensor_tensor(out=t1.t[:, :], in0=pbA.t[:, :], in1=ga.t[:, :],
                                                              op=ALU.mult), reads=(pbA, ga), writes=(t1,))
                        S.op('dve', lambda e: e.tensor_tensor(out=t2.t[:, :], in0=pbH.t[:, :], in1=gb.t[:, :],
                                                              op=ALU.mult), reads=(pbH, gb), writes=(t2,))
                        S.op('pool', lambda e: e.tensor_tensor(out=fT.t[:, c, :], in0=t1.t[:, :], in1=t2.t[:, :],
                                                               op=ALU.add), reads=(t1, t2), writes=(fT,))

                    for blk in range(4):
                        wa = loadw(WBA[blk], 0, 8)
                        wh = loadw(WBH[blk], 0, 8)
                        for cc in range(4):
                            merge_chunk(wa, wh, cc, blk * 4 + cc)

                    if CL < 3:
                        return

                    def tm_block(w, blk, nk, src, kofs, banks, first, last):
                        for s_ in range(4):
                            pb = banks[s_]
                            for k in range(nk):
                                S.op('pe', lambda e, k=k, pb=pb, s_=s_: e.matmul(
                                    pb.t[:, :], lhsT=src.t[:, kofs + k, s_ * 128:(s_ + 1) * 128], rhs=w.t[:, k, :],
                                    start=(first and k == 0), stop=(last and k == nk - 1)), reads=(src, w),
                                     writes=(pb,))

                    for blk in range(4):
                        banks = [PB[(blk % 2) * 4 + q] for q in range(4)]
                        for half in range(2):
                            w = loadw(WO[blk], half * 8, 8)
                            tm_block(w, blk, 8, fT, half * 8, banks, half == 0, half == 1)
                        for s_ in range(4):
                            evac_raw(banks[s_], s_, blk)
                    finalize(0, True, tok0, False)
                    if CL < 4:
                        return

                    def prep_sub(s_):
                        xn = xnc[s_ % 2]
                        S.op('act', lambda e: e.activation(out=xn.t[:, :], in_=x4.t[:, s_, :], func=AF.Square,
                                                           accum_out=ssq1.t[:, s_:s_ + 1]), reads=(x4,),
                             writes=(xn, ssq1))
                        rsqrt(rstdc.t[:, s_:s_ + 1], ssq1.t[:, s_:s_ + 1], D * EPS, (ssq1,), rstdc)
                        S.op('dve', lambda e: e.tensor_scalar(out=xn.t[:, :], in0=x4.t[:, s_, :],
                                                              scalar1=rstdc.t[:, s_:s_ + 1], scalar2=None,
                                                              op0=ALU.mult), reads=(x4, rstdc, xn), writes=(xn,))
                        PSL = int(os.environ.get("MK_PS", "9")) if debug else 9
                        if PSL < 1:
                            return
                        for g in range(2):
                            pb = ctr.next()
                            pbv = pb.t[:, :].bitcast(BF16)
                            for kk in range(8):
                                k = g * 8 + kk
                                S.op('pe', lambda e, kk=kk, k=k, pbv=pbv: e.transpose(
                                    out=pbv[:, kk * 128:(kk + 1) * 128], in_=xn.t[:, k * 128:(k + 1) * 128],
                                    identity=ident.t[:, :]), reads=(xn, ident), writes=(pb,))
                            if PSL < 2:
                                continue
                            for kk in range(8):
                                k = g * 8 + kk
                                S.op('act', lambda e, kk=kk, k=k, pbv=pbv: e.activation(
                                    out=fT.t[:, k, s_ * 128:(s_ + 1) * 128], in_=pbv[:, kk * 128:(kk + 1) * 128],
                                    func=AF.Identity, scale=gsF.t[:, k:k + 1], bias=shF.t[:, k:k + 1]),
                                     reads=(pb, gsF, shF), writes=(fT,))

                    for s_ in range(4):
                        prep_sub(s_)
                    if CL < 5:
                        return

                    def ffn_chunk(w, wv, c2, cc, c, is_a):
                        pb = crot.next()
                        for k in range(KC):
                            S.op('pe', lambda e, k=k: e.matmul(pb.t[:, :], lhsT=wv[:, k, c2 * 128:(c2 + 1) * 128],
                                                               rhs=fT.t[:, k, :], start=(k == 0), stop=(k == KC - 1)),
                                 reads=(w, fT), writes=(pb,))
                        if is_a:
                            S.op('act', lambda e: e.activation(out=sa.t[:, cc, :], in_=pb.t[:, :], func=AF.Silu),
                                 reads=(pb,), writes=(sa,))
                        else:
                            S.op('dve', lambda e: e.tensor_tensor(out=act.t[:, c, :], in0=pb.t[:, :],
                                                                  in1=sa.t[:, cc, :], op=ALU.mult), reads=(pb, sa),
                                 writes=(act,))

                    for blk in range(11):
                        for hc in range(2):
                            w, wv = loadw_cols(WF1[blk], hc * 256)
                            for c2 in range(2):
                                cc = hc * 2 + c2
                                ffn_chunk(w, wv, c2, cc, blk * 4 + cc, True)
                        for hc in range(2):
                            w, wv = loadw_cols(WF1[11 + blk], hc * 256)
                            for c2 in range(2):
                                cc = hc * 2 + c2
                                ffn_chunk(w, wv, c2, cc, blk * 4 + cc, False)

                    if CL < 6:
                        return
                    for blk in range(4):
                        banks = [PB[(blk % 2) * 4 + q] for q in range(4)]
                        for kp in range(4):
                            w = loadw(WF2[blk * 4 + kp], 0, 8)
                            tm_block(w, blk, 8, act, kp * 11, banks, kp == 0, False)
                            w = loadw(WF2[blk * 4 + kp], 8, 3)
                            tm_block(w, blk, 3, act, kp * 11 + 8, banks, False, kp == 3)
                        for s_ in range(4):
                            evac_raw(banks[s_], s_, blk)
                    finalize(1, True, tok0, True)

                for i in range(n_own):
                    ctile(i)
                S.drain('sp')
                S.emit()
        return nc, S, locals()


def _rope_tables():
    pairs = 16
    pos = np.arange(L)
    row = (pos // 64).astype(np.float32)
    col = (pos % 64).astype(np.float32)
    inv = (np.float32(10000.0) ** (-np.arange(pairs, dtype=np.float32) / np.float32(pairs))).astype(np.float32)
    ang = np.concatenate([row[:, None] * inv, col[:, None] * inv], axis=-1).astype(np.float32)
    cos = np.cos(ang).astype(np.float32)
    sin = np.sin(ang).astype(np.float32)
    return np.ascontiguousarray(np.tile(cos.T, (4, 1))), np.ascontiguousarray(np.tile(sin.T, (4, 1)))


def core_inputs(inp, core, shared):
    b = core // 2
    rev = core % 2
    f = lambda a: np.ascontiguousarray(a, dtype=np.float32)
    x = inp['x'][b]
    ctx = inp['ctx'][b]
    if rev:
        x = x[::-1]
        ctx = ctx[::-1]
    key = ('w', rev)
    if key not in shared:
        w_in = inp['w_in'][0]
        lb = inp['hgrn_lb']
        cos2, sin2 = shared['rope']
        if rev:
            w_in = np.concatenate([w_in[:, :1856], w_in[:, 2880:3904], w_in[:, 1856:2880], w_in[:, 3904:]], axis=1)
            lb = lb[:, ::-1]
            cos2 = cos2[:, ::-1]
            sin2 = sin2[:, ::-1]
        shared[key] = dict(w_in=f(w_in), lbT=f(lb.transpose(3, 0, 1, 2).reshape(128, 2, 16)), cos2=f(cos2),
                           sin2=f(sin2))
    if 'common' not in shared:
        ng = inp['norm_g'][0]
        shared['common'] = dict(
            w_mod=f(inp['w_mod'][0]), b_mod=f(inp['b_mod'][0].reshape(1, -1)),
            ng_fm=f(ng.reshape(4, KC, 128).transpose(2, 0, 1)), ng_row=f(ng.reshape(1, -1)),
            qng=f(inp['mla_q_norm'][0].reshape(4, 128).T), kvng=f(inp['mla_kv_norm'][0].reshape(2, 128).T),
            w_uq=f(inp['w_uq'][0]), w_ukv=f(inp['w_ukv'][0]), ong=f(inp['hgrn_o_norm'][0].reshape(128, 1)),
            w_br_mla=f(inp['w_br_mla'][0]), w_br_hgrn=f(inp['w_br_hgrn'][0]), w_out=f(inp['w_out'][0]),
            w_ffn_in=f(inp['w_ffn_in'][0]), w_ffn_out=f(inp['w_ffn_out'][0]))
    m = dict(shared['common'])
    m.update(shared[key])
    cc = np.stack([inp['c'][b], inp['c_ctx']], axis=-1)
    m['cc'] = f(cc.reshape(KC, 128, 2).transpose(1, 0, 2))
    m['xo'] = f(x[:LH])
    m['xt'] = f(x[LH:])
    m['cx'] = f(ctx)
    return m


_NC_CACHE = {}


def kernel(**inputs):
    inp = {k: np.asarray(v) for k, v in inputs.items()}
    if 'nc' not in _NC_CACHE:
        _NC_CACHE['nc'] = build_nc(False)[0]
    nc = _NC_CACHE['nc']
    shared = {'rope': _rope_tables()}
    in_maps = [core_inputs(inp, c, shared) for c in range(8)]
    res = run_bass_kernel_spmd(nc, in_maps, core_ids=list(range(8)))
    out = np.empty((4, L, D), np.float32)
    for c in range(8):
        yv = np.asarray(res.results[c]["y"], dtype=np.float32)
        b = c // 2
        if c % 2 == 0:
            out[b, :LH] = yv
        else:
            out[b, LH:] = yv[::-1]
    return out
```
